# Optimizing a Trainium2 kernel written in Bass

```python
import jax, jax.numpy as jnp
from jax import lax
import numpy as np


D_MODEL = 1024
BATCH = 2
SEQ = 8192
DEPTH = 2

N_MIXERS = 2
N_LAYERS_A = (DEPTH + 1) // 2
N_LAYERS_B = DEPTH // 2
MLA_HEADS = 8
QK_NOPE = 128
QK_ROPE = 64
V_DIM = 128
Q_LORA = 384
KV_LORA = 256
ROPE_THETA = 10000.0
SWA_HEADS = 16
SWA_KV_HEADS = 4
SWA_HEAD_DIM = 64
WINDOW = 128
D_FF = 4 * D_MODEL
BLOCK_Q = 128
EPS = 1e-6

kernel_name = 'hybrid_mla_swa_sink_alibi_adaln'


def rmsnorm(x, g):
    xf = x.astype(jnp.float32)
    y = xf * lax.rsqrt(jnp.mean(xf * xf, axis=-1, keepdims=True) + EPS)
    return (y * g.astype(jnp.float32)).astype(x.dtype)


def modulate(x, g, shift, scale):
    return rmsnorm(x, g) * (1.0 + scale[:, None, :]) + shift[:, None, :]


def rope(x, positions):
    half = QK_ROPE // 2
    inv_freq = ROPE_THETA ** (-jnp.arange(half, dtype=jnp.float32) / half)
    ang = positions.astype(jnp.float32)[..., None] * inv_freq
    shape = ang.shape[:2] + (1,) * (x.ndim - 3) + (half,)
    cos = jnp.cos(ang).reshape(shape)
    sin = jnp.sin(ang).reshape(shape)
    xf = x.astype(jnp.float32)
    x1, x2 = xf[..., :half], xf[..., half:]
    out = jnp.concatenate([x1 * cos - x2 * sin, x1 * sin + x2 * cos], axis=-1)
    return out.astype(x.dtype)


def alibi_slopes(n_heads):
    return jnp.asarray(2.0 ** (-8.0 * np.arange(1, n_heads + 1) / n_heads), dtype=jnp.float32)


def mla(h, positions, w_dq, g_q, w_uq, w_dkv, g_kv, w_ukv, w_o):
    B, S, _ = h.shape
    H = MLA_HEADS
    cq = rmsnorm(h @ w_dq, g_q)
    q = (cq @ w_uq).reshape(B, S, H, QK_NOPE + QK_ROPE)
    q_nope = q[..., :QK_NOPE]
    q_rope = rope(q[..., QK_NOPE:], positions)
    ckv_kr = h @ w_dkv
    ckv = rmsnorm(ckv_kr[..., :KV_LORA], g_kv)
    k_rope = rope(ckv_kr[..., KV_LORA:], positions)
    kv = (ckv @ w_ukv).reshape(B, S, H, QK_NOPE + V_DIM)
    k_nope, v = kv[..., :QK_NOPE], kv[..., QK_NOPE:]
    scale = (QK_NOPE + QK_ROPE) ** -0.5
    n_blk = S // BLOCK_Q
    qn_blk = q_nope.reshape(B, n_blk, BLOCK_Q, H, QK_NOPE).transpose(1, 0, 2, 3, 4)
    qr_blk = q_rope.reshape(B, n_blk, BLOCK_Q, H, QK_ROPE).transpose(1, 0, 2, 3, 4)
    key_idx = jnp.arange(S)

    def one_block(args):
        i, qn, qr = args
        s = (jnp.einsum('bqhd,bkhd->bhqk', qn, k_nope)
             + jnp.einsum('bqhr,bkr->bhqk', qr, k_rope)).astype(jnp.float32) * scale
        q_idx = i * BLOCK_Q + jnp.arange(BLOCK_Q)
        causal = key_idx[None, :] <= q_idx[:, None]
        s = jnp.where(causal[None, None], s, -jnp.inf)
        p = jax.nn.softmax(s, axis=-1).astype(v.dtype)
        return jnp.einsum('bhqk,bkhd->bqhd', p, v)

    o = lax.map(one_block, (jnp.arange(n_blk), qn_blk, qr_blk))
    o = o.transpose(1, 0, 2, 3, 4).reshape(B, S, H * V_DIM)
    return o @ w_o


def swa(h, w_qkv, b_qkv, sinks, w_o, b_o):
    B, S, _ = h.shape
    Hq, Hk, Dh, W = SWA_HEADS, SWA_KV_HEADS, SWA_HEAD_DIM, WINDOW
    G = Hq // Hk
    qkv = h @ w_qkv + b_qkv
    q = qkv[..., :Hq * Dh]
    k = qkv[..., Hq * Dh:(Hq + Hk) * Dh].reshape(B, S, Hk, Dh)
    v = qkv[..., (Hq + Hk) * Dh:].reshape(B, S, Hk, Dh)
    n_blk = S // W
    qb = q.reshape(B, n_blk, W, Hk, G, Dh)

    def band(t):
        tb = t.reshape(B, n_blk, W, Hk, Dh)
        prev = jnp.pad(tb[:, :-1], ((0, 0), (1, 0), (0, 0), (0, 0), (0, 0)))
        return jnp.concatenate([prev, tb], axis=2)

    kb, vb = band(k), band(v)
    s = jnp.einsum('bnqkgd,bnjkd->bnkgqj', qb, kb).astype(jnp.float32) * (Dh ** -0.5)
    dist = W + jnp.arange(W)[:, None] - jnp.arange(2 * W)[None, :]
    in_window = (dist >= 0) & (dist < W)
    real_key = (jnp.arange(n_blk)[:, None] > 0) | (jnp.arange(2 * W)[None, :] >= W)
    mask = in_window[None] & real_key[:, None, :]
    slopes = alibi_slopes(Hq).reshape(Hk, G)
    s = s - slopes[:, :, None, None] * dist.astype(jnp.float32)
    s = jnp.where(mask[None, :, None, None], s, -jnp.inf)
    sink = sinks.astype(jnp.float32).reshape(Hk, G)[:, :, None]
    m = jnp.maximum(s.max(axis=-1), sink)
    p = jnp.exp(s - m[..., None])
    denom = p.sum(axis=-1) + jnp.exp(sink - m)
    p = (p / denom[..., None]).astype(vb.dtype)
    o = jnp.einsum('bnkgqj,bnjkd->bnqkgd', p, vb).reshape(B, S, Hq * Dh)
    return o @ w_o + b_o


def setup_inputs(seed: int = 0) -> dict:
    key = jax.random.key(seed)
    ks = jax.random.split(key, 24)
    f32 = jnp.float32

    def w(k, shape, fan_in, gain=1.0):
        return jax.random.normal(k, shape, f32) * (gain * fan_in ** -0.5)

    def g(k, shape):
        return 1.0 + 0.05 * jax.random.normal(k, shape, f32)

    A, Bn = N_LAYERS_A, N_LAYERS_B
    x = jax.random.normal(ks[0], (BATCH, SEQ, D_MODEL), f32)
    c = jax.random.normal(ks[1], (BATCH, D_MODEL), f32)
    positions = (jnp.arange(SEQ, dtype=jnp.int32)[None, :]
                 + jax.random.randint(ks[2], (BATCH, 1), 0, 1024, dtype=jnp.int32))
    return {
        'x': x,
        'c': c,
        'positions': positions,
        'w_ada': w(ks[3], (DEPTH, D_MODEL, 6 * D_MODEL), D_MODEL, 0.5),
        'b_ada': 0.02 * jax.random.normal(ks[4], (DEPTH, 6 * D_MODEL), f32),
        'g_mix': g(ks[5], (DEPTH, D_MODEL)),
        'g_mlp': g(ks[6], (DEPTH, D_MODEL)),
        'mla_w_dq': w(ks[7], (A, D_MODEL, Q_LORA), D_MODEL),
        'mla_g_q': g(ks[8], (A, Q_LORA)),
        'mla_w_uq': w(ks[9], (A, Q_LORA, MLA_HEADS * (QK_NOPE + QK_ROPE)), Q_LORA),
        'mla_w_dkv': w(ks[10], (A, D_MODEL, KV_LORA + QK_ROPE), D_MODEL),
        'mla_g_kv': g(ks[11], (A, KV_LORA)),
        'mla_w_ukv': w(ks[12], (A, KV_LORA, MLA_HEADS * (QK_NOPE + V_DIM)), KV_LORA),
        'mla_w_o': w(ks[13], (A, MLA_HEADS * V_DIM, D_MODEL), MLA_HEADS * V_DIM),
        'swa_w_qkv': w(ks[14], (Bn, D_MODEL, (SWA_HEADS + 2 * SWA_KV_HEADS) * SWA_HEAD_DIM), D_MODEL),
        'swa_b_qkv': 0.02 * jax.random.normal(ks[15], (Bn, (SWA_HEADS + 2 * SWA_KV_HEADS) * SWA_HEAD_DIM), f32),
        'swa_sinks': 0.5 * jax.random.normal(ks[16], (Bn, SWA_HEADS), f32),
        'swa_w_o': w(ks[17], (Bn, SWA_HEADS * SWA_HEAD_DIM, D_MODEL), SWA_HEADS * SWA_HEAD_DIM),
        'swa_b_o': 0.02 * jax.random.normal(ks[18], (Bn, D_MODEL), f32),
        'w_ff1': w(ks[19], (DEPTH, D_MODEL, D_FF), D_MODEL),
        'w_ff2': w(ks[20], (DEPTH, D_FF, D_MODEL), D_FF),
        'g_final': g(ks[21], (D_MODEL,)),
    }


def reference(x, c, positions, w_ada, b_ada, g_mix, g_mlp,
              mla_w_dq, mla_g_q, mla_w_uq, mla_w_dkv, mla_g_kv, mla_w_ukv, mla_w_o,
              swa_w_qkv, swa_b_qkv, swa_sinks, swa_w_o, swa_b_o,
              w_ff1, w_ff2, g_final):
    cond = jax.nn.silu(c)
    for i in range(DEPTH):
        mod = cond @ w_ada[i] + b_ada[i]
        sh1, sc1, gt1, sh2, sc2, gt2 = jnp.split(mod, 6, axis=-1)
        h = modulate(x, g_mix[i], sh1, sc1)
        j = i // N_MIXERS
        if i % N_MIXERS == 0:
            y = mla(h, positions, mla_w_dq[j], mla_g_q[j], mla_w_uq[j], mla_w_dkv[j],
                    mla_g_kv[j], mla_w_ukv[j], mla_w_o[j])
        else:
            y = swa(h, swa_w_qkv[j], swa_b_qkv[j], swa_sinks[j], swa_w_o[j], swa_b_o[j])
        x = x + gt1[:, None, :] * y
        h = modulate(x, g_mlp[i], sh2, sc2)
        y = jnp.square(jax.nn.relu(h @ w_ff1[i])) @ w_ff2[i]
        x = x + gt2[:, None, :] * y
    return rmsnorm(x, g_final)
```

```python
import math
import numpy as np
from contextlib import ExitStack
import concourse.bass as bass
import concourse.mybir as mybir
from concourse.bass_utils import run_bass_kernel_spmd

F32 = mybir.dt.float32
BF16 = mybir.dt.bfloat16
F16 = mybir.dt.float16
I32 = mybir.dt.int32
AF = mybir.ActivationFunctionType
ALU = mybir.AluOpType

D = 1024
KC = 8
S = 8192
NSEG = 2
BLK = 1024
HALO = 128
SEG = BLK + HALO
T0 = NSEG * SEG
CH = 384
NCHS = SEG // CH
NCH = T0 // CH
KCHUNK = 512
NKC = S // KCHUNK
EPS = 1e-6
MLA_SCALE = 192 ** -0.5
SWA_SCALE = 64 ** -0.5
TWO_PI = 2.0 * math.pi
CW_HI = 6.28125
CW_LO = TWO_PI - 6.28125
PI_SAFE = 3.1415925
NEG_BIG = -30000.0
POOL_KIB = 128


class Ev:
    __slots__ = ("key", "val", "clock", "needed", "eng", "idx")

    def __init__(self, key, val, eng, idx):
        self.key = key
        self.val = val
        self.eng = eng
        self.idx = idx
        self.clock = None
        self.needed = False


class Tile:
    __slots__ = ("name", "w", "r")

    def __init__(self, name=""):
        self.name = name
        self.w = None
        self.r = {}


ENGS = ["pe", "act", "dve", "pool", "sp"]


class Sched:
    def __init__(self):
        self.q = {e: [] for e in ENGS}
        self.seen = {e: {} for e in ENGS}
        self.cnt = {e: 0 for e in ENGS}
        self.dma_cnt = {}
        self.last = {e: None for e in ENGS}
        self.last_dma = {}
        self.bar = []

    def op(self, eng, fn, reads=(), writes=(), dma_key=None, extra=(), strict=False):
        deps = []
        for t in reads:
            if t.w is not None:
                deps.append(t.w)
        for t in writes:
            if t.w is not None:
                deps.append(t.w)
            deps.extend(t.r.values())
        deps.extend(extra)
        deps.extend(self.bar)
        seen = self.seen[eng]
        waits = {}
        for d in deps:
            if d.key == eng and not strict:
                continue
            if seen.get(d.key, -1) >= d.val:
                continue
            for k, v in d.clock.items():
                if seen.get(k, -1) < v:
                    seen[k] = v
            if seen.get(d.key, -1) < d.val:
                seen[d.key] = d.val
            d.needed = True
            cur = waits.get(d.key)
            if cur is None or cur.val < d.val:
                waits[d.key] = d
        idx = self.cnt[eng]
        self.cnt[eng] += 1
        if dma_key is None:
            ev = Ev(eng, idx, eng, idx)
        else:
            v = self.dma_cnt.get(dma_key, 0) + 16
            self.dma_cnt[dma_key] = v
            ev = Ev("dma:" + dma_key, v, eng, idx)
            ev.needed = True
            self.last_dma[dma_key] = ev
        clock = dict(seen)
        clock[eng] = idx
        if dma_key is not None:
            clock[eng] = idx - 1
        ev.clock = clock
        self.q[eng].append((fn, list(waits.values()), ev, dma_key))
        self.last[eng] = ev
        for t in writes:
            t.w = ev
            t.r = {}
        for t in reads:
            t.r[ev.key] = ev
        return ev

    def barrier(self):
        evs = [e for e in self.last.values() if e is not None and not e.key.startswith("dma:")]
        evs += list(self.last_dma.values())
        self.bar = evs


class PoolAlloc:
    def __init__(self, ap, nbytes):
        self.ap = ap
        self.nbytes = nbytes
        self.off = 0

    def alloc(self, shape, dtype):
        esz = 4 if dtype in (F32, I32) else 2
        n = 1
        for s in shape[1:]:
            n *= s
        nb = (n * esz + 63) // 64 * 64
        assert self.off + nb <= self.nbytes, f"pool overflow {self.off + nb} > {self.nbytes}"
        a = self.ap[:, self.off // 4:(self.off + nb) // 4]
        self.off += nb
        if dtype != F32:
            a = a.bitcast(dtype)
        a = a[:, 0:n]
        if len(shape) == 3:
            a = a.rearrange("p (a b) -> p a b", a=shape[1])
        elif len(shape) == 4:
            a = a.rearrange("p (a b c) -> p a b c", a=shape[1], b=shape[2])
        if shape[0] < 128:
            a = a[0:shape[0]]
        return a

    def mark(self):
        return self.off

    def release(self, m):
        self.off = m


def build_program(stop_after="all", debug=False):
    nc = bass.Bass("TRN2", target_bir_lowering=False)
    sc = Sched()
    dumps = []

    def din(name, shape, dt=F32):
        return nc.dram_tensor(name, list(shape), dt, kind="ExternalInput").ap()

    xT = din("xT", [D, T0])
    xTall = din("xTall", [D, S])
    posq = din("posq", [1, T0], I32)
    posall = din("posall", [1, S], I32)
    segmeta = din("segmeta", [1, 4])
    cT = din("cT", [128, KC])
    w_ada = din("w_ada", [2, D, 6 * D])
    b_adaT = din("b_adaT", [2, 128, 48])
    gmixT = din("gmixT", [2, 128, KC])
    gmlpT = din("gmlpT", [2, 128, KC])
    gfinT = din("gfinT", [128, KC])
    w_dq = din("w_dq", [D, 384])
    g_qT = din("g_qT", [128, 3])
    w_uq = din("w_uq", [384, 1536])
    w_uq_sw = din("w_uq_sw", [384, 512])
    w_dkv = din("w_dkv", [D, 320])
    w_dkr_sw = din("w_dkr_sw", [D, 64])
    g_kvT = din("g_kvT", [128, 2])
    w_ukv = din("w_ukv", [256, 2048])
    w_o = din("w_o", [D, D])
    invf = din("invf", [64, 1])
    w_qkv = din("w_qkv", [D, 1536])
    w_k_sw = din("w_k_sw", [D, 256])
    b_qkvT = din("b_qkvT", [128, 12])
    b_k_swT = din("b_k_swT", [128, 2])
    b_v = din("b_v", [1, 256])
    sinks = din("sinks", [1, 16])
    w_o1 = din("w_o1", [D, D])
    b_oT = din("b_oT", [128, KC])
    w_ff1 = din("w_ff1", [2, D, 4 * D])
    w_ff2 = din("w_ff2", [2, 4 * D, D])
    outT = nc.dram_tensor("outT", [D, NSEG * BLK], F32, kind="ExternalOutput").ap()

    es = ExitStack()
    with es:
        E = es.enter_context
        X = E(nc.sbuf_tensor("X", [128, KC, T0], F32))
        POOLT = E(nc.sbuf_tensor("POOLT", [128, POOL_KIB * 256], F32))
        CONST = E(nc.sbuf_tensor("CONST", [128, 16], F32))
        MODS = E(nc.sbuf_tensor("MODS", [128, 2, 48], F32))
        AB = E(nc.sbuf_tensor("AB", [128, 2, 2, KC], F32))
        SMALL = E(nc.sbuf_tensor("SMALL", [128, 64], F32))
        ONESB = E(nc.sbuf_tensor("ONESB", [128, 128], BF16))
        IDENT = E(nc.sbuf_tensor("IDENT", [128, 128], BF16))
        QLOC = E(nc.sbuf_tensor("QLOC", [128, SEG], F16))
        KK = E(nc.sbuf_tensor("KK", [128, NSEG, 64], F32))
        META = E(nc.sbuf_tensor("META", [128, 4], F32))
        BM = E(nc.sbuf_tensor("BM", [128, NSEG, NCHS, 64], F32))
        CONDB = E(nc.sbuf_tensor("CONDB", [128, KC], BF16))
        PS = [E(nc.psum_tensor(f"PS{i}", [128, 512], F32)) for i in range(8)]
        pool = PoolAlloc(POOLT, POOL_KIB * 1024)

        tX = [[Tile(f"X{s}_{c}") for c in range(NCHS)] for s in range(NSEG)]
        tPS = [Tile(f"PS{i}") for i in range(8)]
        tCONST = Tile("const")

        C_GQ = 0
        C_GKV = 3
        C_INVF = 5
        C_SGN = 6
        C_HB = 7
        C_GFIN = 16
        C_BO = 24
        C_GB = 32
        C_BQ = 40
        C_BKSW = 52

        def dump(name, ap, tiles, dt=F32):
            if not debug:
                return
            shape = list(ap.shape)
            dr = nc.dram_tensor("dbg_" + name, shape, dt, kind="ExternalOutput").ap()
            dumps.append("dbg_" + name)
            sc.op("sp", lambda e, dr=dr, ap=ap: e.dma_start(out=dr, in_=ap), reads=tiles, dma_key="dump")


        def MM(out, lhsT, rhs, start, stop, reads, writes):
            return sc.op("pe", lambda e: e.matmul(out, lhsT, rhs, start=start, stop=stop), reads=reads, writes=writes)

        def ACT(out, in_, func, reads, writes, bias=None, scale=None):
            kw = {}
            if bias is not None:
                kw["bias"] = bias
            if scale is not None:
                kw["scale"] = scale
            return sc.op("act", lambda e: e.activation(out=out, in_=in_, func=func, **kw), reads=reads, writes=writes)

        def AMUL(out, in_, c, reads, writes):
            return sc.op("act", lambda e: e.mul(out, in_, c), reads=reads, writes=writes)

        def ACOPY(out, in_, reads, writes):
            return sc.op("act", lambda e: e.copy(out, in_), reads=reads, writes=writes)

        def TT(out, in0, in1, op, reads, writes, strict=False):
            return sc.op("dve", lambda e: e.tensor_tensor(out, in0, in1, op), reads=reads, writes=writes, strict=strict)

        def TS(out, in0, s1, s2, op0, op1, reads, writes, strict=False):
            if op1 is None:
                return sc.op("dve", lambda e: e.tensor_scalar(out, in0, s1, s2, op0), reads=reads, writes=writes, strict=strict)
            return sc.op("dve", lambda e: e.tensor_scalar(out, in0, s1, s2, op0, op1), reads=reads, writes=writes, strict=strict)

        def STT(out, in0, scalar, in1, op0, op1, reads, writes, strict=False):
            return sc.op("dve", lambda e: e.scalar_tensor_tensor(out, in0, scalar, in1, op0, op1), reads=reads, writes=writes, strict=strict)

        def RECIP(out, in_, reads, writes, strict=False):
            return sc.op("dve", lambda e: e.reciprocal(out, in_), reads=reads, writes=writes, strict=strict)

        def VCOPY(out, in_, reads, writes):
            return sc.op("dve", lambda e: e.tensor_copy(out, in_), reads=reads, writes=writes)

        def DMA(q, out, in_, key, reads=(), writes=()):
            return sc.op(q, lambda e: e.dma_start(out=out, in_=in_), reads=reads, writes=writes, dma_key=key)

        tl = Tile("smallloads")

        def setup():
            P = "pool"
            sc.op(P, lambda e: e.memset(ONESB[:], 1.0), writes=[tCONST])
            sc.op(P, lambda e: e.memset(CONST[:, 0:1], EPS), writes=[tCONST])
            sc.op(P, lambda e: e.memset(CONST[:, 1:2], 0.0), writes=[tCONST])
            sc.op(P, lambda e: e.memset(CONST[:, 2:3], 1e-18), writes=[tCONST])
            sc.op(P, lambda e: e.memset(SMALL[0:32, C_SGN:C_SGN + 1], -1.0), writes=[tCONST])
            sc.op(P, lambda e: e.memset(SMALL[32:64, C_SGN:C_SGN + 1], 1.0), writes=[tCONST])
            tmp = pool.alloc([128, 128], F32)
            sc.op(P, lambda e: e.iota(tmp, pattern=[[1, 128]], base=0, channel_multiplier=-1,
                                      allow_small_or_imprecise_dtypes=True), writes=[tCONST])
            sc.op(P, lambda e: e.tensor_single_scalar(IDENT[:], tmp, 0.0, ALU.is_equal), writes=[tCONST])
            sc.op(P, lambda e: e.iota(QLOC[:], pattern=[[1, SEG]], base=0, channel_multiplier=0,
                                      allow_small_or_imprecise_dtypes=True), writes=[tCONST])
            kid = pool.alloc([128, 64], F32)
            sc.op(P, lambda e: e.iota(kid, pattern=[[128, 64]], base=0, channel_multiplier=1,
                                      allow_small_or_imprecise_dtypes=True), writes=[tCONST])
            loads = [
                (META[:], segmeta.partition_broadcast(128)),
                (SMALL[:, C_GQ:C_GQ + 3], g_qT),
                (SMALL[:, C_GKV:C_GKV + 2], g_kvT),
                (SMALL[0:64, C_INVF:C_INVF + 1], invf),
                (SMALL[:, C_GFIN:C_GFIN + 8], gfinT),
                (SMALL[:, C_BO:C_BO + 8], b_oT),
                (SMALL[:, C_BQ:C_BQ + 12], b_qkvT),
                (SMALL[:, C_BKSW:C_BKSW + 2], b_k_swT),
            ]
            for o, i in loads:
                DMA("sp", o, i, "c0", writes=[tl])
            kid0 = pool.alloc([128, 64], F32)
            kb = pool.alloc([128, 64], F32)
            sc.op(P, lambda e: e.iota(kid0, pattern=[[128, 64]], base=0, channel_multiplier=0,
                                      allow_small_or_imprecise_dtypes=True), writes=[tCONST])
            for s in range(NSEG):
                TS(KK[:, s, :], kid, META[:, s:s + 1], None, ALU.subtract, None, [tl, tCONST], [tCONST])
                TS(kb, kid0, META[:, s:s + 1], None, ALU.subtract, None, [tl, tCONST], [tCONST])
                for c in range(NCHS):
                    TS(BM[:, s, c, :], kb, 384.0 * c + 383.0, NEG_BIG, ALU.is_gt, ALU.mult, [], [tCONST])
            TS(SMALL[:, C_HB:C_HB + 2], META[:, 2:4], -1.0, -NEG_BIG, ALU.add, ALU.mult, [tl], [tCONST])

        setup()

        tmod = Tile("mods")

        def phase_A():
            m0 = pool.mark()
            cf = pool.alloc([128, KC], F32)
            tc_ = Tile("c")
            DMA("sp", cf, cT, "c1", writes=[tc_])
            ACT(CONDB[:], cf, AF.Silu, [tc_], [tCONST])
            badd = pool.alloc([128, 2, 48], F32)
            gm = pool.alloc([128, 2, 2, KC], F32)
            tb = Tile("badd")
            DMA("sp", badd, b_adaT.rearrange("l p n -> p l n"), "c1", writes=[tb])
            DMA("sp", gm[:, :, 0, :], gmixT.rearrange("l p n -> p l n"), "c1", writes=[tb])
            DMA("sp", gm[:, :, 1, :], gmlpT.rearrange("l p n -> p l n"), "c1", writes=[tb])
            wa = [pool.alloc([128, KC, 768], BF16) for _ in range(2)]
            twa = [Tile("wa0"), Tile("wa1")]
            n = 0
            for l in range(2):
                wsrc = w_ada[l].rearrange("(kc p) n -> p kc n", p=128)
                for cc in range(8):
                    b = n % 2
                    n += 1
                    DMA("pool", wa[b], wsrc[:, :, cc * 768:(cc + 1) * 768], f"wa{b}", writes=[twa[b]])
                    for nn in range(6):
                        col = l * 48 + cc * 6 + nn
                        for kc in range(KC):
                            MM(PS[0][:, col:col + 1], wa[b][:, kc, nn * 128:(nn + 1) * 128], CONDB[:, kc:kc + 1],
                               kc == 0, kc == KC - 1, [twa[b], tCONST], [tPS[0]])
                TT(MODS[:, l, :], PS[0][:, l * 48:(l + 1) * 48], badd[:, l, :], ALU.add, [tPS[0], tb], [tmod])
                STT(AB[:, l, 0, :], MODS[:, l, 8:16], 1.0, gm[:, l, 0, :], ALU.add, ALU.mult, [tb, tmod], [tmod], strict=True)
                STT(AB[:, l, 1, :], MODS[:, l, 32:40], 1.0, gm[:, l, 1, :], ALU.add, ALU.mult, [tb], [tmod])
            dump("mods", MODS[:], [tmod])
            sc.barrier()
            pool.release(m0)

        phase_A()

        def SH(l, which):
            return MODS[:, l, 0:8] if which == 0 else MODS[:, l, 24:32]

        def GT(l, which):
            return MODS[:, l, 16:24] if which == 0 else MODS[:, l, 40:48]

        def norm_rstd(src3, nk, n, sqbuf, ps_i, rstd, dim, reads, tsq, trstd):
            ACT(sqbuf, src3, AF.Square, reads, [tsq])
            for k in range(nk):
                MM(PS[ps_i][:, 0:n], ONESB[:], sqbuf[:, k, :], k == 0, k == nk - 1, [tsq, tCONST], [tPS[ps_i]])
            ACT(rstd, PS[ps_i][:, 0:n], AF.Ln, [tPS[ps_i]], [trstd], bias=CONST[:, 0:1], scale=1.0 / dim)
            ACT(rstd, rstd, AF.Exp, [], [trstd], scale=-0.5)

        def rope_tables(pos_ap, ang, nq, r, cos2, sin2, tpos, ttab, tt):
            inv = SMALL[0:64, C_INVF:C_INVF + 1]
            sgn = SMALL[0:64, C_SGN:C_SGN + 1]
            VCOPY(ang, pos_ap, [tpos], [tt])
            TS(ang, ang, inv, None, ALU.mult, None, [tl], [tt])
            for which in range(2):
                if which == 1:
                    TS(ang, ang, math.pi / 2, None, ALU.add, None, [], [tt])
                TS(r, ang, 1.0 / TWO_PI, None, ALU.mult, None, [], [tt])
                VCOPY(nq, r, [], [tt])
                STT(r, nq, -CW_HI, ang, ALU.mult, ALU.add, [], [tt])
                STT(r, nq, -CW_LO, r, ALU.mult, ALU.add, [], [tt])
                TS(r, r, -PI_SAFE, PI_SAFE, ALU.max, ALU.min, [], [tt])
                if which == 0:
                    ACT(sin2, r, AF.Sin, [tt, tCONST], [ttab], scale=sgn)
                else:
                    ACT(cos2, r, AF.Sin, [tt], [ttab])

        LAT = pool.alloc([128, 2, S], BF16)
        KR = pool.alloc([128, S], BF16)
        tKRz = Tile("krz")
        sc.op("pool", lambda e: e.memset(KR[64:128, :], 0.0), writes=[tKRz])
        tLAT = [Tile(f"lat{i}") for i in range(NKC)]
        tKR = [Tile(f"kr{i}") for i in range(NKC)]
        m_attn = pool.mark()

        def phase_B():
            wdkv = pool.alloc([128, KC, 384], BF16)
            twd = Tile("wdkv")
            twd2 = Tile("wdkv2")
            DMA("pool", wdkv[:, :, 0:320], w_dkv.rearrange("(kc p) n -> p kc n", p=128), "wdkv", writes=[twd])
            DMA("pool", wdkv[:, :, 320:384], w_dkr_sw.rearrange("(kc p) n -> p kc n", p=128), "wdkv2", writes=[twd2])
            xa = [pool.alloc([128, KC, KCHUNK], F32) for _ in range(2)]
            hb = [pool.alloc([128, KC, KCHUNK], BF16) for _ in range(2)]
            posb = [pool.alloc([64, KCHUNK], I32) for _ in range(2)]
            rstd = pool.alloc([128, KCHUNK], F32)
            rstd2 = pool.alloc([128, KCHUNK], F32)
            sqc = pool.alloc([128, 2, KCHUNK], BF16)
            ang = pool.alloc([64, KCHUNK], F32)
            nq = pool.alloc([64, KCHUNK], I32)
            rr = pool.alloc([64, KCHUNK], F32)
            cos2 = pool.alloc([64, KCHUNK], F32)
            sin2 = pool.alloc([64, KCHUNK], F32)
            ku = pool.alloc([64, KCHUNK], F32)
            kv = pool.alloc([64, KCHUNK], F32)
            txa = [Tile("xa0"), Tile("xa1")]
            th = [Tile("h0"), Tile("h1")]
            tpos = [Tile("pos0"), Tile("pos1")]
            trs = Tile("rstd")
            trs2 = Tile("rstd2")
            tsqc = Tile("sqc")
            ttab = Tile("tab")
            ttmp = Tile("ropetmp")
            tku = Tile("ku")
            xsrc = xTall.rearrange("(kc p) t -> p kc t", p=128)

            txak = [[Tile(f"xa{b}_{k}") for k in range(KC)] for b in range(2)]

            def A1(i):
                b = i % 2
                cols = slice(i * KCHUNK, (i + 1) * KCHUNK)
                DMA("sp", xa[b], xsrc[:, :, cols], f"xa{b}", writes=[txa[b]] + txak[b])
                DMA("sp", posb[b], posall[:, cols].partition_broadcast(64), f"pos{b}", writes=[tpos[b]])
                norm_rstd(xa[b], KC, KCHUNK, hb[b], b, rstd, D, [txa[b]], th[b], trs)
                for kc in range(KC):
                    TT(xa[b][:, kc, :], xa[b][:, kc, :], rstd, ALU.mult, [trs, txa[b]], [txak[b][kc]])

            def A2(i):
                b = i % 2
                for kc in range(KC):
                    ACT(hb[b][:, kc, :], xa[b][:, kc, :], AF.Identity, [txak[b][kc], tmod], [th[b]],
                        bias=SH(0, 0)[:, kc:kc + 1], scale=AB[:, 0, 0, kc:kc + 1])

            def B1(i):
                b = i % 2
                cols = slice(i * KCHUNK, (i + 1) * KCHUNK)
                for (pi, c0, c1, m) in [(2, 0, 128, 128), (3, 128, 256, 128), (4, 256, 320, 64), (5, 320, 384, 64)]:
                    for kc in range(KC):
                        MM(PS[pi][0:m, 0:KCHUNK], wdkv[:, kc, c0:c1], hb[b][:, kc, :], kc == 0, kc == KC - 1,
                           [th[b], twd, twd2], [tPS[pi]])
                ACT(sqc[:, 0, :], PS[2][:, 0:KCHUNK], AF.Square, [tPS[2]], [tsqc])
                ACT(sqc[:, 1, :], PS[3][:, 0:KCHUNK], AF.Square, [tPS[3]], [tsqc])
                for k in range(2):
                    MM(PS[6][:, 0:KCHUNK], ONESB[:], sqc[:, k, :], k == 0, k == 1, [tsqc, tCONST], [tPS[6]])
                ACT(rstd2, PS[6][:, 0:KCHUNK], AF.Ln, [tPS[6]], [trs2], bias=CONST[:, 0:1], scale=1.0 / 256)
                ACT(rstd2, rstd2, AF.Exp, [], [trs2], scale=-0.5)
                for k in range(2):
                    STT(LAT[:, k, cols], PS[2 + k][:, 0:KCHUNK], SMALL[:, C_GKV + k:C_GKV + k + 1], rstd2, ALU.mult, ALU.mult,
                        [tPS[2 + k], trs2, tl], [tLAT[i]])

            def B2(i):
                cols = slice(i * KCHUNK, (i + 1) * KCHUNK)
                TT(ku, PS[4][0:64, 0:KCHUNK], cos2, ALU.mult, [tPS[4], ttab], [tku])
                TT(kv, PS[5][0:64, 0:KCHUNK], sin2, ALU.mult, [tPS[5], ttab], [tku])
                TT(KR[0:64, cols], ku, kv, ALU.add, [tku], [tKR[i]])

            A1(0)
            A2(0)
            rope_tables(posb[0], ang, nq, rr, cos2, sin2, tpos[0], ttab, ttmp)
            for i in range(NKC):
                if i + 1 < NKC:
                    A1(i + 1)
                B1(i)
                if i + 1 < NKC:
                    A2(i + 1)
                B2(i)
                if i + 1 < NKC:
                    rope_tables(posb[(i + 1) % 2], ang, nq, rr, cos2, sin2, tpos[(i + 1) % 2], ttab, ttmp)
            dump("lat", LAT, tLAT, BF16)
            dump("kr", KR[0:64, :], tKR, BF16)
            sc.barrier()

        xsrc_own = xT.rearrange("(kc p) t -> p kc t", p=128)
        for s in range(NSEG):
            for c in range(NCHS):
                cols = slice(s * SEG + c * CH, s * SEG + (c + 1) * CH)
                DMA("sp", X[:, :, cols], xsrc_own[:, :, cols], f"x{s}{c}", writes=[tX[s][c]])
        phase_B()
        pool.release(m_attn)
        if stop_after == "B":
            return finish(nc, sc, es, dumps, X, outT, tX, None)

        def attn_segment(s):
            m0 = pool.mark()
            NK = 4096 if s == 0 else 8192
            cqn = pool.alloc([128, 3, SEG], BF16)
            cos2q = pool.alloc([64, SEG], F32)
            sin2q = pool.alloc([64, SEG], F32)
            qn = pool.alloc([128, SEG], BF16)
            qr = pool.alloc([128, SEG], BF16)
            tqrz = Tile("qrz")
            sc.op("pool", lambda e: e.memset(qr[64:128, :], 0.0), writes=[tqrz])
            wh = [pool.alloc([128, 2304], BF16) for _ in range(2)]
            pbuf = [pool.alloc([128, CH], BF16) for _ in range(4)]
            obuf = [pool.alloc([128, CH], BF16) for _ in range(2)]
            rden = pool.alloc([128, CH], F32)
            tcqn = [Tile(f"cqn{c}") for c in range(NCHS)]
            ttabq = [Tile(f"tabq{c}") for c in range(NCHS)]
            tqn = [Tile(f"qn{c}") for c in range(NCHS)]
            tqr = [Tile(f"qr{c}") for c in range(NCHS)]
            twh = [[Tile(f"wh{b}_{k}") for k in range(4)] for b in range(2)]
            tp = [Tile(f"p{i}") for i in range(4)]
            tob = [Tile("ob0"), Tile("ob1")]
            trden = Tile("rden")
            m1 = pool.mark()
            wdq = pool.alloc([128, KC, 384], BF16)
            twdq = Tile("wdq")
            DMA("pool", wdq, w_dq.rearrange("(kc p) n -> p kc n", p=128), "wdq", writes=[twdq])
            hq = pool.alloc([128, KC, CH], BF16)
            tt = pool.alloc([128, KC, CH], F32)
            rstd = pool.alloc([128, CH], F32)
            rstdq = pool.alloc([128, CH], F32)
            sqq = pool.alloc([128, 3, CH], BF16)
            posb = pool.alloc([64, CH], I32)
            ang = pool.alloc([64, CH], F32)
            nq_ = pool.alloc([64, CH], I32)
            rr = pool.alloc([64, CH], F32)
            th = Tile("hq")
            tttk = [Tile(f"ttq{k}") for k in range(KC)]
            trs = Tile("rs")
            trsq = Tile("rsq")
            tsqq = Tile("sqq")
            tpos = Tile("posq")
            ttmp = Tile("ropetmpq")
            for c in range(NCHS):
                lc = slice(c * CH, (c + 1) * CH)
                gc = slice(s * SEG + c * CH, s * SEG + (c + 1) * CH)
                DMA("sp", posb, posq[:, gc].partition_broadcast(64), "posq", writes=[tpos])
                norm_rstd(X[:, :, gc], KC, CH, hq, 0, rstd, D, [tX[s][c]], th, trs)
                for kc in range(KC):
                    TT(tt[:, kc, :], X[:, kc, gc], rstd, ALU.mult, [trs, tX[s][c]], [tttk[kc]])
                    ACT(hq[:, kc, :], tt[:, kc, :], AF.Identity, [tttk[kc], tmod], [th],
                        bias=SH(0, 0)[:, kc:kc + 1], scale=AB[:, 0, 0, kc:kc + 1])
                for m in range(3):
                    for kc in range(KC):
                        MM(PS[1 + m][:, 0:CH], wdq[:, kc, m * 128:(m + 1) * 128], hq[:, kc, :], kc == 0, kc == KC - 1,
                           [th, twdq], [tPS[1 + m]])
                for m in range(3):
                    ACT(sqq[:, m, :], PS[1 + m][:, 0:CH], AF.Square, [tPS[1 + m]], [tsqq])
                for m in range(3):
                    MM(PS[4][:, 0:CH], ONESB[:], sqq[:, m, :], m == 0, m == 2, [tsqq, tCONST], [tPS[4]])
                ACT(rstdq, PS[4][:, 0:CH], AF.Ln, [tPS[4]], [trsq], bias=CONST[:, 0:1], scale=1.0 / 384)
                ACT(rstdq, rstdq, AF.Exp, [], [trsq], scale=-0.5)
                for m in range(3):
                    STT(cqn[:, m, lc], PS[1 + m][:, 0:CH], SMALL[:, C_GQ + m:C_GQ + m + 1], rstdq, ALU.mult, ALU.mult,
                        [tPS[1 + m], trsq, tl], [tcqn[c]])
                rope_tables(posb, ang, nq_, rr, cos2q[:, lc], sin2q[:, lc], tpos, ttabq[c], ttmp)
            if s == 0:
                dump("cqn0", cqn, tcqn, BF16)
            sc.barrier()
            pool.release(m1)
            kh = pool.alloc([128, NK], BF16)
            vh = pool.alloc([128, NK // 128, 128], BF16)
            u1 = pool.alloc([64, CH], F32)
            u2 = pool.alloc([64, CH], F32)
            NKCH = NK // 512
            tkh = [Tile(f"kh{i}") for i in range(NKCH)]
            tvh = [Tile(f"vh{i}") for i in range(NKCH)]
            tu = Tile("u")
            uq_src = w_uq.rearrange("(m p) n -> p m n", p=128)
            uqs_src = w_uq_sw.rearrange("(m p) n -> p m n", p=128)
            ukv_src = w_ukv.rearrange("(m p) n -> p m n", p=128)

            def wviews(b):
                w = wh[b]
                return (w[:, 0:576].rearrange("p (m n) -> p m n", m=3), w[:, 576:768].rearrange("p (m n) -> p m n", m=3),
                        w[:, 768:1280].rearrange("p (m n) -> p m n", m=2), w[:, 1280:2304])

            def load_wh(h):
                b = h % 2
                wuq, wuqs, wukv, wo_h = wviews(b)
                DMA("pool", wuq, uq_src[:, :, h * 192:(h + 1) * 192], f"wh{b}0", writes=[twh[b][0]])
                DMA("pool", wuqs, uqs_src[:, :, h * 64:(h + 1) * 64], f"wh{b}1", writes=[twh[b][1]])
                DMA("pool", wukv, ukv_src[:, :, h * 256:(h + 1) * 256], f"wh{b}2", writes=[twh[b][2]])
                DMA("pool", wo_h, w_o[h * 128:(h + 1) * 128, :], f"wh{b}3", writes=[twh[b][3]])

            load_wh(0)
            cnt = {"p": 0, "o": 0, "s": 0}

            def head(h):
                b = h % 2
                wuq, wuqs, wukv, wo_h = wviews(b)
                for c in range(NCHS):
                    lc = slice(c * CH, (c + 1) * CH)
                    for m in range(3):
                        MM(PS[6][:, 0:CH], wuq[:, m, 0:128], cqn[:, m, lc], m == 0, m == 2, [twh[b][0], tcqn[c]], [tPS[6]])
                    AMUL(qn[:, lc], PS[6][:, 0:CH], MLA_SCALE, [tPS[6]], [tqn[c]])
                    for m in range(3):
                        MM(PS[7][0:64, 0:CH], wuq[:, m, 128:192], cqn[:, m, lc], m == 0, m == 2, [twh[b][0], tcqn[c]], [tPS[7]])
                    STT(u1, PS[7][0:64, 0:CH], MLA_SCALE, cos2q[:, lc], ALU.mult, ALU.mult, [tPS[7], ttabq[c]], [tu])
                    for m in range(3):
                        MM(PS[7][0:64, 0:CH], wuqs[:, m, :], cqn[:, m, lc], m == 0, m == 2, [twh[b][1], tcqn[c]], [tPS[7]])
                    STT(u2, PS[7][0:64, 0:CH], MLA_SCALE, sin2q[:, lc], ALU.mult, ALU.mult, [tPS[7], ttabq[c]], [tu])
                    TT(qr[0:64, lc], u1, u2, ALU.add, [tu], [tqr[c]])
                for i in range(NKCH):
                    cols = slice(i * 512, (i + 1) * 512)
                    bk = 6 if i % 2 == 0 else 0
                    bv_ = 7 if i % 2 == 0 else 1
                    for m in range(2):
                        MM(PS[bk][:, 0:512], wukv[:, m, 0:128], LAT[:, m, cols], m == 0, m == 1, [twh[b][2], tLAT[i]], [tPS[bk]])
                    if i % 2 == 0:
                        VCOPY(kh[:, cols], PS[bk][:, 0:512], [tPS[bk]], [tkh[i]])
                    else:
                        ACOPY(kh[:, cols], PS[bk][:, 0:512], [tPS[bk]], [tkh[i]])
                    for t in range(4):
                        kt = i * 4 + t
                        for m in range(2):
                            MM(PS[bv_][:, t * 128:(t + 1) * 128], LAT[:, m, kt * 128:(kt + 1) * 128], wukv[:, m, 128:256],
                               m == 0, m == 1, [twh[b][2], tLAT[i]], [tPS[bv_]])
                    if i % 2 == 0:
                        ACOPY(vh[:, i * 4:(i + 1) * 4, :], PS[bv_][:, 0:512].rearrange("p (a b) -> p a b", a=4), [tPS[bv_]], [tvh[i]])
                    else:
                        VCOPY(vh[:, i * 4:(i + 1) * 4, :], PS[bv_][:, 0:512].rearrange("p (a b) -> p a b", a=4), [tPS[bv_]], [tvh[i]])
                info = []
                tiles = []
                for c in range(NCHS):
                    if s == 0:
                        nkt = 26 + 3 * c
                        full_upto = 3 * c - 2
                    else:
                        nkt = 58 + 3 * c
                        full_upto = 30 + 3 * c
                    nkt = min(nkt, NK // 128)
                    ob = cnt["o"] % 2
                    cnt["o"] += 1
                    info.append((nkt, full_upto, ob))
                    tiles += [(c, kt) for kt in range(nkt)]

                def front(c, kt):
                    nkt, full_upto, ob = info[c]
                    lc = slice(c * CH, (c + 1) * CH)
                    sb = cnt["s"] % 2
                    cnt["s"] += 1
                    pS = PS[sb]
                    kcols = slice(kt * 128, (kt + 1) * 128)
                    MM(pS[:, 0:CH], kh[:, kcols], qn[:, lc], True, False, [tkh[kt // 4], tqn[c]], [tPS[sb]])
                    MM(pS[:, 0:CH], KR[:, kcols], qr[:, lc], False, True, [tKR[kt // 4], tqr[c], tKRz, tqrz], [tPS[sb]])
                    pb = cnt["p"] % len(pbuf)
                    cnt["p"] += 1
                    P = pbuf[pb]
                    p0s = [-1, 7, 15, 23] if s == 0 else [31, 39, 47, 55]
                    may_full = kt >= min(p0s) + 3 * c + 3
                    may_diag = any(0 <= kt - (p + 3 * c) <= 2 for p in p0s)
                    if may_full:
                        ACT(P, pS[:, 0:CH], AF.Exp, [tPS[sb], tCONST], [tp[pb]], bias=BM[:, s, c, kt:kt + 1], scale=1.0)
                    else:
                        ACT(P, pS[:, 0:CH], AF.Exp, [tPS[sb]], [tp[pb]])
                    if may_diag:
                        STT(P, QLOC[:, lc], KK[:, s, kt:kt + 1], P, ALU.is_ge, ALU.mult, [tCONST], [tp[pb]])
                    return pb

                def back(c, kt, pb):
                    nkt, full_upto, ob = info[c]
                    lc = slice(c * CH, (c + 1) * CH)
                    gc = slice(s * SEG + c * CH, s * SEG + (c + 1) * CH)
                    po = PS[2 + ob]
                    pl = PS[4 + ob]
                    P = pbuf[pb]
                    MM(po[:, 0:CH], vh[:, kt, :], P, kt == 0, kt == nkt - 1, [tp[pb], tvh[kt // 4]], [tPS[2 + ob]])
                    MM(pl[:, 0:CH], ONESB[:], P, kt == 0, kt == nkt - 1, [tp[pb], tCONST], [tPS[4 + ob]])
                    if kt != nkt - 1:
                        return
                    O = obuf[ob]
                    ACT(rden, pl[:, 0:CH], AF.Ln, [tPS[4 + ob], tCONST], [trden], bias=CONST[:, 2:3], scale=1.0)
                    ACT(rden, rden, AF.Exp, [], [trden], scale=-1.0)
                    TT(O, po[:, 0:CH], rden, ALU.mult, [tPS[2 + ob], trden], [tob[ob]])

                    def proj():
                        for dm in range(KC):
                            MM(PS[6 + dm % 2][:, 0:CH], wo_h[:, dm * 128:(dm + 1) * 128], O, True, True, [tob[ob], twh[b][3]], [tPS[6 + dm % 2]])
                            STT(X[:, dm, gc], PS[6 + dm % 2][:, 0:CH], GT(0, 0)[:, dm:dm + 1], X[:, dm, gc], ALU.mult, ALU.add,
                                [tPS[6 + dm % 2], tmod], [tX[s][c]])
                    deferred.append([8, proj])

                DEPTH = 2
                pend = []
                deferred = []

                def tick():
                    for d in deferred:
                        d[0] -= 1
                    while deferred and deferred[0][0] <= 0:
                        deferred.pop(0)[1]()

                for (c, kt) in tiles:
                    pend.append((c, kt, front(c, kt)))
                    if len(pend) > DEPTH:
                        back(*pend.pop(0))
                        tick()
                while pend:
                    back(*pend.pop(0))
                    tick()
                while deferred:
                    deferred.pop(0)[1]()

            for h in range(8):
                if h + 1 < 8:
                    load_wh(h + 1)
                head(h)
            sc.barrier()
            pool.release(m0)

        for s in range(NSEG):
            attn_segment(s)
        allX = [t for ts in tX for t in ts]
        dump("xa0", X[:], allX)
        if stop_after == "L0attn":
            return finish(nc, sc, es, dumps, X, outT, tX, None)
        pool.release(0)

        def norm_mod_to(l, which, H, tH):
            tt = pool.alloc([128, KC, CH], F32)
            rstd = pool.alloc([128, CH], F32)
            tttk = [Tile(f"tt{k}") for k in range(KC)]
            trs = Tile("rs")
            gmul = AB[:, l, which, :]
            for s in range(NSEG):
                for c in range(NCHS):
                    gc = slice(s * SEG + c * CH, s * SEG + (c + 1) * CH)
                    i = s * NCHS + c
                    norm_rstd(X[:, :, gc], KC, CH, H[:, :, gc], i % 2, rstd, D, [tX[s][c]], tH[i], trs)
                    for kc in range(KC):
                        TT(tt[:, kc, :], X[:, kc, gc], rstd, ALU.mult, [trs, tX[s][c]], [tttk[kc]])
                        ACT(H[:, kc, gc], tt[:, kc, :], AF.Identity, [tttk[kc], tmod], [tH[i]],
                            bias=SH(l, which)[:, kc:kc + 1], scale=gmul[:, kc:kc + 1])

        def mlp(l):
            m0 = pool.mark()
            H = pool.alloc([128, KC, T0], BF16)
            tH = [Tile(f"H{i}") for i in range(NCH)]
            W1 = [pool.alloc([128, KC, 1024], BF16) for _ in range(2)]
            W2 = [pool.alloc([128, 8, 1024], BF16) for _ in range(2)]
            tW1 = [Tile("W1a"), Tile("W1b")]
            tW2 = [Tile("W2a"), Tile("W2b")]
            A = [pool.alloc([128, 8, CH], BF16) for _ in range(2)]
            R = [pool.alloc([128, CH], BF16) for _ in range(2)]
            tA = [Tile("A0"), Tile("A1")]
            tR = [Tile("R0"), Tile("R1")]
            w1src = w_ff1[l].rearrange("(kc p) n -> p kc n", p=128)
            w2src = w_ff2[l].rearrange("(f p) n -> p f n", p=128)

            def loadW(fq):
                b = fq % 2
                DMA("pool", W1[b], w1src[:, :, fq * 1024:(fq + 1) * 1024], f"W1{b}", writes=[tW1[b]])
                DMA("pool", W2[b], w2src[:, fq * 8:(fq + 1) * 8, :], f"W2{b}", writes=[tW2[b]])

            loadW(0)
            m1 = pool.mark()
            norm_mod_to(l, 1, H, tH)
            pool.release(m1)
            cnt = {"u": 0, "a": 0, "y": 0}
            for fq in range(4):
                b = fq % 2
                if fq + 1 < 4:
                    loadW(fq + 1)
                for s in range(NSEG):
                    for c in range(NCHS):
                        gc = slice(s * SEG + c * CH, s * SEG + (c + 1) * CH)
                        i = s * NCHS + c
                        ab = cnt["a"] % 2
                        cnt["a"] += 1
                        for fc in range(8):
                            ub = cnt["u"] % 3
                            cnt["u"] += 1
                            for kc in range(KC):
                                MM(PS[ub][:, 0:CH], W1[b][:, kc, fc * 128:(fc + 1) * 128], H[:, kc, gc], kc == 0, kc == KC - 1,
                                   [tW1[b], tH[i]], [tPS[ub]])
                            rb = fc % 2
                            ACT(R[rb], PS[ub][:, 0:CH], AF.Relu, [tPS[ub]], [tR[rb]])
                            TT(A[ab][:, fc, :], R[rb], R[rb], ALU.mult, [tR[rb]], [tA[ab]])
                        for dm in range(KC):
                            yb = 3 + cnt["y"] % 2
                            cnt["y"] += 1
                            for fc in range(8):
                                MM(PS[yb][:, 0:CH], W2[b][:, fc, dm * 128:(dm + 1) * 128], A[ab][:, fc, :], fc == 0, fc == 7,
                                   [tW2[b], tA[ab]], [tPS[yb]])
                            STT(X[:, dm, gc], PS[yb][:, 0:CH], GT(l, 1)[:, dm:dm + 1], X[:, dm, gc], ALU.mult, ALU.add,
                                [tPS[yb], tmod], [tX[s][c]])
            sc.barrier()
            pool.release(m0)

        mlp(0)
        dump("xm0", X[:], allX)
        if stop_after == "L0":
            return finish(nc, sc, es, dumps, X, outT, tX, None)

        def swa_layer():
            l = 1
            m0 = pool.mark()
            H = pool.alloc([128, KC, T0], BF16)
            tH = [Tile(f"H1_{i}") for i in range(NCH)]
            KT = pool.alloc([128, 2, T0], BF16)
            KTs = pool.alloc([128, 2, T0], BF16)
            NT = T0 // 128
            VP = pool.alloc([128, NT, 4, 65], BF16)
            tKT = [Tile(f"KT{i}") for i in range(NCH)]
            tVP = [Tile(f"VP{i}") for i in range(NT)]
            tvone = Tile("vone")
            sc.op("pool", lambda e: e.memset(VP[:, :, :, 64:65], 1.0), writes=[tvone])
            bvb = pool.alloc([128, 256], F32)
            sk = pool.alloc([128, 16], F32)
            esink = pool.alloc([128, 16], F32)
            bqs = pool.alloc([128, 8], F32)
            m1 = pool.mark()
            norm_mod_to(l, 0, H, tH)
            sc.barrier()
            pool.release(m1)
            wkv = pool.alloc([128, KC, 768], BF16)
            twkv = [Tile("wkv0"), Tile("wkv1")]
            qsrc = w_qkv.rearrange("(kc p) n -> p kc n", p=128)
            DMA("pool", wkv[:, :, 0:512], qsrc[:, :, 1024:1536], "wkv0", writes=[twkv[0]])
            DMA("pool", wkv[:, :, 512:768], w_k_sw.rearrange("(kc p) n -> p kc n", p=128), "wkv1", writes=[twkv[1]])
            tbv = Tile("bvb")
            DMA("sp", bvb, b_v.partition_broadcast(128), "c2", writes=[tbv])
            tsk = Tile("sk")
            DMA("sp", sk, sinks.partition_broadcast(128), "c2", writes=[tsk])
            for i in range(NCH):
                gc = slice(i * CH, (i + 1) * CH)
                for m in range(2):
                    for kc in range(KC):
                        MM(PS[m][:, 0:CH], wkv[:, kc, m * 128:(m + 1) * 128], H[:, kc, gc], kc == 0, kc == KC - 1, [twkv[0], tH[i]], [tPS[m]])
                    ACT(KT[:, m, gc], PS[m][:, 0:CH], AF.Identity, [tPS[m], tl], [tKT[i]], bias=SMALL[:, C_BQ + 8 + m:C_BQ + 9 + m], scale=1.0)
                    for kc in range(KC):
                        MM(PS[2 + m][:, 0:CH], wkv[:, kc, 512 + m * 128:512 + (m + 1) * 128], H[:, kc, gc], kc == 0, kc == KC - 1,
                           [twkv[1], tH[i]], [tPS[2 + m]])
                    ACT(KTs[:, m, gc], PS[2 + m][:, 0:CH], AF.Identity, [tPS[2 + m], tl], [tKT[i]], bias=SMALL[:, C_BKSW + m:C_BKSW + m + 1], scale=1.0)
                for t3 in range(3):
                    t = i * 3 + t3
                    tc = slice(t * 128, (t + 1) * 128)
                    pb = 4 + t % 2
                    for kc in range(KC):
                        MM(PS[pb][:, 0:256], H[:, kc, tc], wkv[:, kc, 256:512], kc == 0, kc == KC - 1, [twkv[0], tH[i]], [tPS[pb]])
                    TT(VP[:, t, :, 0:64], PS[pb][:, 0:256].rearrange("p (g d) -> p g d", g=4), bvb[:].rearrange("p (g d) -> p g d", g=4),
                       ALU.add, [tPS[pb], tbv, tvone], [tVP[t]])
            sc.barrier()
            pool.release(m1)
            if stop_after == "L1kv":
                return True
            wq = pool.alloc([128, KC, 1024], BF16)
            twq = Tile("wq")
            DMA("pool", wq, qsrc[:, :, 0:1024], "wq", writes=[twq])
            BIAS = pool.alloc([128, 4, 2, 512], F32)
            tBIAS = Tile("bias")
            d0 = pool.alloc([128, 128], F32)
            mc = pool.alloc([128, 128], F32)
            mp = pool.alloc([128, 128], F32)
            sc.op("pool", lambda e: e.iota(d0, pattern=[[1, 128]], base=0, channel_multiplier=-1, allow_small_or_imprecise_dtypes=True),
                  writes=[tBIAS])
            TS(mc, d0, 0.0, NEG_BIG, ALU.is_lt, ALU.mult, [tBIAS], [tBIAS])
            TS(mp, d0, 0.0, NEG_BIG, ALU.is_ge, ALU.mult, [], [tBIAS])
            ORDR = [0, 2, 1, 3]
            for hd in range(16):
                g, hh = hd // 4, hd % 4
                j = ORDR.index(hh)
                slope = 2.0 ** (-(hd + 1) / 2.0)
                STT(BIAS[:, g, 1, j * 128:(j + 1) * 128], d0, -slope, mc, ALU.mult, ALU.add, [], [tBIAS])
                STT(BIAS[:, g, 0, j * 128:(j + 1) * 128], d0, -slope, mp, ALU.mult, ALU.add, [], [tBIAS])
                TS(BIAS[:, g, 0, j * 128:(j + 1) * 128], BIAS[:, g, 0, j * 128:(j + 1) * 128], -128.0 * slope, None, ALU.add, None, [], [tBIAS])
            ACT(esink, sk, AF.Exp, [tsk], [tBIAS])
            TS(bqs, SMALL[:, C_BQ:C_BQ + 8], SWA_SCALE, None, ALU.mult, None, [tl], [tBIAS])
            QT = [pool.alloc([128, 8, CH], BF16) for _ in range(2)]
            tQT = [Tile("QT0"), Tile("QT1")]
            SB = [pool.alloc([128, 512], F32) for _ in range(2)]
            tSB = [Tile("SB0"), Tile("SB1")]
            PB = [pool.alloc([128, 512], BF16) for _ in range(6)]
            tPB = [Tile(f"PB{i}") for i in range(6)]
            OTs = [pool.alloc([128, 1024], BF16) for _ in range(2)]
            tOTs = [Tile("OT0"), Tile("OT1")]
            den = pool.alloc([128, 4], F32)
            tden = Tile("den")
            PST = PS[7][:].bitcast(BF16)
            cnt = {"sb": 0, "pb": 0, "sc": 0}
            if stop_after == "L1bias":
                return True
            for i in range(NCH):
                s, c = i // NCHS, i % NCHS
                gc = slice(i * CH, (i + 1) * CH)
                qb = i % 2
                for m in range(8):
                    for kc in range(KC):
                        MM(PS[6][:, 0:CH], wq[:, kc, m * 128:(m + 1) * 128], H[:, kc, gc], kc == 0, kc == KC - 1, [twq, tH[i]], [tPS[6]])
                    ACT(QT[qb][:, m, :], PS[6][:, 0:CH], AF.Identity, [tPS[6], tBIAS], [tQT[qb]], bias=bqs[:, m:m + 1], scale=SWA_SCALE)
                def l1_front(t3, g):
                    lt = c * 3 + t3
                    t = i * 3 + t3
                    qc = slice(t3 * 128, (t3 + 1) * 128)
                    pbs = []
                    for kk in range(2):
                        kt = t - 1 + kk
                        kcols = slice(kt * 128, (kt + 1) * 128)
                        pair = 2 * (cnt["sc"] % 2)
                        cnt["sc"] += 1
                        for hh in range(4):
                            m = 2 * g + hh // 2
                            half = hh % 2
                            pr = slice(half * 64, (half + 1) * 64)
                            Ksrc = KT if half == g % 2 else KTs
                            bank = pair + half
                            col = (hh // 2) * 128
                            MM(PS[bank][:, col:col + 128], Ksrc[pr, g // 2, kcols], QT[qb][pr, m, qc], True, True,
                               [tKT[kt // 3], tQT[qb]], [tPS[bank]])
                        sb = cnt["sb"] % 2
                        cnt["sb"] += 1
                        TT(SB[sb][:, 0:256], PS[pair][:, 0:256], BIAS[:, g, kk, 0:256], ALU.add, [tPS[pair], tBIAS], [tSB[sb]])
                        TT(SB[sb][:, 256:512], PS[pair + 1][:, 0:256], BIAS[:, g, kk, 256:512], ALU.add, [tPS[pair + 1], tBIAS], [tSB[sb]])
                        pb = cnt["pb"] % 6
                        cnt["pb"] += 1
                        pbs.append(pb)
                        if lt == 1 and kk == 0:
                            ACT(PB[pb], SB[sb], AF.Exp, [tSB[sb], tCONST], [tPB[pb]], bias=SMALL[:, C_HB + s:C_HB + s + 1], scale=1.0)
                        else:
                            ACT(PB[pb], SB[sb], AF.Exp, [tSB[sb]], [tPB[pb]])
                    return pbs

                def l1_back(t3, g, pbs):
                    t = i * 3 + t3
                    pso = 4 + g % 2
                    oi = t % 2
                    for hh in range(4):
                        for kk in range(2):
                            kt = t - 1 + kk
                            pb = pbs[kk]
                            j = ORDR.index(hh)
                            MM(PS[pso][:, hh * 65:(hh + 1) * 65], PB[pb][:, j * 128:(j + 1) * 128], VP[:, kt, g, :], kk == 0, kk == 1,
                               [tPB[pb], tVP[kt], tvone], [tPS[pso]])
                    po3 = PS[pso][:, 0:260].rearrange("p (h d) -> p h d", h=4)
                    TT(den, po3[:, :, 64], esink[:, 4 * g:4 * g + 4], ALU.add, [tPS[pso], tBIAS], [tden], strict=True)
                    RECIP(den, den, [tden], [tden], strict=True)
                    for hh in range(4):
                        hd = 4 * g + hh
                        TS(OTs[oi][:, hd * 64:(hd + 1) * 64], PS[pso][:, hh * 65:hh * 65 + 64], den[:, hh:hh + 1], None, ALU.mult, None,
                           [tPS[pso], tden], [tOTs[oi]], strict=(hh == 0))
                    if g != 3:
                        return
                    for m in range(8):
                        sc.op("pe", lambda e, m=m, oi=oi: e.transpose(PST[:, m * 128:(m + 1) * 128], OTs[oi][:, m * 128:(m + 1) * 128], IDENT[:]),
                              reads=[tOTs[oi], tCONST], writes=[tPS[7]])
                    tcols = slice(t * 128, (t + 1) * 128)
                    ACOPY(H[:, :, tcols], PST.rearrange("p (m q) -> p m q", m=8), [tPS[7]], [tH[i]])

                items = [(t3, g) for t3 in range(3) if c * 3 + t3 != 0 for g in range(4)]
                pend = []
                for (t3, g) in items:
                    pend.append((t3, g, l1_front(t3, g)))
                    if len(pend) > 2:
                        l1_back(*pend.pop(0))
                while pend:
                    l1_back(*pend.pop(0))
            sc.barrier()
            pool.release(m1)
            wo1 = pool.alloc([128, KC, 1024], BF16)
            two = Tile("wo1")
            DMA("pool", wo1, w_o1.rearrange("(m p) n -> p m n", p=128), "wo1", writes=[two])
            gb = pool.alloc([128, 8], F32)
            tgb = Tile("gb")
            TT(gb, GT(1, 0), SMALL[:, C_BO:C_BO + 8], ALU.mult, [tmod, tl], [tgb])
            for i in range(NCH):
                s, c = i // NCHS, i % NCHS
                gc = slice(i * CH, (i + 1) * CH)
                for dm in range(KC):
                    pb = dm % 2
                    for m in range(8):
                        MM(PS[pb][:, 0:CH], wo1[:, m, dm * 128:(dm + 1) * 128], H[:, m, gc], m == 0, m == 7, [two, tH[i]], [tPS[pb]])
                    STT(X[:, dm, gc], PS[pb][:, 0:CH], GT(1, 0)[:, dm:dm + 1], X[:, dm, gc], ALU.mult, ALU.add, [tPS[pb], tmod], [tX[s][c]])
                    TS(X[:, dm, gc], X[:, dm, gc], gb[:, dm:dm + 1], None, ALU.add, None, [tgb], [tX[s][c]])
            sc.barrier()
            pool.release(m0)

        if swa_layer():
            return finish(nc, sc, es, dumps, X, outT, tX, None)
        dump("xa1", X[:], allX)
        if stop_after == "L1attn":
            return finish(nc, sc, es, dumps, X, outT, tX, None)
        mlp(1)
        dump("xm1", X[:], allX)

        def final_norm():
            sq = pool.alloc([128, KC, CH], BF16)
            rstd = pool.alloc([128, CH], F32)
            Y = [pool.alloc([128, KC, CH], F32) for _ in range(2)]
            tY = [Tile("Y0"), Tile("Y1")]
            tsq = Tile("sqf")
            trs = Tile("rsf")
            xo = outT.rearrange("(kc p) t -> p kc t", p=128)
            evs = []
            for i in range(NCH):
                s, c = i // NCHS, i % NCHS
                gc = slice(i * CH, (i + 1) * CH)
                yb = i % 2
                norm_rstd(X[:, :, gc], KC, CH, sq, i % 2, rstd, D, [tX[s][c]], tsq, trs)
                for kc in range(KC):
                    STT(Y[yb][:, kc, :], X[:, kc, gc], SMALL[:, C_GFIN + kc:C_GFIN + kc + 1], rstd, ALU.mult, ALU.mult,
                        [tX[s][c], trs, tl], [tY[yb]])
                lo = HALO if c == 0 else 0
                o0 = s * BLK + c * CH - HALO + lo
                evs.append(DMA("sp", xo[:, :, o0:o0 + CH - lo], Y[yb][:, :, lo:CH], "out", reads=[tY[yb]]))
            return evs

        final_norm()
        return finish(nc, sc, es, dumps, X, outT, tX, "done")


def finish(nc, sc, es, dumps, X, outT, tX, mode):
    if mode is None:
        xo = outT.rearrange("(kc p) t -> p kc t", p=128)
        for s in range(NSEG):
            src = X[:, :, s * SEG + HALO:(s + 1) * SEG]
            dst = xo[:, :, s * BLK:(s + 1) * BLK]
            sc.op("sp", lambda e, src=src, dst=dst: e.dma_start(out=dst, in_=src), reads=tX[s], dma_key="out")
    sc.op("sp", lambda e: None, extra=[sc.last_dma[k] for k in ("out", "dump") if k in sc.last_dma])
    emit(nc, sc, es)
    return nc, dumps


_LAST_SC = {}


def emit(nc, sc, es):
    _LAST_SC.clear()
    _LAST_SC.update({e: q for e, q in sc.q.items()})
    _LAST_SC['nwaits'] = [sum(len(w) for (_, w, _, _) in q) for q in sc.q.values()]
    E = es.enter_context
    signo = {}
    for eng in ENGS:
        n = 0
        for (fn, waits, ev, dk) in sc.q[eng]:
            if dk is None and ev.needed:
                n += 1
                signo[(eng, ev.idx)] = n
    esem = {eng: E(nc.semaphore("s_" + eng)) for eng in ENGS}
    dsem = {k: E(nc.semaphore("d_" + k)) for k in sc.dma_cnt}
    block = E(nc.Block())

    def replay(eng, e):
        for (fn, waits, ev, dk) in sc.q[eng]:
            for d in waits:
                if d.key.startswith("dma:"):
                    e.wait_ge(dsem[d.key[4:]], d.val)
                else:
                    e.wait_ge(esem[d.key], signo[(d.key, d.idx)])
            inst = fn(e)
            if inst is None:
                continue
            if dk is not None:
                inst.then_inc(dsem[dk], 16)
            elif ev.needed:
                inst.then_inc(esem[eng], 1)

    @block.tensor
    def _(e):
        replay("pe", e)

    @block.scalar
    def _(e):
        replay("act", e)

    @block.vector
    def _(e):
        replay("dve", e)

    @block.gpsimd
    def _(e):
        replay("pool", e)

    @block.sync
    def _(e):
        replay("sp", e)


def _core_layout(c):
    b, j = c // 4, c % 4
    blocks = [j, 7 - j]
    return b, blocks


def make_in_maps(inp):
    f32 = np.float32
    x = np.asarray(inp["x"], f32)
    pos = np.asarray(inp["positions"], np.int32)
    half = 32
    inv = (10000.0 ** (-np.arange(half, dtype=f32) / half)).astype(f32)
    invf = np.concatenate([inv, inv])[:, None].astype(f32)

    def featT(v):
        return np.ascontiguousarray(np.asarray(v, f32).reshape(-1, 128).T)

    w_uq = np.asarray(inp["mla_w_uq"][0], f32)
    uq = w_uq.reshape(384, 8, 192)
    w_uq_sw = np.ascontiguousarray(np.concatenate([uq[:, :, 160:192], uq[:, :, 128:160]], -1).reshape(384, 512))
    w_dkv = np.asarray(inp["mla_w_dkv"][0], f32)
    w_dkr_sw = np.ascontiguousarray(np.concatenate([w_dkv[:, 288:320], w_dkv[:, 256:288]], -1))
    w_qkv = np.asarray(inp["swa_w_qkv"][0], f32)
    b_qkv = np.asarray(inp["swa_b_qkv"][0], f32)
    wk = w_qkv[:, 1024:1280].reshape(1024, 2, 2, 64)
    w_k_sw = np.ascontiguousarray(wk[:, :, ::-1, :].reshape(1024, 256))
    bk = b_qkv[1024:1280].reshape(2, 2, 64)
    b_k_sw = np.ascontiguousarray(bk[:, ::-1, :].reshape(256))
    shared = {
        "w_ada": np.asarray(inp["w_ada"], f32),
        "b_adaT": np.ascontiguousarray(np.stack([featT(inp["b_ada"][l]) for l in range(2)])),
        "gmixT": np.ascontiguousarray(np.stack([featT(inp["g_mix"][l]) for l in range(2)])),
        "gmlpT": np.ascontiguousarray(np.stack([featT(inp["g_mlp"][l]) for l in range(2)])),
        "gfinT": featT(inp["g_final"]),
        "w_dq": np.asarray(inp["mla_w_dq"][0], f32),
        "g_qT": featT(inp["mla_g_q"][0]),
        "w_uq": w_uq,
        "w_uq_sw": w_uq_sw,
        "w_dkv": w_dkv,
        "w_dkr_sw": w_dkr_sw,
        "g_kvT": featT(inp["mla_g_kv"][0]),
        "w_ukv": np.asarray(inp["mla_w_ukv"][0], f32),
        "w_o": np.asarray(inp["mla_w_o"][0], f32),
        "invf": invf,
        "w_qkv": w_qkv,
        "w_k_sw": w_k_sw,
        "b_qkvT": featT(b_qkv),
        "b_k_swT": featT(b_k_sw),
        "b_v": np.ascontiguousarray(b_qkv[None, 1280:1536]),
        "sinks": np.asarray(inp["swa_sinks"], f32).reshape(1, 16),
        "w_o1": np.asarray(inp["swa_w_o"][0], f32),
        "b_oT": featT(inp["swa_b_o"][0]),
        "w_ff1": np.asarray(inp["w_ff1"], f32),
        "w_ff2": np.asarray(inp["w_ff2"], f32),
    }
    xT_b = [np.ascontiguousarray(x[b].T) for b in range(2)]
    maps = []
    for c in range(8):
        b, blocks = _core_layout(c)
        xt = np.zeros((D, T0), f32)
        pq = np.zeros((1, T0), np.int32)
        meta = np.zeros((1, 4), f32)
        for s, blk in enumerate(blocks):
            lo = blk * BLK - HALO
            hi = (blk + 1) * BLK
            lo_c = max(lo, 0)
            off = s * SEG + (lo_c - lo)
            xt[:, off:s * SEG + SEG] = xT_b[b][:, lo_c:hi]
            pq[0, off:s * SEG + SEG] = pos[b, lo_c:hi]
            meta[0, s] = lo
            meta[0, 2 + s] = 1.0 if lo >= 0 else 0.0
        m = dict(shared)
        m.update({
            "xT": xt, "xTall": xT_b[b], "posq": pq, "posall": np.ascontiguousarray(pos[b][None, :]),
            "segmeta": meta, "cT": featT(inp["c"][b]),
        })
        maps.append(m)
    return maps


def assemble(results):
    out = np.zeros((2, S, D), np.float32)
    for c in range(8):
        b, blocks = _core_layout(c)
        o = results[c]["outT"]
        for s, blk in enumerate(blocks):
            out[b, blk * BLK:(blk + 1) * BLK, :] = o[:, s * BLK:(s + 1) * BLK].T
    return out


_CACHE = {}


def kernel(**inputs):
    maps = make_in_maps(inputs)
    if "nc" not in _CACHE:
        _CACHE["nc"] = build_program()[0]
    res = run_bass_kernel_spmd(_CACHE["nc"], maps, core_ids=list(range(8)))
    return assemble(res.results)
```

```python
import math
import numpy as np
from contextlib import ExitStack
import concourse.bass as bass
import concourse.mybir as mybir
from concourse.bass_utils import run_bass_kernel_spmd

F32 = mybir.dt.float32
BF16 = mybir.dt.bfloat16
F16 = mybir.dt.float16
I32 = mybir.dt.int32
AF = mybir.ActivationFunctionType
ALU = mybir.AluOpType

D = 1024
KC = 8
S = 8192
NSEG = 2
BLK = 1024
HALO = 128
SEG = BLK + HALO
T0 = NSEG * SEG
CH = 384
NCHS = SEG // CH
NCH = T0 // CH
KCHUNK = 512
NKC = S // KCHUNK
EPS = 1e-6
MLA_SCALE = 192 ** -0.5
SWA_SCALE = 64 ** -0.5
TWO_PI = 2.0 * math.pi
CW_HI = 6.28125
CW_LO = TWO_PI - 6.28125
PI_SAFE = 3.1415925
NEG_BIG = -30000.0
POOL_KIB = 128


class Ev:
    __slots__ = ("key", "val", "clock", "needed", "eng", "idx")

    def __init__(self, key, val, eng, idx):
        self.key = key
        self.val = val
        self.eng = eng
        self.idx = idx
        self.clock = None
        self.needed = False


class Tile:
    __slots__ = ("name", "w", "r")

    def __init__(self, name=""):
        self.name = name
        self.w = None
        self.r = {}


ENGS = ["pe", "act", "dve", "pool", "sp"]


class Sched:
    def __init__(self):
        self.q = {e: [] for e in ENGS}
        self.seen = {e: {} for e in ENGS}
        self.cnt = {e: 0 for e in ENGS}
        self.dma_cnt = {}
        self.last = {e: None for e in ENGS}
        self.last_dma = {}
        self.bar = []

    def op(self, eng, fn, reads=(), writes=(), dma_key=None, extra=(), strict=False):
        deps = []
        for t in reads:
            if t.w is not None:
                deps.append(t.w)
        for t in writes:
            if t.w is not None:
                deps.append(t.w)
            deps.extend(t.r.values())
        deps.extend(extra)
        deps.extend(self.bar)
        seen = self.seen[eng]
        waits = {}
        for d in deps:
            if d.key == eng and not strict:
                continue
            if seen.get(d.key, -1) >= d.val:
                continue
            for k, v in d.clock.items():
                if seen.get(k, -1) < v:
                    seen[k] = v
            if seen.get(d.key, -1) < d.val:
                seen[d.key] = d.val
            d.needed = True
            cur = waits.get(d.key)
            if cur is None or cur.val < d.val:
                waits[d.key] = d
        idx = self.cnt[eng]
        self.cnt[eng] += 1
        if dma_key is None:
            ev = Ev(eng, idx, eng, idx)
        else:
            v = self.dma_cnt.get(dma_key, 0) + 16
            self.dma_cnt[dma_key] = v
            ev = Ev("dma:" + dma_key, v, eng, idx)
            ev.needed = True
            self.last_dma[dma_key] = ev
        clock = dict(seen)
        clock[eng] = idx
        if dma_key is not None:
            clock[eng] = idx - 1
        ev.clock = clock
        self.q[eng].append((fn, list(waits.values()), ev, dma_key))
        self.last[eng] = ev
        for t in writes:
            t.w = ev
            t.r = {}
        for t in reads:
            t.r[ev.key] = ev
        return ev

    def barrier(self):
        evs = [e for e in self.last.values() if e is not None and not e.key.startswith("dma:")]
        evs += list(self.last_dma.values())
        self.bar = evs


class PoolAlloc:
    def __init__(self, ap, nbytes):
        self.ap = ap
        self.nbytes = nbytes
        self.off = 0

    def alloc(self, shape, dtype):
        esz = 4 if dtype in (F32, I32) else 2
        n = 1
        for s in shape[1:]:
            n *= s
        nb = (n * esz + 63) // 64 * 64
        assert self.off + nb <= self.nbytes, f"pool overflow {self.off + nb} > {self.nbytes}"
        a = self.ap[:, self.off // 4:(self.off + nb) // 4]
        self.off += nb
        if dtype != F32:
            a = a.bitcast(dtype)
        a = a[:, 0:n]
        if len(shape) == 3:
            a = a.rearrange("p (a b) -> p a b", a=shape[1])
        elif len(shape) == 4:
            a = a.rearrange("p (a b c) -> p a b c", a=shape[1], b=shape[2])
        if shape[0] < 128:
            a = a[0:shape[0]]
        return a

    def mark(self):
        return self.off

    def release(self, m):
        self.off = m


def build_program(stop_after="all", debug=False):
    nc = bass.Bass("TRN2", target_bir_lowering=False)
    sc = Sched()
    dumps = []

    def din(name, shape, dt=F32):
        return nc.dram_tensor(name, list(shape), dt, kind="ExternalInput").ap()

    xT = din("xT", [D, T0])
    xTall = din("xTall", [D, S])
    posq = din("posq", [1, T0], I32)
    posall = din("posall", [1, S], I32)
    segmeta = din("segmeta", [1, 4])
    cT = din("cT", [128, KC])
    w_ada = din("w_ada", [2, D, 6 * D])
    b_adaT = din("b_adaT", [2, 128, 48])
    gmixT = din("gmixT", [2, 128, KC])
    gmlpT = din("gmlpT", [2, 128, KC])
    gfinT = din("gfinT", [128, KC])
    w_dq = din("w_dq", [D, 384])
    g_qT = din("g_qT", [128, 3])
    w_uq = din("w_uq", [384, 1536])
    w_uq_sw = din("w_uq_sw", [384, 512])
    w_dkv = din("w_dkv", [D, 320])
    w_dkr_sw = din("w_dkr_sw", [D, 64])
    g_kvT = din("g_kvT", [128, 2])
    w_ukv = din("w_ukv", [256, 2048])
    w_o = din("w_o", [D, D])
    invf = din("invf", [64, 1])
    w_qkv = din("w_qkv", [D, 1536])
    w_k_sw = din("w_k_sw", [D, 256])
    b_qkvT = din("b_qkvT", [128, 12])
    b_k_swT = din("b_k_swT", [128, 2])
    b_v = din("b_v", [1, 256])
    sinks = din("sinks", [1, 16])
    w_o1 = din("w_o1", [D, D])
    b_oT = din("b_oT", [128, KC])
    w_ff1 = din("w_ff1", [2, D, 4 * D])
    w_ff2 = din("w_ff2", [2, 4 * D, D])
    outT = nc.dram_tensor("outT", [D, NSEG * BLK], F32, kind="ExternalOutput").ap()

    es = ExitStack()
    with es:
        E = es.enter_context
        X = E(nc.sbuf_tensor("X", [128, KC, T0], F32))
        POOLT = E(nc.sbuf_tensor("POOLT", [128, POOL_KIB * 256], F32))
        CONST = E(nc.sbuf_tensor("CONST", [128, 16], F32))
        MODS = E(nc.sbuf_tensor("MODS", [128, 2, 48], F32))
        AB = E(nc.sbuf_tensor("AB", [128, 2, 2, KC], F32))
        SMALL = E(nc.sbuf_tensor("SMALL", [128, 64], F32))
        ONESB = E(nc.sbuf_tensor("ONESB", [128, 128], BF16))
        IDENT = E(nc.sbuf_tensor("IDENT", [128, 128], BF16))
        QLOC = E(nc.sbuf_tensor("QLOC", [128, SEG], F16))
        KK = E(nc.sbuf_tensor("KK", [128, NSEG, 64], F32))
        META = E(nc.sbuf_tensor("META", [128, 4], F32))
        BM = E(nc.sbuf_tensor("BM", [128, NSEG, NCHS, 64], F32))
        CONDB = E(nc.sbuf_tensor("CONDB", [128, KC], BF16))
        PS = [E(nc.psum_tensor(f"PS{i}", [128, 512], F32)) for i in range(8)]
        pool = PoolAlloc(POOLT, POOL_KIB * 1024)

        tX = [[Tile(f"X{s}_{c}") for c in range(NCHS)] for s in range(NSEG)]
        tPS = [Tile(f"PS{i}") for i in range(8)]
        tCONST = Tile("const")

        C_GQ = 0
        C_GKV = 3
        C_INVF = 5
        C_SGN = 6
        C_HB = 7
        C_GFIN = 16
        C_BO = 24
        C_GB = 32
        C_BQ = 40
        C_BKSW = 52

        def dump(name, ap, tiles, dt=F32):
            if not debug:
                return
            shape = list(ap.shape)
            dr = nc.dram_tensor("dbg_" + name, shape, dt, kind="ExternalOutput").ap()
            dumps.append("dbg_" + name)
            sc.op("sp", lambda e, dr=dr, ap=ap: e.dma_start(out=dr, in_=ap), reads=tiles, dma_key="dump")


        def MM(out, lhsT, rhs, start, stop, reads, writes):
            return sc.op("pe", lambda e: e.matmul(out, lhsT, rhs, start=start, stop=stop), reads=reads, writes=writes)

        def ACT(out, in_, func, reads, writes, bias=None, scale=None):
            kw = {}
            if bias is not None:
                kw["bias"] = bias
            if scale is not None:
                kw["scale"] = scale
            return sc.op("act", lambda e: e.activation(out=out, in_=in_, func=func, **kw), reads=reads, writes=writes)

        def AMUL(out, in_, c, reads, writes):
            return sc.op("act", lambda e: e.mul(out, in_, c), reads=reads, writes=writes)

        def ACOPY(out, in_, reads, writes):
            return sc.op("act", lambda e: e.copy(out, in_), reads=reads, writes=writes)

        def TT(out, in0, in1, op, reads, writes, strict=False):
            return sc.op("dve", lambda e: e.tensor_tensor(out, in0, in1, op), reads=reads, writes=writes, strict=strict)

        def TS(out, in0, s1, s2, op0, op1, reads, writes, strict=False):
            if op1 is None:
                return sc.op("dve", lambda e: e.tensor_scalar(out, in0, s1, s2, op0), reads=reads, writes=writes, strict=strict)
            return sc.op("dve", lambda e: e.tensor_scalar(out, in0, s1, s2, op0, op1), reads=reads, writes=writes, strict=strict)

        def STT(out, in0, scalar, in1, op0, op1, reads, writes, strict=False):
            return sc.op("dve", lambda e: e.scalar_tensor_tensor(out, in0, scalar, in1, op0, op1), reads=reads, writes=writes, strict=strict)

        def RECIP(out, in_, reads, writes, strict=False):
            return sc.op("dve", lambda e: e.reciprocal(out, in_), reads=reads, writes=writes, strict=strict)

        def VCOPY(out, in_, reads, writes):
            return sc.op("dve", lambda e: e.tensor_copy(out, in_), reads=reads, writes=writes)

        def DMA(q, out, in_, key, reads=(), writes=()):
            return sc.op(q, lambda e: e.dma_start(out=out, in_=in_), reads=reads, writes=writes, dma_key=key)

        tl = Tile("smallloads")

        def setup():
            P = "pool"
            sc.op(P, lambda e: e.memset(ONESB[:], 1.0), writes=[tCONST])
            sc.op(P, lambda e: e.memset(CONST[:, 0:1], EPS), writes=[tCONST])
            sc.op(P, lambda e: e.memset(CONST[:, 1:2], 0.0), writes=[tCONST])
            sc.op(P, lambda e: e.memset(CONST[:, 2:3], 1e-18), writes=[tCONST])
            sc.op(P, lambda e: e.memset(SMALL[0:32, C_SGN:C_SGN + 1], -1.0), writes=[tCONST])
            sc.op(P, lambda e: e.memset(SMALL[32:64, C_SGN:C_SGN + 1], 1.0), writes=[tCONST])
            tmp = pool.alloc([128, 128], F32)
            sc.op(P, lambda e: e.iota(tmp, pattern=[[1, 128]], base=0, channel_multiplier=-1,
                                      allow_small_or_imprecise_dtypes=True), writes=[tCONST])
            sc.op(P, lambda e: e.tensor_single_scalar(IDENT[:], tmp, 0.0, ALU.is_equal), writes=[tCONST])
            sc.op(P, lambda e: e.iota(QLOC[:], pattern=[[1, SEG]], base=0, channel_multiplier=0,
                                      allow_small_or_imprecise_dtypes=True), writes=[tCONST])
            kid = pool.alloc([128, 64], F32)
            sc.op(P, lambda e: e.iota(kid, pattern=[[128, 64]], base=0, channel_multiplier=1,
                                      allow_small_or_imprecise_dtypes=True), writes=[tCONST])
            loads = [
                (META[:], segmeta.partition_broadcast(128)),
                (SMALL[:, C_GQ:C_GQ + 3], g_qT),
                (SMALL[:, C_GKV:C_GKV + 2], g_kvT),
                (SMALL[0:64, C_INVF:C_INVF + 1], invf),
                (SMALL[:, C_GFIN:C_GFIN + 8], gfinT),
                (SMALL[:, C_BO:C_BO + 8], b_oT),
                (SMALL[:, C_BQ:C_BQ + 12], b_qkvT),
                (SMALL[:, C_BKSW:C_BKSW + 2], b_k_swT),
            ]
            for o, i in loads:
                DMA("sp", o, i, "c0", writes=[tl])
            kid0 = pool.alloc([128, 64], F32)
            kb = pool.alloc([128, 64], F32)
            sc.op(P, lambda e: e.iota(kid0, pattern=[[128, 64]], base=0, channel_multiplier=0,
                                      allow_small_or_imprecise_dtypes=True), writes=[tCONST])
            for s in range(NSEG):
                TS(KK[:, s, :], kid, META[:, s:s + 1], None, ALU.subtract, None, [tl, tCONST], [tCONST])
                TS(kb, kid0, META[:, s:s + 1], None, ALU.subtract, None, [tl, tCONST], [tCONST])
                for c in range(NCHS):
                    TS(BM[:, s, c, :], kb, 384.0 * c + 383.0, NEG_BIG, ALU.is_gt, ALU.mult, [], [tCONST])
            TS(SMALL[:, C_HB:C_HB + 2], META[:, 2:4], -1.0, -NEG_BIG, ALU.add, ALU.mult, [tl], [tCONST])

        setup()

        tmod = Tile("mods")

        def phase_A():
            m0 = pool.mark()
            cf = pool.alloc([128, KC], F32)
            tc_ = Tile("c")
            DMA("sp", cf, cT, "c1", writes=[tc_])
            ACT(CONDB[:], cf, AF.Silu, [tc_], [tCONST])
            badd = pool.alloc([128, 2, 48], F32)
            gm = pool.alloc([128, 2, 2, KC], F32)
            tb = Tile("badd")
            DMA("sp", badd, b_adaT.rearrange("l p n -> p l n"), "c1", writes=[tb])
            DMA("sp", gm[:, :, 0, :], gmixT.rearrange("l p n -> p l n"), "c1", writes=[tb])
            DMA("sp", gm[:, :, 1, :], gmlpT.rearrange("l p n -> p l n"), "c1", writes=[tb])
            wa = [pool.alloc([128, KC, 768], BF16) for _ in range(2)]
            twa = [Tile("wa0"), Tile("wa1")]
            n = 0
            for l in range(2):
                wsrc = w_ada[l].rearrange("(kc p) n -> p kc n", p=128)
                for cc in range(8):
                    b = n % 2
                    n += 1
                    DMA("pool", wa[b], wsrc[:, :, cc * 768:(cc + 1) * 768], f"wa{b}", writes=[twa[b]])
                    for nn in range(6):
                        col = l * 48 + cc * 6 + nn
                        for kc in range(KC):
                            MM(PS[0][:, col:col + 1], wa[b][:, kc, nn * 128:(nn + 1) * 128], CONDB[:, kc:kc + 1],
                               kc == 0, kc == KC - 1, [twa[b], tCONST], [tPS[0]])
                TT(MODS[:, l, :], PS[0][:, l * 48:(l + 1) * 48], badd[:, l, :], ALU.add, [tPS[0], tb], [tmod])
                STT(AB[:, l, 0, :], MODS[:, l, 8:16], 1.0, gm[:, l, 0, :], ALU.add, ALU.mult, [tb, tmod], [tmod], strict=True)
                STT(AB[:, l, 1, :], MODS[:, l, 32:40], 1.0, gm[:, l, 1, :], ALU.add, ALU.mult, [tb], [tmod])
            dump("mods", MODS[:], [tmod])
            sc.barrier()
            pool.release(m0)

        phase_A()

        def SH(l, which):
            return MODS[:, l, 0:8] if which == 0 else MODS[:, l, 24:32]

        def GT(l, which):
            return MODS[:, l, 16:24] if which == 0 else MODS[:, l, 40:48]

        def norm_rstd(src3, nk, n, sqbuf, ps_i, rstd, dim, reads, tsq, trstd):
            ACT(sqbuf, src3, AF.Square, reads, [tsq])
            for k in range(nk):
                MM(PS[ps_i][:, 0:n], ONESB[:], sqbuf[:, k, :], k == 0, k == nk - 1, [tsq, tCONST], [tPS[ps_i]])
            ACT(rstd, PS[ps_i][:, 0:n], AF.Ln, [tPS[ps_i]], [trstd], bias=CONST[:, 0:1], scale=1.0 / dim)
            ACT(rstd, rstd, AF.Exp, [], [trstd], scale=-0.5)

        def rope_tables(pos_ap, ang, nq, r, cos2, sin2, tpos, ttab, tt):
            inv = SMALL[0:64, C_INVF:C_INVF + 1]
            sgn = SMALL[0:64, C_SGN:C_SGN + 1]
            VCOPY(ang, pos_ap, [tpos], [tt])
            TS(ang, ang, inv, None, ALU.mult, None, [tl], [tt])
            for which in range(2):
                if which == 1:
                    TS(ang, ang, math.pi / 2, None, ALU.add, None, [], [tt])
                TS(r, ang, 1.0 / TWO_PI, None, ALU.mult, None, [], [tt])
                VCOPY(nq, r, [], [tt])
                STT(r, nq, -CW_HI, ang, ALU.mult, ALU.add, [], [tt])
                STT(r, nq, -CW_LO, r, ALU.mult, ALU.add, [], [tt])
                TS(r, r, -PI_SAFE, PI_SAFE, ALU.max, ALU.min, [], [tt])
                if which == 0:
                    ACT(sin2, r, AF.Sin, [tt, tCONST], [ttab], scale=sgn)
                else:
                    ACT(cos2, r, AF.Sin, [tt], [ttab])

        LAT = pool.alloc([128, 2, S], BF16)
        KR = pool.alloc([128, S], BF16)
        tKRz = Tile("krz")
        sc.op("pool", lambda e: e.memset(KR[64:128, :], 0.0), writes=[tKRz])
        tLAT = [Tile(f"lat{i}") for i in range(NKC)]
        tKR = [Tile(f"kr{i}") for i in range(NKC)]
        m_attn = pool.mark()

        def phase_B():
            wdkv = pool.alloc([128, KC, 384], BF16)
            twd = Tile("wdkv")
            twd2 = Tile("wdkv2")
            DMA("pool", wdkv[:, :, 0:320], w_dkv.rearrange("(kc p) n -> p kc n", p=128), "wdkv", writes=[twd])
            DMA("pool", wdkv[:, :, 320:384], w_dkr_sw.rearrange("(kc p) n -> p kc n", p=128), "wdkv2", writes=[twd2])
            xa = [pool.alloc([128, KC, KCHUNK], F32) for _ in range(2)]
            hb = [pool.alloc([128, KC, KCHUNK], BF16) for _ in range(2)]
            posb = [pool.alloc([64, KCHUNK], I32) for _ in range(2)]
            rstd = pool.alloc([128, KCHUNK], F32)
            rstd2 = pool.alloc([128, KCHUNK], F32)
            sqc = pool.alloc([128, 2, KCHUNK], BF16)
            ang = pool.alloc([64, KCHUNK], F32)
            nq = pool.alloc([64, KCHUNK], I32)
            rr = pool.alloc([64, KCHUNK], F32)
            cos2 = pool.alloc([64, KCHUNK], F32)
            sin2 = pool.alloc([64, KCHUNK], F32)
            ku = pool.alloc([64, KCHUNK], F32)
            kv = pool.alloc([64, KCHUNK], F32)
            txa = [Tile("xa0"), Tile("xa1")]
            th = [Tile("h0"), Tile("h1")]
            tpos = [Tile("pos0"), Tile("pos1")]
            trs = Tile("rstd")
            trs2 = Tile("rstd2")
            tsqc = Tile("sqc")
            ttab = Tile("tab")
            ttmp = Tile("ropetmp")
            tku = Tile("ku")
            xsrc = xTall.rearrange("(kc p) t -> p kc t", p=128)

            txak = [[Tile(f"xa{b}_{k}") for k in range(KC)] for b in range(2)]

            def A1(i):
                b = i % 2
                cols = slice(i * KCHUNK, (i + 1) * KCHUNK)
                DMA("sp", xa[b], xsrc[:, :, cols], f"xa{b}", writes=[txa[b]] + txak[b])
                DMA("sp", posb[b], posall[:, cols].partition_broadcast(64), f"pos{b}", writes=[tpos[b]])
                norm_rstd(xa[b], KC, KCHUNK, hb[b], b, rstd, D, [txa[b]], th[b], trs)
                for kc in range(KC):
                    TT(xa[b][:, kc, :], xa[b][:, kc, :], rstd, ALU.mult, [trs, txa[b]], [txak[b][kc]])

            def A2(i):
                b = i % 2
                for kc in range(KC):
                    ACT(hb[b][:, kc, :], xa[b][:, kc, :], AF.Identity, [txak[b][kc], tmod], [th[b]],
                        bias=SH(0, 0)[:, kc:kc + 1], scale=AB[:, 0, 0, kc:kc + 1])

            def B1(i):
                b = i % 2
                cols = slice(i * KCHUNK, (i + 1) * KCHUNK)
                for (pi, c0, c1, m) in [(2, 0, 128, 128), (3, 128, 256, 128), (4, 256, 320, 64), (5, 320, 384, 64)]:
                    for kc in range(KC):
                        MM(PS[pi][0:m, 0:KCHUNK], wdkv[:, kc, c0:c1], hb[b][:, kc, :], kc == 0, kc == KC - 1,
                           [th[b], twd, twd2], [tPS[pi]])
                ACT(sqc[:, 0, :], PS[2][:, 0:KCHUNK], AF.Square, [tPS[2]], [tsqc])
                ACT(sqc[:, 1, :], PS[3][:, 0:KCHUNK], AF.Square, [tPS[3]], [tsqc])
                for k in range(2):
                    MM(PS[6][:, 0:KCHUNK], ONESB[:], sqc[:, k, :], k == 0, k == 1, [tsqc, tCONST], [tPS[6]])
                ACT(rstd2, PS[6][:, 0:KCHUNK], AF.Ln, [tPS[6]], [trs2], bias=CONST[:, 0:1], scale=1.0 / 256)
                ACT(rstd2, rstd2, AF.Exp, [], [trs2], scale=-0.5)
                for k in range(2):
                    STT(LAT[:, k, cols], PS[2 + k][:, 0:KCHUNK], SMALL[:, C_GKV + k:C_GKV + k + 1], rstd2, ALU.mult, ALU.mult,
                        [tPS[2 + k], trs2, tl], [tLAT[i]])

            def B2(i):
                cols = slice(i * KCHUNK, (i + 1) * KCHUNK)
                TT(ku, PS[4][0:64, 0:KCHUNK], cos2, ALU.mult, [tPS[4], ttab], [tku])
                TT(kv, PS[5][0:64, 0:KCHUNK], sin2, ALU.mult, [tPS[5], ttab], [tku])
                TT(KR[0:64, cols], ku, kv, ALU.add, [tku], [tKR[i]])

            A1(0)
            A2(0)
            rope_tables(posb[0], ang, nq, rr, cos2, sin2, tpos[0], ttab, ttmp)
            for i in range(NKC):
                if i + 1 < NKC:
                    A1(i + 1)
                B1(i)
                if i + 1 < NKC:
                    A2(i + 1)
                B2(i)
                if i + 1 < NKC:
                    rope_tables(posb[(i + 1) % 2], ang, nq, rr, cos2, sin2, tpos[(i + 1) % 2], ttab, ttmp)
            dump("lat", LAT, tLAT, BF16)
            dump("kr", KR[0:64, :], tKR, BF16)
            sc.barrier()

        xsrc_own = xT.rearrange("(kc p) t -> p kc t", p=128)
        for s in range(NSEG):
            for c in range(NCHS):
                cols = slice(s * SEG + c * CH, s * SEG + (c + 1) * CH)
                DMA("sp", X[:, :, cols], xsrc_own[:, :, cols], f"x{s}{c}", writes=[tX[s][c]])
        phase_B()
        pool.release(m_attn)
        if stop_after == "B":
            return finish(nc, sc, es, dumps, X, outT, tX, None)

        def attn_segment(s):
            m0 = pool.mark()
            NK = 4096 if s == 0 else 8192
            cqn = pool.alloc([128, 3, SEG], BF16)
            cos2q = pool.alloc([64, SEG], F32)
            sin2q = pool.alloc([64, SEG], F32)
            qn = pool.alloc([128, SEG], BF16)
            qr = pool.alloc([128, SEG], BF16)
            tqrz = Tile("qrz")
            sc.op("pool", lambda e: e.memset(qr[64:128, :], 0.0), writes=[tqrz])
            wh = [pool.alloc([128, 2304], BF16) for _ in range(2)]
            pbuf = [pool.alloc([128, CH], BF16) for _ in range(4)]
            obuf = [pool.alloc([128, CH], BF16) for _ in range(2)]
            rden = pool.alloc([128, CH], F32)
            tcqn = [Tile(f"cqn{c}") for c in range(NCHS)]
            ttabq = [Tile(f"tabq{c}") for c in range(NCHS)]
            tqn = [Tile(f"qn{c}") for c in range(NCHS)]
            tqr = [Tile(f"qr{c}") for c in range(NCHS)]
            twh = [[Tile(f"wh{b}_{k}") for k in range(4)] for b in range(2)]
            tp = [Tile(f"p{i}") for i in range(4)]
            tob = [Tile("ob0"), Tile("ob1")]
            trden = Tile("rden")
            m1 = pool.mark()
            wdq = pool.alloc([128, KC, 384], BF16)
            twdq = Tile("wdq")
            DMA("pool", wdq, w_dq.rearrange("(kc p) n -> p kc n", p=128), "wdq", writes=[twdq])
            hq = pool.alloc([128, KC, CH], BF16)
            tt = pool.alloc([128, KC, CH], F32)
            rstd = pool.alloc([128, CH], F32)
            rstdq = pool.alloc([128, CH], F32)
            sqq = pool.alloc([128, 3, CH], BF16)
            posb = pool.alloc([64, CH], I32)
            ang = pool.alloc([64, CH], F32)
            nq_ = pool.alloc([64, CH], I32)
            rr = pool.alloc([64, CH], F32)
            th = Tile("hq")
            tttk = [Tile(f"ttq{k}") for k in range(KC)]
            trs = Tile("rs")
            trsq = Tile("rsq")
            tsqq = Tile("sqq")
            tpos = Tile("posq")
            ttmp = Tile("ropetmpq")
            for c in range(NCHS):
                lc = slice(c * CH, (c + 1) * CH)
                gc = slice(s * SEG + c * CH, s * SEG + (c + 1) * CH)
                DMA("sp", posb, posq[:, gc].partition_broadcast(64), "posq", writes=[tpos])
                norm_rstd(X[:, :, gc], KC, CH, hq, 0, rstd, D, [tX[s][c]], th, trs)
                for kc in range(KC):
                    TT(tt[:, kc, :], X[:, kc, gc], rstd, ALU.mult, [trs, tX[s][c]], [tttk[kc]])
                    ACT(hq[:, kc, :], tt[:, kc, :], AF.Identity, [tttk[kc], tmod], [th],
                        bias=SH(0, 0)[:, kc:kc + 1], scale=AB[:, 0, 0, kc:kc + 1])
                for m in range(3):
                    for kc in range(KC):
                        MM(PS[1 + m][:, 0:CH], wdq[:, kc, m * 128:(m + 1) * 128], hq[:, kc, :], kc == 0, kc == KC - 1,
                           [th, twdq], [tPS[1 + m]])
                for m in range(3):
                    ACT(sqq[:, m, :], PS[1 + m][:, 0:CH], AF.Square, [tPS[1 + m]], [tsqq])
                for m in range(3):
                    MM(PS[4][:, 0:CH], ONESB[:], sqq[:, m, :], m == 0, m == 2, [tsqq, tCONST], [tPS[4]])
                ACT(rstdq, PS[4][:, 0:CH], AF.Ln, [tPS[4]], [trsq], bias=CONST[:, 0:1], scale=1.0 / 384)
                ACT(rstdq, rstdq, AF.Exp, [], [trsq], scale=-0.5)
                for m in range(3):
                    STT(cqn[:, m, lc], PS[1 + m][:, 0:CH], SMALL[:, C_GQ + m:C_GQ + m + 1], rstdq, ALU.mult, ALU.mult,
                        [tPS[1 + m], trsq, tl], [tcqn[c]])
                rope_tables(posb, ang, nq_, rr, cos2q[:, lc], sin2q[:, lc], tpos, ttabq[c], ttmp)
            if s == 0:
                dump("cqn0", cqn, tcqn, BF16)
            sc.barrier()
            pool.release(m1)
            kh = pool.alloc([128, NK], BF16)
            vh = pool.alloc([128, NK // 128, 128], BF16)
            u1 = pool.alloc([64, CH], F32)
            u2 = pool.alloc([64, CH], F32)
            NKCH = NK // 512
            tkh = [Tile(f"kh{i}") for i in range(NKCH)]
            tvh = [Tile(f"vh{i}") for i in range(NKCH)]
            tu = Tile("u")
            uq_src = w_uq.rearrange("(m p) n -> p m n", p=128)
            uqs_src = w_uq_sw.rearrange("(m p) n -> p m n", p=128)
            ukv_src = w_ukv.rearrange("(m p) n -> p m n", p=128)

            def wviews(b):
                w = wh[b]
                return (w[:, 0:576].rearrange("p (m n) -> p m n", m=3), w[:, 576:768].rearrange("p (m n) -> p m n", m=3),
                        w[:, 768:1280].rearrange("p (m n) -> p m n", m=2), w[:, 1280:2304])

            def load_wh(h):
                b = h % 2
                wuq, wuqs, wukv, wo_h = wviews(b)
                DMA("pool", wuq, uq_src[:, :, h * 192:(h + 1) * 192], f"wh{b}0", writes=[twh[b][0]])
                DMA("pool", wuqs, uqs_src[:, :, h * 64:(h + 1) * 64], f"wh{b}1", writes=[twh[b][1]])
                DMA("pool", wukv, ukv_src[:, :, h * 256:(h + 1) * 256], f"wh{b}2", writes=[twh[b][2]])
                DMA("pool", wo_h, w_o[h * 128:(h + 1) * 128, :], f"wh{b}3", writes=[twh[b][3]])

            load_wh(0)
            cnt = {"p": 0, "o": 0, "s": 0}

            def head(h):
                b = h % 2
                wuq, wuqs, wukv, wo_h = wviews(b)
                for c in range(NCHS):
                    lc = slice(c * CH, (c + 1) * CH)
                    for m in range(3):
                        MM(PS[6][:, 0:CH], wuq[:, m, 0:128], cqn[:, m, lc], m == 0, m == 2, [twh[b][0], tcqn[c]], [tPS[6]])
                    AMUL(qn[:, lc], PS[6][:, 0:CH], MLA_SCALE, [tPS[6]], [tqn[c]])
                    for m in range(3):
                        MM(PS[7][0:64, 0:CH], wuq[:, m, 128:192], cqn[:, m, lc], m == 0, m == 2, [twh[b][0], tcqn[c]], [tPS[7]])
                    STT(u1, PS[7][0:64, 0:CH], MLA_SCALE, cos2q[:, lc], ALU.mult, ALU.mult, [tPS[7], ttabq[c]], [tu])
                    for m in range(3):
                        MM(PS[7][0:64, 0:CH], wuqs[:, m, :], cqn[:, m, lc], m == 0, m == 2, [twh[b][1], tcqn[c]], [tPS[7]])
                    STT(u2, PS[7][0:64, 0:CH], MLA_SCALE, sin2q[:, lc], ALU.mult, ALU.mult, [tPS[7], ttabq[c]], [tu])
                    TT(qr[0:64, lc], u1, u2, ALU.add, [tu], [tqr[c]])
                for i in range(NKCH):
                    cols = slice(i * 512, (i + 1) * 512)
                    bk = 6 if i % 2 == 0 else 0
                    bv_ = 7 if i % 2 == 0 else 1
                    for m in range(2):
                        MM(PS[bk][:, 0:512], wukv[:, m, 0:128], LAT[:, m, cols], m == 0, m == 1, [twh[b][2], tLAT[i]], [tPS[bk]])
                    if i % 2 == 0:
                        VCOPY(kh[:, cols], PS[bk][:, 0:512], [tPS[bk]], [tkh[i]])
                    else:
                        ACOPY(kh[:, cols], PS[bk][:, 0:512], [tPS[bk]], [tkh[i]])
                    for t in range(4):
                        kt = i * 4 + t
                        for m in range(2):
                            MM(PS[bv_][:, t * 128:(t + 1) * 128], LAT[:, m, kt * 128:(kt + 1) * 128], wukv[:, m, 128:256],
                               m == 0, m == 1, [twh[b][2], tLAT[i]], [tPS[bv_]])
                    if i % 2 == 0:
                        ACOPY(vh[:, i * 4:(i + 1) * 4, :], PS[bv_][:, 0:512].rearrange("p (a b) -> p a b", a=4), [tPS[bv_]], [tvh[i]])
                    else:
                        VCOPY(vh[:, i * 4:(i + 1) * 4, :], PS[bv_][:, 0:512].rearrange("p (a b) -> p a b", a=4), [tPS[bv_]], [tvh[i]])
                info = []
                tiles = []
                for c in range(NCHS):
                    if s == 0:
                        nkt = 26 + 3 * c
                        full_upto = 3 * c - 2
                    else:
                        nkt = 58 + 3 * c
                        full_upto = 30 + 3 * c
                    nkt = min(nkt, NK // 128)
                    ob = cnt["o"] % 2
                    cnt["o"] += 1
                    info.append((nkt, full_upto, ob))
                    tiles += [(c, kt) for kt in range(nkt)]

                def front(c, kt):
                    nkt, full_upto, ob = info[c]
                    lc = slice(c * CH, (c + 1) * CH)
                    sb = cnt["s"] % 2
                    cnt["s"] += 1
                    pS = PS[sb]
                    kcols = slice(kt * 128, (kt + 1) * 128)
                    MM(pS[:, 0:CH], kh[:, kcols], qn[:, lc], True, False, [tkh[kt // 4], tqn[c]], [tPS[sb]])
                    MM(pS[:, 0:CH], KR[:, kcols], qr[:, lc], False, True, [tKR[kt // 4], tqr[c], tKRz, tqrz], [tPS[sb]])
                    pb = cnt["p"] % len(pbuf)
                    cnt["p"] += 1
                    P = pbuf[pb]
                    p0s = [-1, 7, 15, 23] if s == 0 else [31, 39, 47, 55]
                    may_full = kt >= min(p0s) + 3 * c + 3
                    may_diag = any(0 <= kt - (p + 3 * c) <= 2 for p in p0s)
                    if may_full:
                        ACT(P, pS[:, 0:CH], AF.Exp, [tPS[sb], tCONST], [tp[pb]], bias=BM[:, s, c, kt:kt + 1], scale=1.0)
                    else:
                        ACT(P, pS[:, 0:CH], AF.Exp, [tPS[sb]], [tp[pb]])
                    if may_diag:
                        STT(P, QLOC[:, lc], KK[:, s, kt:kt + 1], P, ALU.is_ge, ALU.mult, [tCONST], [tp[pb]])
                    return pb

                def back(c, kt, pb):
                    nkt, full_upto, ob = info[c]
                    lc = slice(c * CH, (c + 1) * CH)
                    gc = slice(s * SEG + c * CH, s * SEG + (c + 1) * CH)
                    po = PS[2 + ob]
                    pl = PS[4 + ob]
                    P = pbuf[pb]
                    MM(po[:, 0:CH], vh[:, kt, :], P, kt == 0, kt == nkt - 1, [tp[pb], tvh[kt // 4]], [tPS[2 + ob]])
                    MM(pl[:, 0:CH], ONESB[:], P, kt == 0, kt == nkt - 1, [tp[pb], tCONST], [tPS[4 + ob]])
                    if kt != nkt - 1:
                        return
                    O = obuf[ob]
                    ACT(rden, pl[:, 0:CH], AF.Ln, [tPS[4 + ob], tCONST], [trden], bias=CONST[:, 2:3], scale=1.0)
                    ACT(rden, rden, AF.Exp, [], [trden], scale=-1.0)
                    TT(O, po[:, 0:CH], rden, ALU.mult, [tPS[2 + ob], trden], [tob[ob]])

                    def proj():
                        for dm in range(KC):
                            MM(PS[6 + dm % 2][:, 0:CH], wo_h[:, dm * 128:(dm + 1) * 128], O, True, True, [tob[ob], twh[b][3]], [tPS[6 + dm % 2]])
                            STT(X[:, dm, gc], PS[6 + dm % 2][:, 0:CH], GT(0, 0)[:, dm:dm + 1], X[:, dm, gc], ALU.mult, ALU.add,
                                [tPS[6 + dm % 2], tmod], [tX[s][c]])
                    deferred.append([8, proj])

                DEPTH = 2
                pend = []
                deferred = []

                def tick():
                    for d in deferred:
                        d[0] -= 1
                    while deferred and deferred[0][0] <= 0:
                        deferred.pop(0)[1]()

                for (c, kt) in tiles:
                    pend.append((c, kt, front(c, kt)))
                    if len(pend) > DEPTH:
                        back(*pend.pop(0))
                        tick()
                while pend:
                    back(*pend.pop(0))
                    tick()
                while deferred:
                    deferred.pop(0)[1]()

            for h in range(8):
                if h + 1 < 8:
                    load_wh(h + 1)
                head(h)
            sc.barrier()
            pool.release(m0)

        for s in range(NSEG):
            attn_segment(s)
        allX = [t for ts in tX for t in ts]
        dump("xa0", X[:], allX)
        if stop_after == "L0attn":
            return finish(nc, sc, es, dumps, X, outT, tX, None)
        pool.release(0)

        def chunk_list(own):
            out = []
            for s in range(NSEG):
                segs = [(128, 384), (512, 384), (896, 256)] if own else [(0, 384), (384, 384), (768, 384)]
                for (off, w) in segs:
                    g0 = s * SEG + off
                    olds = sorted(set([off // CH, (off + w - 1) // CH]))
                    out.append(dict(s=s, off=off, gc=slice(g0, g0 + w), w=w, xt=[tX[s][o] for o in olds],
                                    olds=[s * NCHS + o for o in olds]))
            return out

        def norm_mod_to(l, which, H, tH, chunks):
            tt = pool.alloc([128, KC, CH], F32)
            rstd = pool.alloc([128, CH], F32)
            tttk = [Tile(f"tt{k}") for k in range(KC)]
            trs = Tile("rs")
            gmul = AB[:, l, which, :]
            for i, ch in enumerate(chunks):
                gc, w = ch["gc"], ch["w"]
                norm_rstd(X[:, :, gc], KC, w, H[:, :, gc], i % 2, rstd[:, 0:w], D, ch["xt"], tH[i], trs)
                for kc in range(KC):
                    TT(tt[:, kc, 0:w], X[:, kc, gc], rstd[:, 0:w], ALU.mult, [trs] + ch["xt"], [tttk[kc]])
                    ACT(H[:, kc, gc], tt[:, kc, 0:w], AF.Identity, [tttk[kc], tmod], [tH[i]],
                        bias=SH(l, which)[:, kc:kc + 1], scale=gmul[:, kc:kc + 1])

        def mlp(l, chunks):
            m0 = pool.mark()
            H = pool.alloc([128, KC, T0], BF16)
            tH = [Tile(f"H{i}") for i in range(len(chunks))]
            W1 = [pool.alloc([128, KC, 1024], BF16) for _ in range(2)]
            W2 = [pool.alloc([128, 8, 1024], BF16) for _ in range(2)]
            tW1 = [Tile("W1a"), Tile("W1b")]
            tW2 = [Tile("W2a"), Tile("W2b")]
            A = [pool.alloc([128, 8, CH], BF16) for _ in range(2)]
            R = [pool.alloc([128, CH], BF16) for _ in range(2)]
            tA = [Tile("A0"), Tile("A1")]
            tR = [Tile("R0"), Tile("R1")]
            w1src = w_ff1[l].rearrange("(kc p) n -> p kc n", p=128)
            w2src = w_ff2[l].rearrange("(f p) n -> p f n", p=128)

            def loadW(fq):
                b = fq % 2
                DMA("pool", W1[b], w1src[:, :, fq * 1024:(fq + 1) * 1024], f"W1{b}", writes=[tW1[b]])
                DMA("pool", W2[b], w2src[:, fq * 8:(fq + 1) * 8, :], f"W2{b}", writes=[tW2[b]])

            loadW(0)
            m1 = pool.mark()
            norm_mod_to(l, 1, H, tH, chunks)
            pool.release(m1)
            cnt = {"u": 0, "a": 0, "y": 0}
            for fq in range(4):
                b = fq % 2
                if fq + 1 < 4:
                    loadW(fq + 1)
                for i, ch in enumerate(chunks):
                    gc, w = ch["gc"], ch["w"]
                    ab = cnt["a"] % 2
                    cnt["a"] += 1
                    for fc in range(8):
                        ub = cnt["u"] % 3
                        cnt["u"] += 1
                        for kc in range(KC):
                            MM(PS[ub][:, 0:w], W1[b][:, kc, fc * 128:(fc + 1) * 128], H[:, kc, gc], kc == 0, kc == KC - 1,
                               [tW1[b], tH[i]], [tPS[ub]])
                        rb = fc % 2
                        ACT(R[rb][:, 0:w], PS[ub][:, 0:w], AF.Relu, [tPS[ub]], [tR[rb]])
                        TT(A[ab][:, fc, 0:w], R[rb][:, 0:w], R[rb][:, 0:w], ALU.mult, [tR[rb]], [tA[ab]])
                    for dm in range(KC):
                        yb = 3 + cnt["y"] % 2
                        cnt["y"] += 1
                        for fc in range(8):
                            MM(PS[yb][:, 0:w], W2[b][:, fc, dm * 128:(dm + 1) * 128], A[ab][:, fc, 0:w], fc == 0, fc == 7,
                               [tW2[b], tA[ab]], [tPS[yb]])
                        STT(X[:, dm, gc], PS[yb][:, 0:w], GT(l, 1)[:, dm:dm + 1], X[:, dm, gc], ALU.mult, ALU.add,
                            [tPS[yb], tmod], ch["xt"])
            sc.barrier()
            pool.release(m0)

        mlp(0, chunk_list(False))
        dump("xm0", X[:], allX)
        if stop_after == "L0":
            return finish(nc, sc, es, dumps, X, outT, tX, None)

        def swa_layer():
            l = 1
            m0 = pool.mark()
            H = pool.alloc([128, KC, T0], BF16)
            tH = [Tile(f"H1_{i}") for i in range(NCH)]
            KT = pool.alloc([128, 2, T0], BF16)
            KTs = pool.alloc([128, 2, T0], BF16)
            NT = T0 // 128
            VP = pool.alloc([128, NT, 4, 65], BF16)
            tKT = [Tile(f"KT{i}") for i in range(NCH)]
            tVP = [Tile(f"VP{i}") for i in range(NT)]
            tvone = Tile("vone")
            sc.op("pool", lambda e: e.memset(VP[:, :, :, 64:65], 1.0), writes=[tvone])
            bvb = pool.alloc([128, 256], F32)
            sk = pool.alloc([128, 16], F32)
            esink = pool.alloc([128, 16], F32)
            bqs = pool.alloc([128, 8], F32)
            m1 = pool.mark()
            norm_mod_to(l, 0, H, tH, chunk_list(False))
            sc.barrier()
            pool.release(m1)
            wkv = pool.alloc([128, KC, 768], BF16)
            twkv = [Tile("wkv0"), Tile("wkv1")]
            qsrc = w_qkv.rearrange("(kc p) n -> p kc n", p=128)
            DMA("pool", wkv[:, :, 0:512], qsrc[:, :, 1024:1536], "wkv0", writes=[twkv[0]])
            DMA("pool", wkv[:, :, 512:768], w_k_sw.rearrange("(kc p) n -> p kc n", p=128), "wkv1", writes=[twkv[1]])
            tbv = Tile("bvb")
            DMA("sp", bvb, b_v.partition_broadcast(128), "c2", writes=[tbv])
            tsk = Tile("sk")
            DMA("sp", sk, sinks.partition_broadcast(128), "c2", writes=[tsk])
            for i in range(NCH):
                gc = slice(i * CH, (i + 1) * CH)
                for m in range(2):
                    for kc in range(KC):
                        MM(PS[m][:, 0:CH], wkv[:, kc, m * 128:(m + 1) * 128], H[:, kc, gc], kc == 0, kc == KC - 1, [twkv[0], tH[i]], [tPS[m]])
                    ACT(KT[:, m, gc], PS[m][:, 0:CH], AF.Identity, [tPS[m], tl], [tKT[i]], bias=SMALL[:, C_BQ + 8 + m:C_BQ + 9 + m], scale=1.0)
                    for kc in range(KC):
                        MM(PS[2 + m][:, 0:CH], wkv[:, kc, 512 + m * 128:512 + (m + 1) * 128], H[:, kc, gc], kc == 0, kc == KC - 1,
                           [twkv[1], tH[i]], [tPS[2 + m]])
                    ACT(KTs[:, m, gc], PS[2 + m][:, 0:CH], AF.Identity, [tPS[2 + m], tl], [tKT[i]], bias=SMALL[:, C_BKSW + m:C_BKSW + m + 1], scale=1.0)
                for t3 in range(3):
                    t = i * 3 + t3
                    tc = slice(t * 128, (t + 1) * 128)
                    pb = 4 + t % 2
                    for kc in range(KC):
                        MM(PS[pb][:, 0:256], H[:, kc, tc], wkv[:, kc, 256:512], kc == 0, kc == KC - 1, [twkv[0], tH[i]], [tPS[pb]])
                    TT(VP[:, t, :, 0:64], PS[pb][:, 0:256].rearrange("p (g d) -> p g d", g=4), bvb[:].rearrange("p (g d) -> p g d", g=4),
                       ALU.add, [tPS[pb], tbv, tvone], [tVP[t]])
            sc.barrier()
            pool.release(m1)
            if stop_after == "L1kv":
                return True
            wq = pool.alloc([128, KC, 1024], BF16)
            twq = Tile("wq")
            DMA("pool", wq, qsrc[:, :, 0:1024], "wq", writes=[twq])
            BIAS = pool.alloc([128, 4, 2, 512], F32)
            tBIAS = Tile("bias")
            d0 = pool.alloc([128, 128], F32)
            mc = pool.alloc([128, 128], F32)
            mp = pool.alloc([128, 128], F32)
            sc.op("pool", lambda e: e.iota(d0, pattern=[[1, 128]], base=0, channel_multiplier=-1, allow_small_or_imprecise_dtypes=True),
                  writes=[tBIAS])
            TS(mc, d0, 0.0, NEG_BIG, ALU.is_lt, ALU.mult, [tBIAS], [tBIAS])
            TS(mp, d0, 0.0, NEG_BIG, ALU.is_ge, ALU.mult, [], [tBIAS])
            ORDR = [0, 2, 1, 3]
            for hd in range(16):
                g, hh = hd // 4, hd % 4
                j = ORDR.index(hh)
                slope = 2.0 ** (-(hd + 1) / 2.0)
                STT(BIAS[:, g, 1, j * 128:(j + 1) * 128], d0, -slope, mc, ALU.mult, ALU.add, [], [tBIAS])
                STT(BIAS[:, g, 0, j * 128:(j + 1) * 128], d0, -slope, mp, ALU.mult, ALU.add, [], [tBIAS])
                TS(BIAS[:, g, 0, j * 128:(j + 1) * 128], BIAS[:, g, 0, j * 128:(j + 1) * 128], -128.0 * slope, None, ALU.add, None, [], [tBIAS])
            ACT(esink, sk, AF.Exp, [tsk], [tBIAS])
            TS(bqs, SMALL[:, C_BQ:C_BQ + 8], SWA_SCALE, None, ALU.mult, None, [tl], [tBIAS])
            QT = [pool.alloc([128, 8, CH], BF16) for _ in range(2)]
            tQT = [Tile("QT0"), Tile("QT1")]
            SB = [pool.alloc([128, 512], F32) for _ in range(2)]
            tSB = [Tile("SB0"), Tile("SB1")]
            PB = [pool.alloc([128, 512], BF16) for _ in range(6)]
            tPB = [Tile(f"PB{i}") for i in range(6)]
            OTs = [pool.alloc([128, 1024], BF16) for _ in range(2)]
            tOTs = [Tile("OT0"), Tile("OT1")]
            den = pool.alloc([128, 4], F32)
            tden = Tile("den")
            PST = PS[7][:].bitcast(BF16)
            cnt = {"sb": 0, "pb": 0, "sc": 0}
            if stop_after == "L1bias":
                return True
            for i in range(NCH):
                s, c = i // NCHS, i % NCHS
                gc = slice(i * CH, (i + 1) * CH)
                qb = i % 2
                for m in range(8):
                    for kc in range(KC):
                        MM(PS[6][:, 0:CH], wq[:, kc, m * 128:(m + 1) * 128], H[:, kc, gc], kc == 0, kc == KC - 1, [twq, tH[i]], [tPS[6]])
                    ACT(QT[qb][:, m, :], PS[6][:, 0:CH], AF.Identity, [tPS[6], tBIAS], [tQT[qb]], bias=bqs[:, m:m + 1], scale=SWA_SCALE)
                def l1_front(t3, g):
                    lt = c * 3 + t3
                    t = i * 3 + t3
                    qc = slice(t3 * 128, (t3 + 1) * 128)
                    pbs = []
                    for kk in range(2):
                        kt = t - 1 + kk
                        kcols = slice(kt * 128, (kt + 1) * 128)
                        pair = 2 * (cnt["sc"] % 2)
                        cnt["sc"] += 1
                        for hh in range(4):
                            m = 2 * g + hh // 2
                            half = hh % 2
                            pr = slice(half * 64, (half + 1) * 64)
                            Ksrc = KT if half == g % 2 else KTs
                            bank = pair + half
                            col = (hh // 2) * 128
                            MM(PS[bank][:, col:col + 128], Ksrc[pr, g // 2, kcols], QT[qb][pr, m, qc], True, True,
                               [tKT[kt // 3], tQT[qb]], [tPS[bank]])
                        sb = cnt["sb"] % 2
                        cnt["sb"] += 1
                        TT(SB[sb][:, 0:256], PS[pair][:, 0:256], BIAS[:, g, kk, 0:256], ALU.add, [tPS[pair], tBIAS], [tSB[sb]])
                        TT(SB[sb][:, 256:512], PS[pair + 1][:, 0:256], BIAS[:, g, kk, 256:512], ALU.add, [tPS[pair + 1], tBIAS], [tSB[sb]])
                        pb = cnt["pb"] % 6
                        cnt["pb"] += 1
                        pbs.append(pb)
                        if lt == 1 and kk == 0:
                            ACT(PB[pb], SB[sb], AF.Exp, [tSB[sb], tCONST], [tPB[pb]], bias=SMALL[:, C_HB + s:C_HB + s + 1], scale=1.0)
                        else:
                            ACT(PB[pb], SB[sb], AF.Exp, [tSB[sb]], [tPB[pb]])
                    return pbs

                def l1_back(t3, g, pbs):
                    t = i * 3 + t3
                    pso = 4 + g % 2
                    oi = t % 2
                    for hh in range(4):
                        for kk in range(2):
                            kt = t - 1 + kk
                            pb = pbs[kk]
                            j = ORDR.index(hh)
                            MM(PS[pso][:, hh * 65:(hh + 1) * 65], PB[pb][:, j * 128:(j + 1) * 128], VP[:, kt, g, :], kk == 0, kk == 1,
                               [tPB[pb], tVP[kt], tvone], [tPS[pso]])
                    po3 = PS[pso][:, 0:260].rearrange("p (h d) -> p h d", h=4)
                    TT(den, po3[:, :, 64], esink[:, 4 * g:4 * g + 4], ALU.add, [tPS[pso], tBIAS], [tden], strict=True)
                    RECIP(den, den, [tden], [tden], strict=True)
                    for hh in range(4):
                        hd = 4 * g + hh
                        TS(OTs[oi][:, hd * 64:(hd + 1) * 64], PS[pso][:, hh * 65:hh * 65 + 64], den[:, hh:hh + 1], None, ALU.mult, None,
                           [tPS[pso], tden], [tOTs[oi]], strict=(hh == 0))
                    if g != 3:
                        return
                    for m in range(8):
                        sc.op("pe", lambda e, m=m, oi=oi: e.transpose(PST[:, m * 128:(m + 1) * 128], OTs[oi][:, m * 128:(m + 1) * 128], IDENT[:]),
                              reads=[tOTs[oi], tCONST], writes=[tPS[7]])
                    tcols = slice(t * 128, (t + 1) * 128)
                    ACOPY(H[:, :, tcols], PST.rearrange("p (m q) -> p m q", m=8), [tPS[7]], [tH[i]])

                items = [(t3, g) for t3 in range(3) if c * 3 + t3 != 0 for g in range(4)]
                pend = []
                for (t3, g) in items:
                    pend.append((t3, g, l1_front(t3, g)))
                    if len(pend) > 2:
                        l1_back(*pend.pop(0))
                while pend:
                    l1_back(*pend.pop(0))
            sc.barrier()
            pool.release(m1)
            wo1 = pool.alloc([128, KC, 1024], BF16)
            two = Tile("wo1")
            DMA("pool", wo1, w_o1.rearrange("(m p) n -> p m n", p=128), "wo1", writes=[two])
            gb = pool.alloc([128, 8], F32)
            tgb = Tile("gb")
            TT(gb, GT(1, 0), SMALL[:, C_BO:C_BO + 8], ALU.mult, [tmod, tl], [tgb])
            for ch in chunk_list(True):
                gc, w = ch["gc"], ch["w"]
                hts = [tH[o] for o in ch["olds"]]
                for dm in range(KC):
                    pb = dm % 2
                    for m in range(8):
                        MM(PS[pb][:, 0:w], wo1[:, m, dm * 128:(dm + 1) * 128], H[:, m, gc], m == 0, m == 7, [two] + hts, [tPS[pb]])
                    STT(X[:, dm, gc], PS[pb][:, 0:w], GT(1, 0)[:, dm:dm + 1], X[:, dm, gc], ALU.mult, ALU.add, [tPS[pb], tmod], ch["xt"])
                    TS(X[:, dm, gc], X[:, dm, gc], gb[:, dm:dm + 1], None, ALU.add, None, [tgb], ch["xt"])
            sc.barrier()
            pool.release(m0)

        if swa_layer():
            return finish(nc, sc, es, dumps, X, outT, tX, None)
        dump("xa1", X[:], allX)
        if stop_after == "L1attn":
            return finish(nc, sc, es, dumps, X, outT, tX, None)
        mlp(1, chunk_list(True))
        dump("xm1", X[:], allX)

        def final_norm():
            sq = pool.alloc([128, KC, CH], BF16)
            rstd = pool.alloc([128, CH], F32)
            Y = [pool.alloc([128, KC, CH], F32) for _ in range(2)]
            tY = [Tile("Y0"), Tile("Y1")]
            tsq = Tile("sqf")
            trs = Tile("rsf")
            xo = outT.rearrange("(kc p) t -> p kc t", p=128)
            evs = []
            for i, ch in enumerate(chunk_list(True)):
                gc, w, s_ = ch["gc"], ch["w"], ch["s"]
                yb = i % 2
                norm_rstd(X[:, :, gc], KC, w, sq[:, :, 0:w], i % 2, rstd[:, 0:w], D, ch["xt"], tsq, trs)
                for kc in range(KC):
                    STT(Y[yb][:, kc, 0:w], X[:, kc, gc], SMALL[:, C_GFIN + kc:C_GFIN + kc + 1], rstd[:, 0:w], ALU.mult, ALU.mult,
                        ch["xt"] + [trs, tl], [tY[yb]])
                o0 = s_ * BLK + ch["off"] - HALO
                evs.append(DMA("sp", xo[:, :, o0:o0 + w], Y[yb][:, :, 0:w], "out", reads=[tY[yb]]))
            return evs

        final_norm()
        return finish(nc, sc, es, dumps, X, outT, tX, "done")


def finish(nc, sc, es, dumps, X, outT, tX, mode):
    if mode is None:
        xo = outT.rearrange("(kc p) t -> p kc t", p=128)
        for s in range(NSEG):
            src = X[:, :, s * SEG + HALO:(s + 1) * SEG]
            dst = xo[:, :, s * BLK:(s + 1) * BLK]
            sc.op("sp", lambda e, src=src, dst=dst: e.dma_start(out=dst, in_=src), reads=tX[s], dma_key="out")
    sc.op("sp", lambda e: None, extra=[sc.last_dma[k] for k in ("out", "dump") if k in sc.last_dma])
    emit(nc, sc, es)
    return nc, dumps


_LAST_SC = {}


def emit(nc, sc, es):
    _LAST_SC.clear()
    _LAST_SC.update({e: q for e, q in sc.q.items()})
    _LAST_SC['nwaits'] = [sum(len(w) for (_, w, _, _) in q) for q in sc.q.values()]
    E = es.enter_context
    signo = {}
    for eng in ENGS:
        n = 0
        for (fn, waits, ev, dk) in sc.q[eng]:
            if dk is None and ev.needed:
                n += 1
                signo[(eng, ev.idx)] = n
    esem = {eng: E(nc.semaphore("s_" + eng)) for eng in ENGS}
    dsem = {k: E(nc.semaphore("d_" + k)) for k in sc.dma_cnt}
    block = E(nc.Block())

    def replay(eng, e):
        for (fn, waits, ev, dk) in sc.q[eng]:
            for d in waits:
                if d.key.startswith("dma:"):
                    e.wait_ge(dsem[d.key[4:]], d.val)
                else:
                    e.wait_ge(esem[d.key], signo[(d.key, d.idx)])
            inst = fn(e)
            if inst is None:
                continue
            if dk is not None:
                inst.then_inc(dsem[dk], 16)
            elif ev.needed:
                inst.then_inc(esem[eng], 1)

    @block.tensor
    def _(e):
        replay("pe", e)

    @block.scalar
    def _(e):
        replay("act", e)

    @block.vector
    def _(e):
        replay("dve", e)

    @block.gpsimd
    def _(e):
        replay("pool", e)

    @block.sync
    def _(e):
        replay("sp", e)


def _core_layout(c):
    b, j = c // 4, c % 4
    blocks = [j, 7 - j]
    return b, blocks


def make_in_maps(inp):
    f32 = np.float32
    x = np.asarray(inp["x"], f32)
    pos = np.asarray(inp["positions"], np.int32)
    half = 32
    inv = (10000.0 ** (-np.arange(half, dtype=f32) / half)).astype(f32)
    invf = np.concatenate([inv, inv])[:, None].astype(f32)

    def featT(v):
        return np.ascontiguousarray(np.asarray(v, f32).reshape(-1, 128).T)

    w_uq = np.asarray(inp["mla_w_uq"][0], f32)
    uq = w_uq.reshape(384, 8, 192)
    w_uq_sw = np.ascontiguousarray(np.concatenate([uq[:, :, 160:192], uq[:, :, 128:160]], -1).reshape(384, 512))
    w_dkv = np.asarray(inp["mla_w_dkv"][0], f32)
    w_dkr_sw = np.ascontiguousarray(np.concatenate([w_dkv[:, 288:320], w_dkv[:, 256:288]], -1))
    w_qkv = np.asarray(inp["swa_w_qkv"][0], f32)
    b_qkv = np.asarray(inp["swa_b_qkv"][0], f32)
    wk = w_qkv[:, 1024:1280].reshape(1024, 2, 2, 64)
    w_k_sw = np.ascontiguousarray(wk[:, :, ::-1, :].reshape(1024, 256))
    bk = b_qkv[1024:1280].reshape(2, 2, 64)
    b_k_sw = np.ascontiguousarray(bk[:, ::-1, :].reshape(256))
    shared = {
        "w_ada": np.asarray(inp["w_ada"], f32),
        "b_adaT": np.ascontiguousarray(np.stack([featT(inp["b_ada"][l]) for l in range(2)])),
        "gmixT": np.ascontiguousarray(np.stack([featT(inp["g_mix"][l]) for l in range(2)])),
        "gmlpT": np.ascontiguousarray(np.stack([featT(inp["g_mlp"][l]) for l in range(2)])),
        "gfinT": featT(inp["g_final"]),
        "w_dq": np.asarray(inp["mla_w_dq"][0], f32),
        "g_qT": featT(inp["mla_g_q"][0]),
        "w_uq": w_uq,
        "w_uq_sw": w_uq_sw,
        "w_dkv": w_dkv,
        "w_dkr_sw": w_dkr_sw,
        "g_kvT": featT(inp["mla_g_kv"][0]),
        "w_ukv": np.asarray(inp["mla_w_ukv"][0], f32),
        "w_o": np.asarray(inp["mla_w_o"][0], f32),
        "invf": invf,
        "w_qkv": w_qkv,
        "w_k_sw": w_k_sw,
        "b_qkvT": featT(b_qkv),
        "b_k_swT": featT(b_k_sw),
        "b_v": np.ascontiguousarray(b_qkv[None, 1280:1536]),
        "sinks": np.asarray(inp["swa_sinks"], f32).reshape(1, 16),
        "w_o1": np.asarray(inp["swa_w_o"][0], f32),
        "b_oT": featT(inp["swa_b_o"][0]),
        "w_ff1": np.asarray(inp["w_ff1"], f32),
        "w_ff2": np.asarray(inp["w_ff2"], f32),
    }
    xT_b = [np.ascontiguousarray(x[b].T) for b in range(2)]
    maps = []
    for c in range(8):
        b, blocks = _core_layout(c)
        xt = np.zeros((D, T0), f32)
        pq = np.zeros((1, T0), np.int32)
        meta = np.zeros((1, 4), f32)
        for s, blk in enumerate(blocks):
            lo = blk * BLK - HALO
            hi = (blk + 1) * BLK
            lo_c = max(lo, 0)
            off = s * SEG + (lo_c - lo)
            xt[:, off:s * SEG + SEG] = xT_b[b][:, lo_c:hi]
            pq[0, off:s * SEG + SEG] = pos[b, lo_c:hi]
            meta[0, s] = lo
            meta[0, 2 + s] = 1.0 if lo >= 0 else 0.0
        m = dict(shared)
        m.update({
            "xT": xt, "xTall": xT_b[b], "posq": pq, "posall": np.ascontiguousarray(pos[b][None, :]),
            "segmeta": meta, "cT": featT(inp["c"][b]),
        })
        maps.append(m)
    return maps


def assemble(results):
    out = np.zeros((2, S, D), np.float32)
    for c in range(8):
        b, blocks = _core_layout(c)
        o = results[c]["outT"]
        for s, blk in enumerate(blocks):
            out[b, blk * BLK:(blk + 1) * BLK, :] = o[:, s * BLK:(s + 1) * BLK].T
    return out


_CACHE = {}


def kernel(**inputs):
    maps = make_in_maps(inputs)
    if "nc" not in _CACHE:
        _CACHE["nc"] = build_program()[0]
    res = run_bass_kernel_spmd(_CACHE["nc"], maps, core_ids=list(range(8)))
    return assemble(res.results)
```

```python
import math
import numpy as np
from contextlib import ExitStack
import concourse.bass as bass
import concourse.mybir as mybir
from concourse.bass_utils import run_bass_kernel_spmd

F32 = mybir.dt.float32
BF16 = mybir.dt.bfloat16
F16 = mybir.dt.float16
I32 = mybir.dt.int32
AF = mybir.ActivationFunctionType
ALU = mybir.AluOpType

D = 1024
KC = 8
S = 8192
NSEG = 2
BLK = 1024
HALO = 128
SEG = BLK + HALO
T0 = NSEG * SEG
CH = 384
NCHS = SEG // CH
NCH = T0 // CH
KCHUNK = 512
NKC = S // KCHUNK
EPS = 1e-6
MLA_SCALE = 192 ** -0.5
SWA_SCALE = 64 ** -0.5
TWO_PI = 2.0 * math.pi
CW_HI = 6.28125
CW_LO = TWO_PI - 6.28125
PI_SAFE = 3.1415925
NEG_BIG = -30000.0
POOL_KIB = 128


class Ev:
    __slots__ = ("key", "val", "clock", "needed", "eng", "idx")

    def __init__(self, key, val, eng, idx):
        self.key = key
        self.val = val
        self.eng = eng
        self.idx = idx
        self.clock = None
        self.needed = False


class Tile:
    __slots__ = ("name", "w", "r")

    def __init__(self, name=""):
        self.name = name
        self.w = None
        self.r = {}


ENGS = ["pe", "act", "dve", "pool", "sp"]


class Sched:
    def __init__(self):
        self.q = {e: [] for e in ENGS}
        self.seen = {e: {} for e in ENGS}
        self.cnt = {e: 0 for e in ENGS}
        self.dma_cnt = {}
        self.last = {e: None for e in ENGS}
        self.last_dma = {}
        self.bar = []

    def op(self, eng, fn, reads=(), writes=(), dma_key=None, extra=(), strict=False):
        deps = []
        for t in reads:
            if t.w is not None:
                deps.append(t.w)
        for t in writes:
            if t.w is not None:
                deps.append(t.w)
            deps.extend(t.r.values())
        deps.extend(extra)
        deps.extend(self.bar)
        seen = self.seen[eng]
        waits = {}
        for d in deps:
            if d.key == eng and not strict:
                continue
            if seen.get(d.key, -1) >= d.val:
                continue
            for k, v in d.clock.items():
                if seen.get(k, -1) < v:
                    seen[k] = v
            if seen.get(d.key, -1) < d.val:
                seen[d.key] = d.val
            d.needed = True
            cur = waits.get(d.key)
            if cur is None or cur.val < d.val:
                waits[d.key] = d
        idx = self.cnt[eng]
        self.cnt[eng] += 1
        if dma_key is None:
            ev = Ev(eng, idx, eng, idx)
        else:
            v = self.dma_cnt.get(dma_key, 0) + 16
            self.dma_cnt[dma_key] = v
            ev = Ev("dma:" + dma_key, v, eng, idx)
            ev.needed = True
            self.last_dma[dma_key] = ev
        clock = dict(seen)
        clock[eng] = idx
        if dma_key is not None:
            clock[eng] = idx - 1
        ev.clock = clock
        self.q[eng].append((fn, list(waits.values()), ev, dma_key))
        self.last[eng] = ev
        for t in writes:
            t.w = ev
            t.r = {}
        for t in reads:
            t.r[ev.key] = ev
        return ev

    def barrier(self):
        evs = [e for e in self.last.values() if e is not None and not e.key.startswith("dma:")]
        evs += list(self.last_dma.values())
        self.bar = evs


class PoolAlloc:
    def __init__(self, ap, nbytes):
        self.ap = ap
        self.nbytes = nbytes
        self.off = 0

    def alloc(self, shape, dtype):
        esz = 4 if dtype in (F32, I32) else 2
        n = 1
        for s in shape[1:]:
            n *= s
        nb = (n * esz + 63) // 64 * 64
        assert self.off + nb <= self.nbytes, f"pool overflow {self.off + nb} > {self.nbytes}"
        a = self.ap[:, self.off // 4:(self.off + nb) // 4]
        self.off += nb
        if dtype != F32:
            a = a.bitcast(dtype)
        a = a[:, 0:n]
        if len(shape) == 3:
            a = a.rearrange("p (a b) -> p a b", a=shape[1])
        elif len(shape) == 4:
            a = a.rearrange("p (a b c) -> p a b c", a=shape[1], b=shape[2])
        if shape[0] < 128:
            a = a[0:shape[0]]
        return a

    def mark(self):
        return self.off

    def release(self, m):
        self.off = m


def build_program(stop_after="all", debug=False):
    nc = bass.Bass("TRN2", target_bir_lowering=False)
    sc = Sched()
    dumps = []

    def din(name, shape, dt=F32):
        return nc.dram_tensor(name, list(shape), dt, kind="ExternalInput").ap()

    xT = din("xT", [D, T0])
    xTall = din("xTall", [D, S])
    posq = din("posq", [1, T0], I32)
    posall = din("posall", [1, S], I32)
    segmeta = din("segmeta", [1, 4])
    cT = din("cT", [128, KC])
    w_ada = din("w_ada", [2, D, 6 * D])
    b_adaT = din("b_adaT", [2, 128, 48])
    gmixT = din("gmixT", [2, 128, KC])
    gmlpT = din("gmlpT", [2, 128, KC])
    gfinT = din("gfinT", [128, KC])
    w_dq = din("w_dq", [D, 384])
    g_qT = din("g_qT", [128, 3])
    w_uq = din("w_uq", [384, 1536])
    w_uq_sw = din("w_uq_sw", [384, 512])
    w_dkv = din("w_dkv", [D, 320])
    w_dkr_sw = din("w_dkr_sw", [D, 64])
    g_kvT = din("g_kvT", [128, 2])
    w_ukv = din("w_ukv", [256, 2048])
    w_o = din("w_o", [D, D])
    invf = din("invf", [64, 1])
    w_qkv = din("w_qkv", [D, 1536])
    w_k_sw = din("w_k_sw", [D, 256])
    b_qkvT = din("b_qkvT", [128, 12])
    b_k_swT = din("b_k_swT", [128, 2])
    b_v = din("b_v", [1, 256])
    sinks = din("sinks", [1, 16])
    w_o1 = din("w_o1", [D, D])
    b_oT = din("b_oT", [128, KC])
    w_ff1 = din("w_ff1", [2, D, 4 * D])
    w_ff2 = din("w_ff2", [2, 4 * D, D])
    outT = nc.dram_tensor("outT", [D, NSEG * BLK], F32, kind="ExternalOutput").ap()

    es = ExitStack()
    with es:
        E = es.enter_context
        X = E(nc.sbuf_tensor("X", [128, KC, T0], F32))
        POOLT = E(nc.sbuf_tensor("POOLT", [128, POOL_KIB * 256], F32))
        CONST = E(nc.sbuf_tensor("CONST", [128, 16], F32))
        MODS = E(nc.sbuf_tensor("MODS", [128, 2, 48], F32))
        AB = E(nc.sbuf_tensor("AB", [128, 2, 2, KC], F32))
        SMALL = E(nc.sbuf_tensor("SMALL", [128, 64], F32))
        ONESB = E(nc.sbuf_tensor("ONESB", [128, 128], BF16))
        IDENT = E(nc.sbuf_tensor("IDENT", [128, 128], BF16))
        QLOC = E(nc.sbuf_tensor("QLOC", [128, SEG], F16))
        KK = E(nc.sbuf_tensor("KK", [128, NSEG, 64], F32))
        META = E(nc.sbuf_tensor("META", [128, 4], F32))
        BM = E(nc.sbuf_tensor("BM", [128, NSEG, NCHS, 64], F32))
        CONDB = E(nc.sbuf_tensor("CONDB", [128, KC], BF16))
        PS = [E(nc.psum_tensor(f"PS{i}", [128, 512], F32)) for i in range(8)]
        pool = PoolAlloc(POOLT, POOL_KIB * 1024)

        tX = [[Tile(f"X{s}_{c}") for c in range(NCHS)] for s in range(NSEG)]
        tPS = [Tile(f"PS{i}") for i in range(8)]
        tCONST = Tile("const")

        C_GQ = 0
        C_GKV = 3
        C_INVF = 5
        C_SGN = 6
        C_HB = 7
        C_GFIN = 16
        C_BO = 24
        C_GB = 32
        C_BQ = 40
        C_BKSW = 52

        def dump(name, ap, tiles, dt=F32):
            if not debug:
                return
            shape = list(ap.shape)
            dr = nc.dram_tensor("dbg_" + name, shape, dt, kind="ExternalOutput").ap()
            dumps.append("dbg_" + name)
            sc.op("sp", lambda e, dr=dr, ap=ap: e.dma_start(out=dr, in_=ap), reads=tiles, dma_key="dump")


        def MM(out, lhsT, rhs, start, stop, reads, writes):
            return sc.op("pe", lambda e: e.matmul(out, lhsT, rhs, start=start, stop=stop), reads=reads, writes=writes)

        def ACT(out, in_, func, reads, writes, bias=None, scale=None):
            kw = {}
            if bias is not None:
                kw["bias"] = bias
            if scale is not None:
                kw["scale"] = scale
            return sc.op("act", lambda e: e.activation(out=out, in_=in_, func=func, **kw), reads=reads, writes=writes)

        def AMUL(out, in_, c, reads, writes):
            return sc.op("act", lambda e: e.mul(out, in_, c), reads=reads, writes=writes)

        def ACOPY(out, in_, reads, writes):
            return sc.op("act", lambda e: e.copy(out, in_), reads=reads, writes=writes)

        def TT(out, in0, in1, op, reads, writes, strict=False):
            return sc.op("dve", lambda e: e.tensor_tensor(out, in0, in1, op), reads=reads, writes=writes, strict=strict)

        def TS(out, in0, s1, s2, op0, op1, reads, writes, strict=False):
            if op1 is None:
                return sc.op("dve", lambda e: e.tensor_scalar(out, in0, s1, s2, op0), reads=reads, writes=writes, strict=strict)
            return sc.op("dve", lambda e: e.tensor_scalar(out, in0, s1, s2, op0, op1), reads=reads, writes=writes, strict=strict)

        def STT(out, in0, scalar, in1, op0, op1, reads, writes, strict=False):
            return sc.op("dve", lambda e: e.scalar_tensor_tensor(out, in0, scalar, in1, op0, op1), reads=reads, writes=writes, strict=strict)

        def RECIP(out, in_, reads, writes, strict=False):
            return sc.op("dve", lambda e: e.reciprocal(out, in_), reads=reads, writes=writes, strict=strict)

        def VCOPY(out, in_, reads, writes):
            return sc.op("dve", lambda e: e.tensor_copy(out, in_), reads=reads, writes=writes)

        def DMA(q, out, in_, key, reads=(), writes=()):
            return sc.op(q, lambda e: e.dma_start(out=out, in_=in_), reads=reads, writes=writes, dma_key=key)

        tl = Tile("smallloads")

        def setup():
            P = "pool"
            sc.op(P, lambda e: e.memset(ONESB[:], 1.0), writes=[tCONST])
            sc.op(P, lambda e: e.memset(CONST[:, 0:1], EPS), writes=[tCONST])
            sc.op(P, lambda e: e.memset(CONST[:, 1:2], 0.0), writes=[tCONST])
            sc.op(P, lambda e: e.memset(CONST[:, 2:3], 1e-18), writes=[tCONST])
            sc.op(P, lambda e: e.memset(SMALL[0:32, C_SGN:C_SGN + 1], -1.0), writes=[tCONST])
            sc.op(P, lambda e: e.memset(SMALL[32:64, C_SGN:C_SGN + 1], 1.0), writes=[tCONST])
            tmp = pool.alloc([128, 128], F32)
            sc.op(P, lambda e: e.iota(tmp, pattern=[[1, 128]], base=0, channel_multiplier=-1,
                                      allow_small_or_imprecise_dtypes=True), writes=[tCONST])
            sc.op(P, lambda e: e.tensor_single_scalar(IDENT[:], tmp, 0.0, ALU.is_equal), writes=[tCONST])
            sc.op(P, lambda e: e.iota(QLOC[:], pattern=[[1, SEG]], base=0, channel_multiplier=0,
                                      allow_small_or_imprecise_dtypes=True), writes=[tCONST])
            kid = pool.alloc([128, 64], F32)
            sc.op(P, lambda e: e.iota(kid, pattern=[[128, 64]], base=0, channel_multiplier=1,
                                      allow_small_or_imprecise_dtypes=True), writes=[tCONST])
            loads = [
                (META[:], segmeta.partition_broadcast(128)),
                (SMALL[:, C_GQ:C_GQ + 3], g_qT),
                (SMALL[:, C_GKV:C_GKV + 2], g_kvT),
                (SMALL[0:64, C_INVF:C_INVF + 1], invf),
                (SMALL[:, C_GFIN:C_GFIN + 8], gfinT),
                (SMALL[:, C_BO:C_BO + 8], b_oT),
                (SMALL[:, C_BQ:C_BQ + 12], b_qkvT),
                (SMALL[:, C_BKSW:C_BKSW + 2], b_k_swT),
            ]
            for o, i in loads:
                DMA("sp", o, i, "c0", writes=[tl])
            kid0 = pool.alloc([128, 64], F32)
            kb = pool.alloc([128, 64], F32)
            sc.op(P, lambda e: e.iota(kid0, pattern=[[128, 64]], base=0, channel_multiplier=0,
                                      allow_small_or_imprecise_dtypes=True), writes=[tCONST])
            for s in range(NSEG):
                TS(KK[:, s, :], kid, META[:, s:s + 1], None, ALU.subtract, None, [tl, tCONST], [tCONST])
                TS(kb, kid0, META[:, s:s + 1], None, ALU.subtract, None, [tl, tCONST], [tCONST])
                for c in range(NCHS):
                    TS(BM[:, s, c, :], kb, 384.0 * c + 383.0, NEG_BIG, ALU.is_gt, ALU.mult, [], [tCONST])
            TS(SMALL[:, C_HB:C_HB + 2], META[:, 2:4], -1.0, -NEG_BIG, ALU.add, ALU.mult, [tl], [tCONST])

        setup()

        tmod = Tile("mods")

        def phase_A():
            m0 = pool.mark()
            cf = pool.alloc([128, KC], F32)
            tc_ = Tile("c")
            DMA("sp", cf, cT, "c1", writes=[tc_])
            ACT(CONDB[:], cf, AF.Silu, [tc_], [tCONST])
            badd = pool.alloc([128, 2, 48], F32)
            gm = pool.alloc([128, 2, 2, KC], F32)
            tb = Tile("badd")
            DMA("sp", badd, b_adaT.rearrange("l p n -> p l n"), "c1", writes=[tb])
            DMA("sp", gm[:, :, 0, :], gmixT.rearrange("l p n -> p l n"), "c1", writes=[tb])
            DMA("sp", gm[:, :, 1, :], gmlpT.rearrange("l p n -> p l n"), "c1", writes=[tb])
            wa = [pool.alloc([128, KC, 768], BF16) for _ in range(2)]
            twa = [Tile("wa0"), Tile("wa1")]
            n = 0
            for l in range(2):
                wsrc = w_ada[l].rearrange("(kc p) n -> p kc n", p=128)
                for cc in range(8):
                    b = n % 2
                    n += 1
                    DMA("pool", wa[b], wsrc[:, :, cc * 768:(cc + 1) * 768], f"wa{b}", writes=[twa[b]])
                    for nn in range(6):
                        col = l * 48 + cc * 6 + nn
                        for kc in range(KC):
                            MM(PS[0][:, col:col + 1], wa[b][:, kc, nn * 128:(nn + 1) * 128], CONDB[:, kc:kc + 1],
                               kc == 0, kc == KC - 1, [twa[b], tCONST], [tPS[0]])
                TT(MODS[:, l, :], PS[0][:, l * 48:(l + 1) * 48], badd[:, l, :], ALU.add, [tPS[0], tb], [tmod])
                STT(AB[:, l, 0, :], MODS[:, l, 8:16], 1.0, gm[:, l, 0, :], ALU.add, ALU.mult, [tb, tmod], [tmod], strict=True)
                STT(AB[:, l, 1, :], MODS[:, l, 32:40], 1.0, gm[:, l, 1, :], ALU.add, ALU.mult, [tb], [tmod])
            dump("mods", MODS[:], [tmod])
            sc.barrier()
            pool.release(m0)

        phase_A()

        def SH(l, which):
            return MODS[:, l, 0:8] if which == 0 else MODS[:, l, 24:32]

        def GT(l, which):
            return MODS[:, l, 16:24] if which == 0 else MODS[:, l, 40:48]

        def norm_rstd(src3, nk, n, sqbuf, ps_i, rstd, dim, reads, tsq, trstd):
            ACT(sqbuf, src3, AF.Square, reads, [tsq])
            for k in range(nk):
                MM(PS[ps_i][:, 0:n], ONESB[:], sqbuf[:, k, :], k == 0, k == nk - 1, [tsq, tCONST], [tPS[ps_i]])
            ACT(rstd, PS[ps_i][:, 0:n], AF.Ln, [tPS[ps_i]], [trstd], bias=CONST[:, 0:1], scale=1.0 / dim)
            ACT(rstd, rstd, AF.Exp, [], [trstd], scale=-0.5)

        def rope_tables(pos_ap, ang, nq, r, cos2, sin2, tpos, ttab, tt):
            inv = SMALL[0:64, C_INVF:C_INVF + 1]
            sgn = SMALL[0:64, C_SGN:C_SGN + 1]
            VCOPY(ang, pos_ap, [tpos], [tt])
            TS(ang, ang, inv, None, ALU.mult, None, [tl], [tt])
            for which in range(2):
                if which == 1:
                    TS(ang, ang, math.pi / 2, None, ALU.add, None, [], [tt])
                TS(r, ang, 1.0 / TWO_PI, None, ALU.mult, None, [], [tt])
                VCOPY(nq, r, [], [tt])
                STT(r, nq, -CW_HI, ang, ALU.mult, ALU.add, [], [tt])
                STT(r, nq, -CW_LO, r, ALU.mult, ALU.add, [], [tt])
                TS(r, r, -PI_SAFE, PI_SAFE, ALU.max, ALU.min, [], [tt])
                if which == 0:
                    ACT(sin2, r, AF.Sin, [tt, tCONST], [ttab], scale=sgn)
                else:
                    ACT(cos2, r, AF.Sin, [tt], [ttab])

        LAT = pool.alloc([128, 2, S], BF16)
        KR = pool.alloc([128, S], BF16)
        tKRz = Tile("krz")
        sc.op("pool", lambda e: e.memset(KR[64:128, :], 0.0), writes=[tKRz])
        tLAT = [Tile(f"lat{i}") for i in range(NKC)]
        tKR = [Tile(f"kr{i}") for i in range(NKC)]
        m_attn = pool.mark()

        def phase_B():
            wdkv = pool.alloc([128, KC, 384], BF16)
            twd = Tile("wdkv")
            twd2 = Tile("wdkv2")
            DMA("pool", wdkv[:, :, 0:320], w_dkv.rearrange("(kc p) n -> p kc n", p=128), "wdkv", writes=[twd])
            DMA("pool", wdkv[:, :, 320:384], w_dkr_sw.rearrange("(kc p) n -> p kc n", p=128), "wdkv2", writes=[twd2])
            xa = [pool.alloc([128, KC, KCHUNK], F32) for _ in range(2)]
            hb = [pool.alloc([128, KC, KCHUNK], BF16) for _ in range(2)]
            posb = [pool.alloc([64, KCHUNK], I32) for _ in range(2)]
            rstd = pool.alloc([128, KCHUNK], F32)
            rstd2 = pool.alloc([128, KCHUNK], F32)
            sqc = pool.alloc([128, 2, KCHUNK], BF16)
            ang = pool.alloc([64, KCHUNK], F32)
            nq = pool.alloc([64, KCHUNK], I32)
            rr = pool.alloc([64, KCHUNK], F32)
            cos2 = pool.alloc([64, KCHUNK], F32)
            sin2 = pool.alloc([64, KCHUNK], F32)
            ku = pool.alloc([64, KCHUNK], F32)
            kv = pool.alloc([64, KCHUNK], F32)
            txa = [Tile("xa0"), Tile("xa1")]
            th = [Tile("h0"), Tile("h1")]
            tpos = [Tile("pos0"), Tile("pos1")]
            trs = Tile("rstd")
            trs2 = Tile("rstd2")
            tsqc = Tile("sqc")
            ttab = Tile("tab")
            ttmp = Tile("ropetmp")
            tku = Tile("ku")
            xsrc = xTall.rearrange("(kc p) t -> p kc t", p=128)

            txak = [[Tile(f"xa{b}_{k}") for k in range(KC)] for b in range(2)]

            def A1(i):
                b = i % 2
                cols = slice(i * KCHUNK, (i + 1) * KCHUNK)
                DMA("sp", xa[b], xsrc[:, :, cols], f"xa{b}", writes=[txa[b]] + txak[b])
                DMA("sp", posb[b], posall[:, cols].partition_broadcast(64), f"pos{b}", writes=[tpos[b]])
                norm_rstd(xa[b], KC, KCHUNK, hb[b], b, rstd, D, [txa[b]], th[b], trs)
                for kc in range(KC):
                    TT(xa[b][:, kc, :], xa[b][:, kc, :], rstd, ALU.mult, [trs, txa[b]], [txak[b][kc]])

            def A2(i):
                b = i % 2
                for kc in range(KC):
                    ACT(hb[b][:, kc, :], xa[b][:, kc, :], AF.Identity, [txak[b][kc], tmod], [th[b]],
                        bias=SH(0, 0)[:, kc:kc + 1], scale=AB[:, 0, 0, kc:kc + 1])

            def B1(i):
                b = i % 2
                cols = slice(i * KCHUNK, (i + 1) * KCHUNK)
                for (pi, c0, c1, m) in [(2, 0, 128, 128), (3, 128, 256, 128), (4, 256, 320, 64), (5, 320, 384, 64)]:
                    for kc in range(KC):
                        MM(PS[pi][0:m, 0:KCHUNK], wdkv[:, kc, c0:c1], hb[b][:, kc, :], kc == 0, kc == KC - 1,
                           [th[b], twd, twd2], [tPS[pi]])
                ACT(sqc[:, 0, :], PS[2][:, 0:KCHUNK], AF.Square, [tPS[2]], [tsqc])
                ACT(sqc[:, 1, :], PS[3][:, 0:KCHUNK], AF.Square, [tPS[3]], [tsqc])
                for k in range(2):
                    MM(PS[6][:, 0:KCHUNK], ONESB[:], sqc[:, k, :], k == 0, k == 1, [tsqc, tCONST], [tPS[6]])
                ACT(rstd2, PS[6][:, 0:KCHUNK], AF.Ln, [tPS[6]], [trs2], bias=CONST[:, 0:1], scale=1.0 / 256)
                ACT(rstd2, rstd2, AF.Exp, [], [trs2], scale=-0.5)
                for k in range(2):
                    STT(LAT[:, k, cols], PS[2 + k][:, 0:KCHUNK], SMALL[:, C_GKV + k:C_GKV + k + 1], rstd2, ALU.mult, ALU.mult,
                        [tPS[2 + k], trs2, tl], [tLAT[i]])

            def B2(i):
                cols = slice(i * KCHUNK, (i + 1) * KCHUNK)
                TT(ku, PS[4][0:64, 0:KCHUNK], cos2, ALU.mult, [tPS[4], ttab], [tku])
                TT(kv, PS[5][0:64, 0:KCHUNK], sin2, ALU.mult, [tPS[5], ttab], [tku])
                TT(KR[0:64, cols], ku, kv, ALU.add, [tku], [tKR[i]])

            A1(0)
            A2(0)
            rope_tables(posb[0], ang, nq, rr, cos2, sin2, tpos[0], ttab, ttmp)
            for i in range(NKC):
                if i + 1 < NKC:
                    A1(i + 1)
                B1(i)
                if i + 1 < NKC:
                    A2(i + 1)
                B2(i)
                if i + 1 < NKC:
                    rope_tables(posb[(i + 1) % 2], ang, nq, rr, cos2, sin2, tpos[(i + 1) % 2], ttab, ttmp)
            dump("lat", LAT, tLAT, BF16)
            dump("kr", KR[0:64, :], tKR, BF16)
            sc.barrier()

        xsrc_own = xT.rearrange("(kc p) t -> p kc t", p=128)
        for s in range(NSEG):
            for c in range(NCHS):
                cols = slice(s * SEG + c * CH, s * SEG + (c + 1) * CH)
                DMA("sp", X[:, :, cols], xsrc_own[:, :, cols], f"x{s}{c}", writes=[tX[s][c]])
        phase_B()
        pool.release(m_attn)
        if stop_after == "B":
            return finish(nc, sc, es, dumps, X, outT, tX, None)

        def attn_segment(s):
            m0 = pool.mark()
            NK = 4096 if s == 0 else 8192
            cqn = pool.alloc([128, 3, SEG], BF16)
            cos2q = pool.alloc([64, SEG], F32)
            sin2q = pool.alloc([64, SEG], F32)
            qn = pool.alloc([128, SEG], BF16)
            qr = pool.alloc([128, SEG], BF16)
            tqrz = Tile("qrz")
            sc.op("pool", lambda e: e.memset(qr[64:128, :], 0.0), writes=[tqrz])
            wh = [pool.alloc([128, 2304], BF16) for _ in range(2)]
            pbuf = [pool.alloc([128, CH], BF16) for _ in range(4)]
            obuf = [pool.alloc([128, CH], BF16) for _ in range(2)]
            rden = pool.alloc([128, CH], F32)
            tcqn = [Tile(f"cqn{c}") for c in range(NCHS)]
            ttabq = [Tile(f"tabq{c}") for c in range(NCHS)]
            tqn = [Tile(f"qn{c}") for c in range(NCHS)]
            tqr = [Tile(f"qr{c}") for c in range(NCHS)]
            twh = [[Tile(f"wh{b}_{k}") for k in range(4)] for b in range(2)]
            tp = [Tile(f"p{i}") for i in range(4)]
            tob = [Tile("ob0"), Tile("ob1")]
            trden = Tile("rden")
            m1 = pool.mark()
            wdq = pool.alloc([128, KC, 384], BF16)
            twdq = Tile("wdq")
            DMA("pool", wdq, w_dq.rearrange("(kc p) n -> p kc n", p=128), "wdq", writes=[twdq])
            hq = pool.alloc([128, KC, CH], BF16)
            tt = pool.alloc([128, KC, CH], F32)
            rstd = pool.alloc([128, CH], F32)
            rstdq = pool.alloc([128, CH], F32)
            sqq = pool.alloc([128, 3, CH], BF16)
            posb = pool.alloc([64, CH], I32)
            ang = pool.alloc([64, CH], F32)
            nq_ = pool.alloc([64, CH], I32)
            rr = pool.alloc([64, CH], F32)
            th = Tile("hq")
            tttk = [Tile(f"ttq{k}") for k in range(KC)]
            trs = Tile("rs")
            trsq = Tile("rsq")
            tsqq = Tile("sqq")
            tpos = Tile("posq")
            ttmp = Tile("ropetmpq")
            for c in range(NCHS):
                lc = slice(c * CH, (c + 1) * CH)
                gc = slice(s * SEG + c * CH, s * SEG + (c + 1) * CH)
                DMA("sp", posb, posq[:, gc].partition_broadcast(64), "posq", writes=[tpos])
                norm_rstd(X[:, :, gc], KC, CH, hq, 0, rstd, D, [tX[s][c]], th, trs)
                for kc in range(KC):
                    TT(tt[:, kc, :], X[:, kc, gc], rstd, ALU.mult, [trs, tX[s][c]], [tttk[kc]])
                    ACT(hq[:, kc, :], tt[:, kc, :], AF.Identity, [tttk[kc], tmod], [th],
                        bias=SH(0, 0)[:, kc:kc + 1], scale=AB[:, 0, 0, kc:kc + 1])
                for m in range(3):
                    for kc in range(KC):
                        MM(PS[1 + m][:, 0:CH], wdq[:, kc, m * 128:(m + 1) * 128], hq[:, kc, :], kc == 0, kc == KC - 1,
                           [th, twdq], [tPS[1 + m]])
                for m in range(3):
                    ACT(sqq[:, m, :], PS[1 + m][:, 0:CH], AF.Square, [tPS[1 + m]], [tsqq])
                for m in range(3):
                    MM(PS[4][:, 0:CH], ONESB[:], sqq[:, m, :], m == 0, m == 2, [tsqq, tCONST], [tPS[4]])
                ACT(rstdq, PS[4][:, 0:CH], AF.Ln, [tPS[4]], [trsq], bias=CONST[:, 0:1], scale=1.0 / 384)
                ACT(rstdq, rstdq, AF.Exp, [], [trsq], scale=-0.5)
                for m in range(3):
                    STT(cqn[:, m, lc], PS[1 + m][:, 0:CH], SMALL[:, C_GQ + m:C_GQ + m + 1], rstdq, ALU.mult, ALU.mult,
                        [tPS[1 + m], trsq, tl], [tcqn[c]])
                rope_tables(posb, ang, nq_, rr, cos2q[:, lc], sin2q[:, lc], tpos, ttabq[c], ttmp)
            if s == 0:
                dump("cqn0", cqn, tcqn, BF16)
            sc.barrier()
            pool.release(m1)
            kh = pool.alloc([128, NK], BF16)
            vh = pool.alloc([128, NK // 128, 128], BF16)
            u1 = pool.alloc([64, CH], F32)
            u2 = pool.alloc([64, CH], F32)
            NKCH = NK // 512
            tkh = [Tile(f"kh{i}") for i in range(NKCH)]
            tvh = [Tile(f"vh{i}") for i in range(NKCH)]
            tu = Tile("u")
            uq_src = w_uq.rearrange("(m p) n -> p m n", p=128)
            uqs_src = w_uq_sw.rearrange("(m p) n -> p m n", p=128)
            ukv_src = w_ukv.rearrange("(m p) n -> p m n", p=128)

            def wviews(b):
                w = wh[b]
                return (w[:, 0:576].rearrange("p (m n) -> p m n", m=3), w[:, 576:768].rearrange("p (m n) -> p m n", m=3),
                        w[:, 768:1280].rearrange("p (m n) -> p m n", m=2), w[:, 1280:2304])

            def load_wh(h):
                b = h % 2
                wuq, wuqs, wukv, wo_h = wviews(b)
                DMA("pool", wuq, uq_src[:, :, h * 192:(h + 1) * 192], f"wh{b}0", writes=[twh[b][0]])
                DMA("pool", wuqs, uqs_src[:, :, h * 64:(h + 1) * 64], f"wh{b}1", writes=[twh[b][1]])
                DMA("pool", wukv, ukv_src[:, :, h * 256:(h + 1) * 256], f"wh{b}2", writes=[twh[b][2]])
                DMA("pool", wo_h, w_o[h * 128:(h + 1) * 128, :], f"wh{b}3", writes=[twh[b][3]])

            load_wh(0)
            cnt = {"p": 0, "o": 0, "s": 0}

            def head(h):
                b = h % 2
                wuq, wuqs, wukv, wo_h = wviews(b)
                for c in range(NCHS):
                    lc = slice(c * CH, (c + 1) * CH)
                    for m in range(3):
                        MM(PS[6][:, 0:CH], wuq[:, m, 0:128], cqn[:, m, lc], m == 0, m == 2, [twh[b][0], tcqn[c]], [tPS[6]])
                    AMUL(qn[:, lc], PS[6][:, 0:CH], MLA_SCALE, [tPS[6]], [tqn[c]])
                    for m in range(3):
                        MM(PS[7][0:64, 0:CH], wuq[:, m, 128:192], cqn[:, m, lc], m == 0, m == 2, [twh[b][0], tcqn[c]], [tPS[7]])
                    STT(u1, PS[7][0:64, 0:CH], MLA_SCALE, cos2q[:, lc], ALU.mult, ALU.mult, [tPS[7], ttabq[c]], [tu])
                    for m in range(3):
                        MM(PS[7][0:64, 0:CH], wuqs[:, m, :], cqn[:, m, lc], m == 0, m == 2, [twh[b][1], tcqn[c]], [tPS[7]])
                    STT(u2, PS[7][0:64, 0:CH], MLA_SCALE, sin2q[:, lc], ALU.mult, ALU.mult, [tPS[7], ttabq[c]], [tu])
                    TT(qr[0:64, lc], u1, u2, ALU.add, [tu], [tqr[c]])
                for i in range(NKCH):
                    cols = slice(i * 512, (i + 1) * 512)
                    bk = 6 if i % 2 == 0 else 0
                    bv_ = 7 if i % 2 == 0 else 1
                    for m in range(2):
                        MM(PS[bk][:, 0:512], wukv[:, m, 0:128], LAT[:, m, cols], m == 0, m == 1, [twh[b][2], tLAT[i]], [tPS[bk]])
                    if i % 2 == 0:
                        VCOPY(kh[:, cols], PS[bk][:, 0:512], [tPS[bk]], [tkh[i]])
                    else:
                        ACOPY(kh[:, cols], PS[bk][:, 0:512], [tPS[bk]], [tkh[i]])
                    for t in range(4):
                        kt = i * 4 + t
                        for m in range(2):
                            MM(PS[bv_][:, t * 128:(t + 1) * 128], LAT[:, m, kt * 128:(kt + 1) * 128], wukv[:, m, 128:256],
                               m == 0, m == 1, [twh[b][2], tLAT[i]], [tPS[bv_]])
                    if i % 2 == 0:
                        ACOPY(vh[:, i * 4:(i + 1) * 4, :], PS[bv_][:, 0:512].rearrange("p (a b) -> p a b", a=4), [tPS[bv_]], [tvh[i]])
                    else:
                        VCOPY(vh[:, i * 4:(i + 1) * 4, :], PS[bv_][:, 0:512].rearrange("p (a b) -> p a b", a=4), [tPS[bv_]], [tvh[i]])
                info = []
                tiles = []
                for c in range(NCHS):
                    if s == 0:
                        nkt = 26 + 3 * c
                        full_upto = 3 * c - 2
                    else:
                        nkt = 58 + 3 * c
                        full_upto = 30 + 3 * c
                    nkt = min(nkt, NK // 128)
                    ob = cnt["o"] % 2
                    cnt["o"] += 1
                    info.append((nkt, full_upto, ob))
                    tiles += [(c, kt) for kt in range(nkt)]

                def front(c, kt):
                    nkt, full_upto, ob = info[c]
                    lc = slice(c * CH, (c + 1) * CH)
                    sb = cnt["s"] % 2
                    cnt["s"] += 1
                    pS = PS[sb]
                    kcols = slice(kt * 128, (kt + 1) * 128)
                    MM(pS[:, 0:CH], kh[:, kcols], qn[:, lc], True, False, [tkh[kt // 4], tqn[c]], [tPS[sb]])
                    MM(pS[:, 0:CH], KR[:, kcols], qr[:, lc], False, True, [tKR[kt // 4], tqr[c], tKRz, tqrz], [tPS[sb]])
                    pb = cnt["p"] % len(pbuf)
                    cnt["p"] += 1
                    P = pbuf[pb]
                    p0s = [-1, 7, 15, 23] if s == 0 else [31, 39, 47, 55]
                    may_full = kt >= min(p0s) + 3 * c + 3
                    may_diag = any(0 <= kt - (p + 3 * c) <= 2 for p in p0s)
                    if may_full:
                        ACT(P, pS[:, 0:CH], AF.Exp, [tPS[sb], tCONST], [tp[pb]], bias=BM[:, s, c, kt:kt + 1], scale=1.0)
                    else:
                        ACT(P, pS[:, 0:CH], AF.Exp, [tPS[sb]], [tp[pb]])
                    if may_diag:
                        STT(P, QLOC[:, lc], KK[:, s, kt:kt + 1], P, ALU.is_ge, ALU.mult, [tCONST], [tp[pb]])
                    return pb

                def back(c, kt, pb):
                    nkt, full_upto, ob = info[c]
                    lc = slice(c * CH, (c + 1) * CH)
                    gc = slice(s * SEG + c * CH, s * SEG + (c + 1) * CH)
                    po = PS[2 + ob]
                    pl = PS[4 + ob]
                    P = pbuf[pb]
                    MM(po[:, 0:CH], vh[:, kt, :], P, kt == 0, kt == nkt - 1, [tp[pb], tvh[kt // 4]], [tPS[2 + ob]])
                    MM(pl[:, 0:CH], ONESB[:], P, kt == 0, kt == nkt - 1, [tp[pb], tCONST], [tPS[4 + ob]])
                    if kt != nkt - 1:
                        return
                    O = obuf[ob]
                    ACT(rden, pl[:, 0:CH], AF.Ln, [tPS[4 + ob], tCONST], [trden], bias=CONST[:, 2:3], scale=1.0)
                    ACT(rden, rden, AF.Exp, [], [trden], scale=-1.0)
                    TT(O, po[:, 0:CH], rden, ALU.mult, [tPS[2 + ob], trden], [tob[ob]])

                    def proj():
                        for dm in range(KC):
                            MM(PS[6 + dm % 2][:, 0:CH], wo_h[:, dm * 128:(dm + 1) * 128], O, True, True, [tob[ob], twh[b][3]], [tPS[6 + dm % 2]])
                            STT(X[:, dm, gc], PS[6 + dm % 2][:, 0:CH], GT(0, 0)[:, dm:dm + 1], X[:, dm, gc], ALU.mult, ALU.add,
                                [tPS[6 + dm % 2], tmod], [tX[s][c]])
                    deferred.append([8, proj])

                DEPTH = 2
                pend = []
                deferred = []

                def tick():
                    for d in deferred:
                        d[0] -= 1
                    while deferred and deferred[0][0] <= 0:
                        deferred.pop(0)[1]()

                for (c, kt) in tiles:
                    pend.append((c, kt, front(c, kt)))
                    if len(pend) > DEPTH:
                        back(*pend.pop(0))
                        tick()
                while pend:
                    back(*pend.pop(0))
                    tick()
                while deferred:
                    deferred.pop(0)[1]()

            for h in range(8):
                if h + 1 < 8:
                    load_wh(h + 1)
                head(h)
            sc.barrier()
            pool.release(m0)

        for s in range(NSEG):
            attn_segment(s)
        allX = [t for ts in tX for t in ts]
        dump("xa0", X[:], allX)
        if stop_after == "L0attn":
            return finish(nc, sc, es, dumps, X, outT, tX, None)
        pool.release(0)

        def chunk_list(own):
            out = []
            for s in range(NSEG):
                segs = [(128, 384), (512, 384), (896, 256)] if own else [(0, 384), (384, 384), (768, 384)]
                for (off, w) in segs:
                    g0 = s * SEG + off
                    olds = sorted(set([off // CH, (off + w - 1) // CH]))
                    out.append(dict(s=s, off=off, gc=slice(g0, g0 + w), w=w, xt=[tX[s][o] for o in olds],
                                    olds=[s * NCHS + o for o in olds]))
            return out

        def norm_mod_to(l, which, H, tH, chunks):
            tt = pool.alloc([128, KC, CH], F32)
            rstd = pool.alloc([128, CH], F32)
            tttk = [Tile(f"tt{k}") for k in range(KC)]
            trs = Tile("rs")
            gmul = AB[:, l, which, :]
            for i, ch in enumerate(chunks):
                gc, w = ch["gc"], ch["w"]
                norm_rstd(X[:, :, gc], KC, w, H[:, :, gc], i % 2, rstd[:, 0:w], D, ch["xt"], tH[i], trs)
                for kc in range(KC):
                    TT(tt[:, kc, 0:w], X[:, kc, gc], rstd[:, 0:w], ALU.mult, [trs] + ch["xt"], [tttk[kc]])
                    ACT(H[:, kc, gc], tt[:, kc, 0:w], AF.Identity, [tttk[kc], tmod], [tH[i]],
                        bias=SH(l, which)[:, kc:kc + 1], scale=gmul[:, kc:kc + 1])

        def mlp(l, chunks):
            m0 = pool.mark()
            H = pool.alloc([128, KC, T0], BF16)
            tH = [Tile(f"H{i}") for i in range(len(chunks))]
            W1 = [pool.alloc([128, KC, 1024], BF16) for _ in range(2)]
            W2 = [pool.alloc([128, 8, 1024], BF16) for _ in range(2)]
            tW1 = [Tile("W1a"), Tile("W1b")]
            tW2 = [Tile("W2a"), Tile("W2b")]
            A = [pool.alloc([128, 8, CH], BF16) for _ in range(2)]
            R = [pool.alloc([128, CH], BF16) for _ in range(2)]
            tA = [Tile("A0"), Tile("A1")]
            tR = [Tile("R0"), Tile("R1")]
            w1src = w_ff1[l].rearrange("(kc p) n -> p kc n", p=128)
            w2src = w_ff2[l].rearrange("(f p) n -> p f n", p=128)

            def loadW(fq):
                b = fq % 2
                DMA("pool", W1[b], w1src[:, :, fq * 1024:(fq + 1) * 1024], f"W1{b}", writes=[tW1[b]])
                DMA("pool", W2[b], w2src[:, fq * 8:(fq + 1) * 8, :], f"W2{b}", writes=[tW2[b]])

            loadW(0)
            m1 = pool.mark()
            norm_mod_to(l, 1, H, tH, chunks)
            pool.release(m1)
            cnt = {"u": 0, "a": 0, "y": 0}
            for fq in range(4):
                b = fq % 2
                if fq + 1 < 4:
                    loadW(fq + 1)
                for i, ch in enumerate(chunks):
                    gc, w = ch["gc"], ch["w"]
                    ab = cnt["a"] % 2
                    cnt["a"] += 1
                    for fc in range(8):
                        ub = cnt["u"] % 3
                        cnt["u"] += 1
                        for kc in range(KC):
                            MM(PS[ub][:, 0:w], W1[b][:, kc, fc * 128:(fc + 1) * 128], H[:, kc, gc], kc == 0, kc == KC - 1,
                               [tW1[b], tH[i]], [tPS[ub]])
                        rb = fc % 2
                        ACT(R[rb][:, 0:w], PS[ub][:, 0:w], AF.Relu, [tPS[ub]], [tR[rb]])
                        TT(A[ab][:, fc, 0:w], R[rb][:, 0:w], R[rb][:, 0:w], ALU.mult, [tR[rb]], [tA[ab]])
                    for dm in range(KC):
                        yb = 3 + cnt["y"] % 2
                        cnt["y"] += 1
                        for fc in range(8):
                            MM(PS[yb][:, 0:w], W2[b][:, fc, dm * 128:(dm + 1) * 128], A[ab][:, fc, 0:w], fc == 0, fc == 7,
                               [tW2[b], tA[ab]], [tPS[yb]])
                        STT(X[:, dm, gc], PS[yb][:, 0:w], GT(l, 1)[:, dm:dm + 1], X[:, dm, gc], ALU.mult, ALU.add,
                            [tPS[yb], tmod], ch["xt"])
            sc.barrier()
            pool.release(m0)

        mlp(0, chunk_list(False))
        dump("xm0", X[:], allX)
        if stop_after == "L0":
            return finish(nc, sc, es, dumps, X, outT, tX, None)

        def swa_layer():
            l = 1
            m0 = pool.mark()
            H = pool.alloc([128, KC, T0], BF16)
            tH = [Tile(f"H1_{i}") for i in range(NCH)]
            KT = pool.alloc([128, 2, T0], BF16)
            KTs = pool.alloc([128, 2, T0], BF16)
            NT = T0 // 128
            VP = pool.alloc([128, NT, 4, 65], BF16)
            tKT = [Tile(f"KT{i}") for i in range(NCH)]
            tVP = [Tile(f"VP{i}") for i in range(NT)]
            tvone = Tile("vone")
            sc.op("pool", lambda e: e.memset(VP[:, :, :, 64:65], 1.0), writes=[tvone])
            bvb = pool.alloc([128, 256], F32)
            sk = pool.alloc([128, 16], F32)
            esink = pool.alloc([128, 16], F32)
            bqs = pool.alloc([128, 8], F32)
            m1 = pool.mark()
            norm_mod_to(l, 0, H, tH, chunk_list(False))
            sc.barrier()
            pool.release(m1)
            wkv = pool.alloc([128, KC, 768], BF16)
            twkv = [Tile("wkv0"), Tile("wkv1")]
            qsrc = w_qkv.rearrange("(kc p) n -> p kc n", p=128)
            DMA("pool", wkv[:, :, 0:512], qsrc[:, :, 1024:1536], "wkv0", writes=[twkv[0]])
            DMA("pool", wkv[:, :, 512:768], w_k_sw.rearrange("(kc p) n -> p kc n", p=128), "wkv1", writes=[twkv[1]])
            tbv = Tile("bvb")
            DMA("sp", bvb, b_v.partition_broadcast(128), "c2", writes=[tbv])
            tsk = Tile("sk")
            DMA("sp", sk, sinks.partition_broadcast(128), "c2", writes=[tsk])
            for i in range(NCH):
                gc = slice(i * CH, (i + 1) * CH)
                for m in range(2):
                    for kc in range(KC):
                        MM(PS[m][:, 0:CH], wkv[:, kc, m * 128:(m + 1) * 128], H[:, kc, gc], kc == 0, kc == KC - 1, [twkv[0], tH[i]], [tPS[m]])
                    ACT(KT[:, m, gc], PS[m][:, 0:CH], AF.Identity, [tPS[m], tl], [tKT[i]], bias=SMALL[:, C_BQ + 8 + m:C_BQ + 9 + m], scale=1.0)
                    for kc in range(KC):
                        MM(PS[2 + m][:, 0:CH], wkv[:, kc, 512 + m * 128:512 + (m + 1) * 128], H[:, kc, gc], kc == 0, kc == KC - 1,
                           [twkv[1], tH[i]], [tPS[2 + m]])
                    ACT(KTs[:, m, gc], PS[2 + m][:, 0:CH], AF.Identity, [tPS[2 + m], tl], [tKT[i]], bias=SMALL[:, C_BKSW + m:C_BKSW + m + 1], scale=1.0)
                for t3 in range(3):
                    t = i * 3 + t3
                    tc = slice(t * 128, (t + 1) * 128)
                    pb = 4 + t % 2
                    for kc in range(KC):
                        MM(PS[pb][:, 0:256], H[:, kc, tc], wkv[:, kc, 256:512], kc == 0, kc == KC - 1, [twkv[0], tH[i]], [tPS[pb]])
                    TT(VP[:, t, :, 0:64], PS[pb][:, 0:256].rearrange("p (g d) -> p g d", g=4), bvb[:].rearrange("p (g d) -> p g d", g=4),
                       ALU.add, [tPS[pb], tbv, tvone], [tVP[t]])
            sc.barrier()
            pool.release(m1)
            if stop_after == "L1kv":
                return True
            wq = pool.alloc([128, KC, 1024], BF16)
            twq = Tile("wq")
            DMA("pool", wq, qsrc[:, :, 0:1024], "wq", writes=[twq])
            BIAS = pool.alloc([128, 4, 2, 512], F32)
            tBIAS = Tile("bias")
            d0 = pool.alloc([128, 128], F32)
            mc = pool.alloc([128, 128], F32)
            mp = pool.alloc([128, 128], F32)
            sc.op("pool", lambda e: e.iota(d0, pattern=[[1, 128]], base=0, channel_multiplier=-1, allow_small_or_imprecise_dtypes=True),
                  writes=[tBIAS])
            TS(mc, d0, 0.0, NEG_BIG, ALU.is_lt, ALU.mult, [tBIAS], [tBIAS])
            TS(mp, d0, 0.0, NEG_BIG, ALU.is_ge, ALU.mult, [], [tBIAS])
            ORDR = [0, 2, 1, 3]
            for hd in range(16):
                g, hh = hd // 4, hd % 4
                j = ORDR.index(hh)
                slope = 2.0 ** (-(hd + 1) / 2.0)
                STT(BIAS[:, g, 1, j * 128:(j + 1) * 128], d0, -slope, mc, ALU.mult, ALU.add, [], [tBIAS])
                STT(BIAS[:, g, 0, j * 128:(j + 1) * 128], d0, -slope, mp, ALU.mult, ALU.add, [], [tBIAS])
                TS(BIAS[:, g, 0, j * 128:(j + 1) * 128], BIAS[:, g, 0, j * 128:(j + 1) * 128], -128.0 * slope, None, ALU.add, None, [], [tBIAS])
            ACT(esink, sk, AF.Exp, [tsk], [tBIAS])
            TS(bqs, SMALL[:, C_BQ:C_BQ + 8], SWA_SCALE, None, ALU.mult, None, [tl], [tBIAS])
            QT = [pool.alloc([128, 8, CH], BF16) for _ in range(2)]
            tQT = [Tile("QT0"), Tile("QT1")]
            SB = [pool.alloc([128, 512], F32) for _ in range(2)]
            tSB = [Tile("SB0"), Tile("SB1")]
            PB = [pool.alloc([128, 512], BF16) for _ in range(6)]
            tPB = [Tile(f"PB{i}") for i in range(6)]
            OTs = [pool.alloc([128, 1024], BF16) for _ in range(2)]
            tOTs = [Tile("OT0"), Tile("OT1")]
            den = pool.alloc([128, 4], F32)
            tden = Tile("den")
            PST = PS[7][:].bitcast(BF16)
            cnt = {"sb": 0, "pb": 0, "sc": 0}
            if stop_after == "L1bias":
                return True
            if True:
                def qgen(i):
                    gc = slice(i * CH, (i + 1) * CH)
                    qb = i % 2
                    for m in range(8):
                        pq = 6 + m % 2
                        for kc in range(KC):
                            MM(PS[pq][:, 0:CH], wq[:, kc, m * 128:(m + 1) * 128], H[:, kc, gc], kc == 0, kc == KC - 1, [twq, tH[i]], [tPS[pq]])
                        ACT(QT[qb][:, m, :], PS[pq][:, 0:CH], AF.Identity, [tPS[pq], tBIAS], [tQT[qb]], bias=bqs[:, m:m + 1], scale=SWA_SCALE)

                def l1_front(i, t3, g):
                    s, c, qb = i // NCHS, i % NCHS, i % 2
                    lt = c * 3 + t3
                    t = i * 3 + t3
                    qc = slice(t3 * 128, (t3 + 1) * 128)
                    pbs = []
                    for kk in range(2):
                        kt = t - 1 + kk
                        kcols = slice(kt * 128, (kt + 1) * 128)
                        pair = 2 * (cnt["sc"] % 2)
                        cnt["sc"] += 1
                        for hh in range(4):
                            m = 2 * g + hh // 2
                            half = hh % 2
                            pr = slice(half * 64, (half + 1) * 64)
                            Ksrc = KT if half == g % 2 else KTs
                            bank = pair + half
                            col = (hh // 2) * 128
                            MM(PS[bank][:, col:col + 128], Ksrc[pr, g // 2, kcols], QT[qb][pr, m, qc], True, True,
                               [tKT[kt // 3], tQT[qb]], [tPS[bank]])
                        sb = cnt["sb"] % 2
                        cnt["sb"] += 1
                        TT(SB[sb][:, 0:256], PS[pair][:, 0:256], BIAS[:, g, kk, 0:256], ALU.add, [tPS[pair], tBIAS], [tSB[sb]])
                        TT(SB[sb][:, 256:512], PS[pair + 1][:, 0:256], BIAS[:, g, kk, 256:512], ALU.add, [tPS[pair + 1], tBIAS], [tSB[sb]])
                        pb = cnt["pb"] % 6
                        cnt["pb"] += 1
                        pbs.append(pb)
                        if lt == 1 and kk == 0:
                            ACT(PB[pb], SB[sb], AF.Exp, [tSB[sb], tCONST], [tPB[pb]], bias=SMALL[:, C_HB + s:C_HB + s + 1], scale=1.0)
                        else:
                            ACT(PB[pb], SB[sb], AF.Exp, [tSB[sb]], [tPB[pb]])
                    return pbs

                def l1_back(i, t3, g, pbs):
                    t = i * 3 + t3
                    pso = 4 + g % 2
                    oi = t % 2
                    for hh in range(4):
                        for kk in range(2):
                            kt = t - 1 + kk
                            pb = pbs[kk]
                            j = ORDR.index(hh)
                            MM(PS[pso][:, hh * 65:(hh + 1) * 65], PB[pb][:, j * 128:(j + 1) * 128], VP[:, kt, g, :], kk == 0, kk == 1,
                               [tPB[pb], tVP[kt], tvone], [tPS[pso]])
                    po3 = PS[pso][:, 0:260].rearrange("p (h d) -> p h d", h=4)
                    TT(den, po3[:, :, 64], esink[:, 4 * g:4 * g + 4], ALU.add, [tPS[pso], tBIAS], [tden], strict=True)
                    RECIP(den, den, [tden], [tden], strict=True)
                    for hh in range(4):
                        hd = 4 * g + hh
                        TS(OTs[oi][:, hd * 64:(hd + 1) * 64], PS[pso][:, hh * 65:hh * 65 + 64], den[:, hh:hh + 1], None, ALU.mult, None,
                           [tPS[pso], tden], [tOTs[oi]], strict=(hh == 0))
                    if g != 3:
                        return
                    for m in range(8):
                        sc.op("pe", lambda e, m=m, oi=oi: e.transpose(PST[:, m * 128:(m + 1) * 128], OTs[oi][:, m * 128:(m + 1) * 128], IDENT[:]),
                              reads=[tOTs[oi], tCONST], writes=[tPS[7]])
                    tcols = slice(t * 128, (t + 1) * 128)
                    ACOPY(H[:, :, tcols], PST.rearrange("p (m q) -> p m q", m=8), [tPS[7]], [tH[i]])

                items = [(i, t3, g) for i in range(NCH) for t3 in range(3) if (i % NCHS) * 3 + t3 != 0 for g in range(4)]
                pend = []
                qdone = set()

                def ensure_q(i):
                    if i not in qdone:
                        qdone.add(i)
                        qgen(i)

                for idx, (i, t3, g) in enumerate(items):
                    ensure_q(i)
                    pend.append((i, t3, g, l1_front(i, t3, g)))
                    if idx + 3 < len(items):
                        ensure_q(items[idx + 3][0])
                    if len(pend) > 2:
                        l1_back(*pend.pop(0))
                while pend:
                    l1_back(*pend.pop(0))
            sc.barrier()
            pool.release(m1)
            wo1 = pool.alloc([128, KC, 1024], BF16)
            two = Tile("wo1")
            DMA("pool", wo1, w_o1.rearrange("(m p) n -> p m n", p=128), "wo1", writes=[two])
            gb = pool.alloc([128, 8], F32)
            tgb = Tile("gb")
            TT(gb, GT(1, 0), SMALL[:, C_BO:C_BO + 8], ALU.mult, [tmod, tl], [tgb])
            for ch in chunk_list(True):
                gc, w = ch["gc"], ch["w"]
                hts = [tH[o] for o in ch["olds"]]
                for dm in range(KC):
                    pb = dm % 2
                    for m in range(8):
                        MM(PS[pb][:, 0:w], wo1[:, m, dm * 128:(dm + 1) * 128], H[:, m, gc], m == 0, m == 7, [two] + hts, [tPS[pb]])
                    STT(X[:, dm, gc], PS[pb][:, 0:w], GT(1, 0)[:, dm:dm + 1], X[:, dm, gc], ALU.mult, ALU.add, [tPS[pb], tmod], ch["xt"])
                    TS(X[:, dm, gc], X[:, dm, gc], gb[:, dm:dm + 1], None, ALU.add, None, [tgb], ch["xt"])
            sc.barrier()
            pool.release(m0)

        if swa_layer():
            return finish(nc, sc, es, dumps, X, outT, tX, None)
        dump("xa1", X[:], allX)
        if stop_after == "L1attn":
            return finish(nc, sc, es, dumps, X, outT, tX, None)
        mlp(1, chunk_list(True))
        dump("xm1", X[:], allX)

        def final_norm():
            sq = pool.alloc([128, KC, CH], BF16)
            rstd = pool.alloc([128, CH], F32)
            Y = [pool.alloc([128, KC, CH], F32) for _ in range(2)]
            tY = [Tile("Y0"), Tile("Y1")]
            tsq = Tile("sqf")
            trs = Tile("rsf")
            xo = outT.rearrange("(kc p) t -> p kc t", p=128)
            evs = []
            for i, ch in enumerate(chunk_list(True)):
                gc, w, s_ = ch["gc"], ch["w"], ch["s"]
                yb = i % 2
                norm_rstd(X[:, :, gc], KC, w, sq[:, :, 0:w], i % 2, rstd[:, 0:w], D, ch["xt"], tsq, trs)
                for kc in range(KC):
                    STT(Y[yb][:, kc, 0:w], X[:, kc, gc], SMALL[:, C_GFIN + kc:C_GFIN + kc + 1], rstd[:, 0:w], ALU.mult, ALU.mult,
                        ch["xt"] + [trs, tl], [tY[yb]])
                o0 = s_ * BLK + ch["off"] - HALO
                evs.append(DMA("sp", xo[:, :, o0:o0 + w], Y[yb][:, :, 0:w], "out", reads=[tY[yb]]))
            return evs

        final_norm()
        return finish(nc, sc, es, dumps, X, outT, tX, "done")


def finish(nc, sc, es, dumps, X, outT, tX, mode):
    if mode is None:
        xo = outT.rearrange("(kc p) t -> p kc t", p=128)
        for s in range(NSEG):
            src = X[:, :, s * SEG + HALO:(s + 1) * SEG]
            dst = xo[:, :, s * BLK:(s + 1) * BLK]
            sc.op("sp", lambda e, src=src, dst=dst: e.dma_start(out=dst, in_=src), reads=tX[s], dma_key="out")
    sc.op("sp", lambda e: None, extra=[sc.last_dma[k] for k in ("out", "dump") if k in sc.last_dma])
    emit(nc, sc, es)
    return nc, dumps


_LAST_SC = {}


def emit(nc, sc, es):
    _LAST_SC.clear()
    _LAST_SC.update({e: q for e, q in sc.q.items()})
    _LAST_SC['nwaits'] = [sum(len(w) for (_, w, _, _) in q) for q in sc.q.values()]
    E = es.enter_context
    signo = {}
    for eng in ENGS:
        n = 0
        for (fn, waits, ev, dk) in sc.q[eng]:
            if dk is None and ev.needed:
                n += 1
                signo[(eng, ev.idx)] = n
    esem = {eng: E(nc.semaphore("s_" + eng)) for eng in ENGS}
    dsem = {k: E(nc.semaphore("d_" + k)) for k in sc.dma_cnt}
    block = E(nc.Block())

    def replay(eng, e):
        for (fn, waits, ev, dk) in sc.q[eng]:
            for d in waits:
                if d.key.startswith("dma:"):
                    e.wait_ge(dsem[d.key[4:]], d.val)
                else:
                    e.wait_ge(esem[d.key], signo[(d.key, d.idx)])
            inst = fn(e)
            if inst is None:
                continue
            if dk is not None:
                inst.then_inc(dsem[dk], 16)
            elif ev.needed:
                inst.then_inc(esem[eng], 1)

    @block.tensor
    def _(e):
        replay("pe", e)

    @block.scalar
    def _(e):
        replay("act", e)

    @block.vector
    def _(e):
        replay("dve", e)

    @block.gpsimd
    def _(e):
        replay("pool", e)

    @block.sync
    def _(e):
        replay("sp", e)


def _core_layout(c):
    b, j = c // 4, c % 4
    blocks = [j, 7 - j]
    return b, blocks


def make_in_maps(inp):
    f32 = np.float32
    x = np.asarray(inp["x"], f32)
    pos = np.asarray(inp["positions"], np.int32)
    half = 32
    inv = (10000.0 ** (-np.arange(half, dtype=f32) / half)).astype(f32)
    invf = np.concatenate([inv, inv])[:, None].astype(f32)

    def featT(v):
        return np.ascontiguousarray(np.asarray(v, f32).reshape(-1, 128).T)

    w_uq = np.asarray(inp["mla_w_uq"][0], f32)
    uq = w_uq.reshape(384, 8, 192)
    w_uq_sw = np.ascontiguousarray(np.concatenate([uq[:, :, 160:192], uq[:, :, 128:160]], -1).reshape(384, 512))
    w_dkv = np.asarray(inp["mla_w_dkv"][0], f32)
    w_dkr_sw = np.ascontiguousarray(np.concatenate([w_dkv[:, 288:320], w_dkv[:, 256:288]], -1))
    w_qkv = np.asarray(inp["swa_w_qkv"][0], f32)
    b_qkv = np.asarray(inp["swa_b_qkv"][0], f32)
    wk = w_qkv[:, 1024:1280].reshape(1024, 2, 2, 64)
    w_k_sw = np.ascontiguousarray(wk[:, :, ::-1, :].reshape(1024, 256))
    bk = b_qkv[1024:1280].reshape(2, 2, 64)
    b_k_sw = np.ascontiguousarray(bk[:, ::-1, :].reshape(256))
    shared = {
        "w_ada": np.asarray(inp["w_ada"], f32),
        "b_adaT": np.ascontiguousarray(np.stack([featT(inp["b_ada"][l]) for l in range(2)])),
        "gmixT": np.ascontiguousarray(np.stack([featT(inp["g_mix"][l]) for l in range(2)])),
        "gmlpT": np.ascontiguousarray(np.stack([featT(inp["g_mlp"][l]) for l in range(2)])),
        "gfinT": featT(inp["g_final"]),
        "w_dq": np.asarray(inp["mla_w_dq"][0], f32),
        "g_qT": featT(inp["mla_g_q"][0]),
        "w_uq": w_uq,
        "w_uq_sw": w_uq_sw,
        "w_dkv": w_dkv,
        "w_dkr_sw": w_dkr_sw,
        "g_kvT": featT(inp["mla_g_kv"][0]),
        "w_ukv": np.asarray(inp["mla_w_ukv"][0], f32),
        "w_o": np.asarray(inp["mla_w_o"][0], f32),
        "invf": invf,
        "w_qkv": w_qkv,
        "w_k_sw": w_k_sw,
        "b_qkvT": featT(b_qkv),
        "b_k_swT": featT(b_k_sw),
        "b_v": np.ascontiguousarray(b_qkv[None, 1280:1536]),
        "sinks": np.asarray(inp["swa_sinks"], f32).reshape(1, 16),
        "w_o1": np.asarray(inp["swa_w_o"][0], f32),
        "b_oT": featT(inp["swa_b_o"][0]),
        "w_ff1": np.asarray(inp["w_ff1"], f32),
        "w_ff2": np.asarray(inp["w_ff2"], f32),
    }
    xT_b = [np.ascontiguousarray(x[b].T) for b in range(2)]
    maps = []
    for c in range(8):
        b, blocks = _core_layout(c)
        xt = np.zeros((D, T0), f32)
        pq = np.zeros((1, T0), np.int32)
        meta = np.zeros((1, 4), f32)
        for s, blk in enumerate(blocks):
            lo = blk * BLK - HALO
            hi = (blk + 1) * BLK
            lo_c = max(lo, 0)
            off = s * SEG + (lo_c - lo)
            xt[:, off:s * SEG + SEG] = xT_b[b][:, lo_c:hi]
            pq[0, off:s * SEG + SEG] = pos[b, lo_c:hi]
            meta[0, s] = lo
            meta[0, 2 + s] = 1.0 if lo >= 0 else 0.0
        m = dict(shared)
        m.update({
            "xT": xt, "xTall": xT_b[b], "posq": pq, "posall": np.ascontiguousarray(pos[b][None, :]),
            "segmeta": meta, "cT": featT(inp["c"][b]),
        })
        maps.append(m)
    return maps


def assemble(results):
    out = np.zeros((2, S, D), np.float32)
    for c in range(8):
        b, blocks = _core_layout(c)
        o = results[c]["outT"]
        for s, blk in enumerate(blocks):
            out[b, blk * BLK:(blk + 1) * BLK, :] = o[:, s * BLK:(s + 1) * BLK].T
    return out


_CACHE = {}


def kernel(**inputs):
    maps = make_in_maps(inputs)
    if "nc" not in _CACHE:
        _CACHE["nc"] = build_program()[0]
    res = run_bass_kernel_spmd(_CACHE["nc"], maps, core_ids=list(range(8)))
    return assemble(res.results)
```

```python
import math
import numpy as np
from contextlib import ExitStack
import concourse.bass as bass
import concourse.mybir as mybir
from concourse.bass_utils import run_bass_kernel_spmd

F32 = mybir.dt.float32
BF16 = mybir.dt.bfloat16
F16 = mybir.dt.float16
I32 = mybir.dt.int32
AF = mybir.ActivationFunctionType
ALU = mybir.AluOpType

D = 1024
KC = 8
S = 8192
NSEG = 2
BLK = 1024
HALO = 128
SEG = BLK + HALO
T0 = NSEG * SEG
CH = 384
NCHS = SEG // CH
NCH = T0 // CH
KCHUNK = 512
NKC = S // KCHUNK
EPS = 1e-6
MLA_SCALE = 192 ** -0.5
SWA_SCALE = 64 ** -0.5
TWO_PI = 2.0 * math.pi
CW_HI = 6.28125
CW_LO = TWO_PI - 6.28125
PI_SAFE = 3.1415925
NEG_BIG = -30000.0
POOL_KIB = 128


class Ev:
    __slots__ = ("key", "val", "clock", "needed", "eng", "idx")

    def __init__(self, key, val, eng, idx):
        self.key = key
        self.val = val
        self.eng = eng
        self.idx = idx
        self.clock = None
        self.needed = False


class Tile:
    __slots__ = ("name", "w", "r")

    def __init__(self, name=""):
        self.name = name
        self.w = None
        self.r = {}


ENGS = ["pe", "act", "dve", "pool", "sp"]


class Sched:
    def __init__(self):
        self.q = {e: [] for e in ENGS}
        self.seen = {e: {} for e in ENGS}
        self.cnt = {e: 0 for e in ENGS}
        self.dma_cnt = {}
        self.last = {e: None for e in ENGS}
        self.last_dma = {}
        self.bar = []

    def op(self, eng, fn, reads=(), writes=(), dma_key=None, extra=(), strict=False):
        deps = []
        for t in reads:
            if t.w is not None:
                deps.append(t.w)
        for t in writes:
            if t.w is not None:
                deps.append(t.w)
            deps.extend(t.r.values())
        deps.extend(extra)
        deps.extend(self.bar)
        seen = self.seen[eng]
        waits = {}
        for d in deps:
            if d.key == eng and not strict:
                continue
            if seen.get(d.key, -1) >= d.val:
                continue
            for k, v in d.clock.items():
                if seen.get(k, -1) < v:
                    seen[k] = v
            if seen.get(d.key, -1) < d.val:
                seen[d.key] = d.val
            d.needed = True
            cur = waits.get(d.key)
            if cur is None or cur.val < d.val:
                waits[d.key] = d
        idx = self.cnt[eng]
        self.cnt[eng] += 1
        if dma_key is None:
            ev = Ev(eng, idx, eng, idx)
        else:
            v = self.dma_cnt.get(dma_key, 0) + 16
            self.dma_cnt[dma_key] = v
            ev = Ev("dma:" + dma_key, v, eng, idx)
            ev.needed = True
            self.last_dma[dma_key] = ev
        clock = dict(seen)
        clock[eng] = idx
        if dma_key is not None:
            clock[eng] = idx - 1
        ev.clock = clock
        self.q[eng].append((fn, list(waits.values()), ev, dma_key))
        self.last[eng] = ev
        for t in writes:
            t.w = ev
            t.r = {}
        for t in reads:
            t.r[ev.key] = ev
        return ev

    def barrier(self):
        evs = [e for e in self.last.values() if e is not None and not e.key.startswith("dma:")]
        evs += list(self.last_dma.values())
        self.bar = evs


class PoolAlloc:
    def __init__(self, ap, nbytes):
        self.ap = ap
        self.nbytes = nbytes
        self.off = 0

    def alloc(self, shape, dtype):
        esz = 4 if dtype in (F32, I32) else 2
        n = 1
        for s in shape[1:]:
            n *= s
        nb = (n * esz + 63) // 64 * 64
        assert self.off + nb <= self.nbytes, f"pool overflow {self.off + nb} > {self.nbytes}"
        a = self.ap[:, self.off // 4:(self.off + nb) // 4]
        self.off += nb
        if dtype != F32:
            a = a.bitcast(dtype)
        a = a[:, 0:n]
        if len(shape) == 3:
            a = a.rearrange("p (a b) -> p a b", a=shape[1])
        elif len(shape) == 4:
            a = a.rearrange("p (a b c) -> p a b c", a=shape[1], b=shape[2])
        if shape[0] < 128:
            a = a[0:shape[0]]
        return a

    def mark(self):
        return self.off

    def release(self, m):
        self.off = m


def build_program(stop_after="all", debug=False):
    nc = bass.Bass("TRN2", target_bir_lowering=False)
    sc = Sched()
    dumps = []

    def din(name, shape, dt=F32):
        return nc.dram_tensor(name, list(shape), dt, kind="ExternalInput").ap()

    xT = din("xT", [D, T0])
    xTall = din("xTall", [D, S])
    posq = din("posq", [1, T0], I32)
    posall = din("posall", [1, S], I32)
    segmeta = din("segmeta", [1, 4])
    cT = din("cT", [128, KC])
    w_ada = din("w_ada", [2, D, 6 * D])
    b_adaT = din("b_adaT", [2, 128, 48])
    gmixT = din("gmixT", [2, 128, KC])
    gmlpT = din("gmlpT", [2, 128, KC])
    gfinT = din("gfinT", [128, KC])
    w_dq = din("w_dq", [D, 384])
    g_qT = din("g_qT", [128, 3])
    w_uq = din("w_uq", [384, 1536])
    w_uq_sw = din("w_uq_sw", [384, 512])
    w_dkv = din("w_dkv", [D, 320])
    w_dkr_sw = din("w_dkr_sw", [D, 64])
    g_kvT = din("g_kvT", [128, 2])
    w_ukv = din("w_ukv", [256, 2048])
    w_o = din("w_o", [D, D])
    invf = din("invf", [64, 1])
    w_qkv = din("w_qkv", [D, 1536])
    w_k_sw = din("w_k_sw", [D, 256])
    b_qkvT = din("b_qkvT", [128, 12])
    b_k_swT = din("b_k_swT", [128, 2])
    b_v = din("b_v", [1, 256])
    sinks = din("sinks", [1, 16])
    w_o1 = din("w_o1", [D, D])
    b_oT = din("b_oT", [128, KC])
    w_ff1 = din("w_ff1", [2, D, 4 * D])
    w_ff2 = din("w_ff2", [2, 4 * D, D])
    outT = nc.dram_tensor("outT", [D, NSEG * BLK], F32, kind="ExternalOutput").ap()

    es = ExitStack()
    with es:
        E = es.enter_context
        X = E(nc.sbuf_tensor("X", [128, KC, T0], F32))
        POOLT = E(nc.sbuf_tensor("POOLT", [128, POOL_KIB * 256], F32))
        CONST = E(nc.sbuf_tensor("CONST", [128, 16], F32))
        MODS = E(nc.sbuf_tensor("MODS", [128, 2, 48], F32))
        AB = E(nc.sbuf_tensor("AB", [128, 2, 2, KC], F32))
        SMALL = E(nc.sbuf_tensor("SMALL", [128, 64], F32))
        ONESB = E(nc.sbuf_tensor("ONESB", [128, 128], BF16))
        IDENT = E(nc.sbuf_tensor("IDENT", [128, 128], BF16))
        QLOC = E(nc.sbuf_tensor("QLOC", [128, SEG], F16))
        KK = E(nc.sbuf_tensor("KK", [128, NSEG, 64], F32))
        META = E(nc.sbuf_tensor("META", [128, 4], F32))
        BM = E(nc.sbuf_tensor("BM", [128, NSEG, NCHS, 64], F32))
        CONDB = E(nc.sbuf_tensor("CONDB", [128, KC], BF16))
        PS = [E(nc.psum_tensor(f"PS{i}", [128, 512], F32)) for i in range(8)]
        pool = PoolAlloc(POOLT, POOL_KIB * 1024)

        tX = [[Tile(f"X{s}_{c}") for c in range(NCHS)] for s in range(NSEG)]
        tPS = [Tile(f"PS{i}") for i in range(8)]
        tCONST = Tile("const")

        C_GQ = 0
        C_GKV = 3
        C_INVF = 5
        C_SGN = 6
        C_HB = 7
        C_GFIN = 16
        C_BO = 24
        C_GB = 32
        C_BQ = 40
        C_BKSW = 52

        def dump(name, ap, tiles, dt=F32):
            if not debug:
                return
            shape = list(ap.shape)
            dr = nc.dram_tensor("dbg_" + name, shape, dt, kind="ExternalOutput").ap()
            dumps.append("dbg_" + name)
            sc.op("sp", lambda e, dr=dr, ap=ap: e.dma_start(out=dr, in_=ap), reads=tiles, dma_key="dump")


        def MM(out, lhsT, rhs, start, stop, reads, writes):
            return sc.op("pe", lambda e: e.matmul(out, lhsT, rhs, start=start, stop=stop), reads=reads, writes=writes)

        def ACT(out, in_, func, reads, writes, bias=None, scale=None):
            kw = {}
            if bias is not None:
                kw["bias"] = bias
            if scale is not None:
                kw["scale"] = scale
            return sc.op("act", lambda e: e.activation(out=out, in_=in_, func=func, **kw), reads=reads, writes=writes)

        def AMUL(out, in_, c, reads, writes):
            return sc.op("act", lambda e: e.mul(out, in_, c), reads=reads, writes=writes)

        def ACOPY(out, in_, reads, writes):
            return sc.op("act", lambda e: e.copy(out, in_), reads=reads, writes=writes)

        def TT(out, in0, in1, op, reads, writes, strict=False):
            return sc.op("dve", lambda e: e.tensor_tensor(out, in0, in1, op), reads=reads, writes=writes, strict=strict)

        def TS(out, in0, s1, s2, op0, op1, reads, writes, strict=False):
            if op1 is None:
                return sc.op("dve", lambda e: e.tensor_scalar(out, in0, s1, s2, op0), reads=reads, writes=writes, strict=strict)
            return sc.op("dve", lambda e: e.tensor_scalar(out, in0, s1, s2, op0, op1), reads=reads, writes=writes, strict=strict)

        def STT(out, in0, scalar, in1, op0, op1, reads, writes, strict=False):
            return sc.op("dve", lambda e: e.scalar_tensor_tensor(out, in0, scalar, in1, op0, op1), reads=reads, writes=writes, strict=strict)

        def RECIP(out, in_, reads, writes, strict=False):
            return sc.op("dve", lambda e: e.reciprocal(out, in_), reads=reads, writes=writes, strict=strict)

        def VCOPY(out, in_, reads, writes):
            return sc.op("dve", lambda e: e.tensor_copy(out, in_), reads=reads, writes=writes)

        def DMA(q, out, in_, key, reads=(), writes=()):
            return sc.op(q, lambda e: e.dma_start(out=out, in_=in_), reads=reads, writes=writes, dma_key=key)

        tl = Tile("smallloads")

        def setup():
            P = "pool"
            sc.op(P, lambda e: e.memset(ONESB[:], 1.0), writes=[tCONST])
            sc.op(P, lambda e: e.memset(CONST[:, 0:1], EPS), writes=[tCONST])
            sc.op(P, lambda e: e.memset(CONST[:, 1:2], 0.0), writes=[tCONST])
            sc.op(P, lambda e: e.memset(CONST[:, 2:3], 1e-18), writes=[tCONST])
            sc.op(P, lambda e: e.memset(SMALL[0:32, C_SGN:C_SGN + 1], -1.0), writes=[tCONST])
            sc.op(P, lambda e: e.memset(SMALL[32:64, C_SGN:C_SGN + 1], 1.0), writes=[tCONST])
            tmp = pool.alloc([128, 128], F32)
            sc.op(P, lambda e: e.iota(tmp, pattern=[[1, 128]], base=0, channel_multiplier=-1,
                                      allow_small_or_imprecise_dtypes=True), writes=[tCONST])
            sc.op(P, lambda e: e.tensor_single_scalar(IDENT[:], tmp, 0.0, ALU.is_equal), writes=[tCONST])
            sc.op(P, lambda e: e.iota(QLOC[:], pattern=[[1, SEG]], base=0, channel_multiplier=0,
                                      allow_small_or_imprecise_dtypes=True), writes=[tCONST])
            kid = pool.alloc([128, 64], F32)
            sc.op(P, lambda e: e.iota(kid, pattern=[[128, 64]], base=0, channel_multiplier=1,
                                      allow_small_or_imprecise_dtypes=True), writes=[tCONST])
            loads = [
                (META[:], segmeta.partition_broadcast(128)),
                (SMALL[:, C_GQ:C_GQ + 3], g_qT),
                (SMALL[:, C_GKV:C_GKV + 2], g_kvT),
                (SMALL[0:64, C_INVF:C_INVF + 1], invf),
                (SMALL[:, C_GFIN:C_GFIN + 8], gfinT),
                (SMALL[:, C_BO:C_BO + 8], b_oT),
                (SMALL[:, C_BQ:C_BQ + 12], b_qkvT),
                (SMALL[:, C_BKSW:C_BKSW + 2], b_k_swT),
            ]
            for o, i in loads:
                DMA("sp", o, i, "c0", writes=[tl])
            kid0 = pool.alloc([128, 64], F32)
            kb = pool.alloc([128, 64], F32)
            sc.op(P, lambda e: e.iota(kid0, pattern=[[128, 64]], base=0, channel_multiplier=0,
                                      allow_small_or_imprecise_dtypes=True), writes=[tCONST])
            for s in range(NSEG):
                TS(KK[:, s, :], kid, META[:, s:s + 1], None, ALU.subtract, None, [tl, tCONST], [tCONST])
                TS(kb, kid0, META[:, s:s + 1], None, ALU.subtract, None, [tl, tCONST], [tCONST])
                for c in range(NCHS):
                    TS(BM[:, s, c, :], kb, 384.0 * c + 383.0, NEG_BIG, ALU.is_gt, ALU.mult, [], [tCONST])
            TS(SMALL[:, C_HB:C_HB + 2], META[:, 2:4], -1.0, -NEG_BIG, ALU.add, ALU.mult, [tl], [tCONST])

        setup()

        tmod = Tile("mods")

        def phase_A():
            m0 = pool.mark()
            cf = pool.alloc([128, KC], F32)
            tc_ = Tile("c")
            DMA("sp", cf, cT, "c1", writes=[tc_])
            ACT(CONDB[:], cf, AF.Silu, [tc_], [tCONST])
            badd = pool.alloc([128, 2, 48], F32)
            gm = pool.alloc([128, 2, 2, KC], F32)
            tb = Tile("badd")
            DMA("sp", badd, b_adaT.rearrange("l p n -> p l n"), "c1", writes=[tb])
            DMA("sp", gm[:, :, 0, :], gmixT.rearrange("l p n -> p l n"), "c1", writes=[tb])
            DMA("sp", gm[:, :, 1, :], gmlpT.rearrange("l p n -> p l n"), "c1", writes=[tb])
            wa = [pool.alloc([128, KC, 768], BF16) for _ in range(2)]
            twa = [Tile("wa0"), Tile("wa1")]
            n = 0
            for l in range(2):
                wsrc = w_ada[l].rearrange("(kc p) n -> p kc n", p=128)
                for cc in range(8):
                    b = n % 2
                    n += 1
                    DMA("pool", wa[b], wsrc[:, :, cc * 768:(cc + 1) * 768], f"wa{b}", writes=[twa[b]])
                    for nn in range(6):
                        col = l * 48 + cc * 6 + nn
                        for kc in range(KC):
                            MM(PS[0][:, col:col + 1], wa[b][:, kc, nn * 128:(nn + 1) * 128], CONDB[:, kc:kc + 1],
                               kc == 0, kc == KC - 1, [twa[b], tCONST], [tPS[0]])
                TT(MODS[:, l, :], PS[0][:, l * 48:(l + 1) * 48], badd[:, l, :], ALU.add, [tPS[0], tb], [tmod])
                STT(AB[:, l, 0, :], MODS[:, l, 8:16], 1.0, gm[:, l, 0, :], ALU.add, ALU.mult, [tb, tmod], [tmod], strict=True)
                STT(AB[:, l, 1, :], MODS[:, l, 32:40], 1.0, gm[:, l, 1, :], ALU.add, ALU.mult, [tb], [tmod])
            dump("mods", MODS[:], [tmod])
            sc.barrier()
            pool.release(m0)

        phase_A()

        def SH(l, which):
            return MODS[:, l, 0:8] if which == 0 else MODS[:, l, 24:32]

        def GT(l, which):
            return MODS[:, l, 16:24] if which == 0 else MODS[:, l, 40:48]

        def norm_rstd(src3, nk, n, sqbuf, ps_i, rstd, dim, reads, tsq, trstd):
            ACT(sqbuf, src3, AF.Square, reads, [tsq])
            for k in range(nk):
                MM(PS[ps_i][:, 0:n], ONESB[:], sqbuf[:, k, :], k == 0, k == nk - 1, [tsq, tCONST], [tPS[ps_i]])
            ACT(rstd, PS[ps_i][:, 0:n], AF.Ln, [tPS[ps_i]], [trstd], bias=CONST[:, 0:1], scale=1.0 / dim)
            ACT(rstd, rstd, AF.Exp, [], [trstd], scale=-0.5)

        def rope_tables(pos_ap, ang, nq, r, cos2, sin2, tpos, ttab, tt):
            inv = SMALL[0:64, C_INVF:C_INVF + 1]
            sgn = SMALL[0:64, C_SGN:C_SGN + 1]
            VCOPY(ang, pos_ap, [tpos], [tt])
            TS(ang, ang, inv, None, ALU.mult, None, [tl], [tt])
            for which in range(2):
                if which == 1:
                    TS(ang, ang, math.pi / 2, None, ALU.add, None, [], [tt])
                TS(r, ang, 1.0 / TWO_PI, None, ALU.mult, None, [], [tt])
                VCOPY(nq, r, [], [tt])
                STT(r, nq, -CW_HI, ang, ALU.mult, ALU.add, [], [tt])
                STT(r, nq, -CW_LO, r, ALU.mult, ALU.add, [], [tt])
                TS(r, r, -PI_SAFE, PI_SAFE, ALU.max, ALU.min, [], [tt])
                if which == 0:
                    ACT(sin2, r, AF.Sin, [tt, tCONST], [ttab], scale=sgn)
                else:
                    ACT(cos2, r, AF.Sin, [tt], [ttab])

        LAT = pool.alloc([128, 2, S], BF16)
        KR = pool.alloc([128, S], BF16)
        tKRz = Tile("krz")
        sc.op("pool", lambda e: e.memset(KR[64:128, :], 0.0), writes=[tKRz])
        tLAT = [Tile(f"lat{i}") for i in range(NKC)]
        tKR = [Tile(f"kr{i}") for i in range(NKC)]
        m_attn = pool.mark()

        def phase_B():
            wdkv = pool.alloc([128, KC, 384], BF16)
            twd = Tile("wdkv")
            twd2 = Tile("wdkv2")
            DMA("pool", wdkv[:, :, 0:320], w_dkv.rearrange("(kc p) n -> p kc n", p=128), "wdkv", writes=[twd])
            DMA("pool", wdkv[:, :, 320:384], w_dkr_sw.rearrange("(kc p) n -> p kc n", p=128), "wdkv2", writes=[twd2])
            xa = [pool.alloc([128, KC, KCHUNK], F32) for _ in range(2)]
            hb = [pool.alloc([128, KC, KCHUNK], BF16) for _ in range(2)]
            posb = [pool.alloc([64, KCHUNK], I32) for _ in range(2)]
            rstd = pool.alloc([128, KCHUNK], F32)
            rstd2 = pool.alloc([128, KCHUNK], F32)
            sqc = pool.alloc([128, 2, KCHUNK], BF16)
            ang = pool.alloc([64, KCHUNK], F32)
            nq = pool.alloc([64, KCHUNK], I32)
            rr = pool.alloc([64, KCHUNK], F32)
            cos2 = pool.alloc([64, KCHUNK], F32)
            sin2 = pool.alloc([64, KCHUNK], F32)
            ku = pool.alloc([64, KCHUNK], F32)
            kv = pool.alloc([64, KCHUNK], F32)
            txa = [Tile("xa0"), Tile("xa1")]
            th = [Tile("h0"), Tile("h1")]
            tpos = [Tile("pos0"), Tile("pos1")]
            trs = Tile("rstd")
            trs2 = Tile("rstd2")
            tsqc = Tile("sqc")
            ttab = Tile("tab")
            ttmp = Tile("ropetmp")
            tku = Tile("ku")
            xsrc = xTall.rearrange("(kc p) t -> p kc t", p=128)

            txak = [[Tile(f"xa{b}_{k}") for k in range(KC)] for b in range(2)]

            def A1(i):
                b = i % 2
                cols = slice(i * KCHUNK, (i + 1) * KCHUNK)
                DMA("sp", xa[b], xsrc[:, :, cols], f"xa{b}", writes=[txa[b]] + txak[b])
                DMA("sp", posb[b], posall[:, cols].partition_broadcast(64), f"pos{b}", writes=[tpos[b]])
                norm_rstd(xa[b], KC, KCHUNK, hb[b], b, rstd, D, [txa[b]], th[b], trs)
                for kc in range(KC):
                    TT(xa[b][:, kc, :], xa[b][:, kc, :], rstd, ALU.mult, [trs, txa[b]], [txak[b][kc]])

            def A2(i):
                b = i % 2
                for kc in range(KC):
                    ACT(hb[b][:, kc, :], xa[b][:, kc, :], AF.Identity, [txak[b][kc], tmod], [th[b]],
                        bias=SH(0, 0)[:, kc:kc + 1], scale=AB[:, 0, 0, kc:kc + 1])

            def B1(i):
                b = i % 2
                cols = slice(i * KCHUNK, (i + 1) * KCHUNK)
                for (pi, c0, c1, m) in [(2, 0, 128, 128), (3, 128, 256, 128), (4, 256, 320, 64), (5, 320, 384, 64)]:
                    for kc in range(KC):
                        MM(PS[pi][0:m, 0:KCHUNK], wdkv[:, kc, c0:c1], hb[b][:, kc, :], kc == 0, kc == KC - 1,
                           [th[b], twd, twd2], [tPS[pi]])
                ACT(sqc[:, 0, :], PS[2][:, 0:KCHUNK], AF.Square, [tPS[2]], [tsqc])
                ACT(sqc[:, 1, :], PS[3][:, 0:KCHUNK], AF.Square, [tPS[3]], [tsqc])
                for k in range(2):
                    MM(PS[6][:, 0:KCHUNK], ONESB[:], sqc[:, k, :], k == 0, k == 1, [tsqc, tCONST], [tPS[6]])
                ACT(rstd2, PS[6][:, 0:KCHUNK], AF.Ln, [tPS[6]], [trs2], bias=CONST[:, 0:1], scale=1.0 / 256)
                ACT(rstd2, rstd2, AF.Exp, [], [trs2], scale=-0.5)
                for k in range(2):
                    STT(LAT[:, k, cols], PS[2 + k][:, 0:KCHUNK], SMALL[:, C_GKV + k:C_GKV + k + 1], rstd2, ALU.mult, ALU.mult,
                        [tPS[2 + k], trs2, tl], [tLAT[i]])

            def B2(i):
                cols = slice(i * KCHUNK, (i + 1) * KCHUNK)
                TT(ku, PS[4][0:64, 0:KCHUNK], cos2, ALU.mult, [tPS[4], ttab], [tku])
                TT(kv, PS[5][0:64, 0:KCHUNK], sin2, ALU.mult, [tPS[5], ttab], [tku])
                TT(KR[0:64, cols], ku, kv, ALU.add, [tku], [tKR[i]])

            A1(0)
            A2(0)
            rope_tables(posb[0], ang, nq, rr, cos2, sin2, tpos[0], ttab, ttmp)
            for i in range(NKC):
                if i + 1 < NKC:
                    A1(i + 1)
                B1(i)
                if i + 1 < NKC:
                    A2(i + 1)
                B2(i)
                if i + 1 < NKC:
                    rope_tables(posb[(i + 1) % 2], ang, nq, rr, cos2, sin2, tpos[(i + 1) % 2], ttab, ttmp)
            dump("lat", LAT, tLAT, BF16)
            dump("kr", KR[0:64, :], tKR, BF16)
            sc.barrier()

        xsrc_own = xT.rearrange("(kc p) t -> p kc t", p=128)
        for s in range(NSEG):
            for c in range(NCHS):
                cols = slice(s * SEG + c * CH, s * SEG + (c + 1) * CH)
                DMA("sp", X[:, :, cols], xsrc_own[:, :, cols], f"x{s}{c}", writes=[tX[s][c]])
        phase_B()
        pool.release(m_attn)
        if stop_after == "B":
            return finish(nc, sc, es, dumps, X, outT, tX, None)

        def attn_segment(s):
            m0 = pool.mark()
            NK = 4096 if s == 0 else 8192
            cqn = pool.alloc([128, 3, SEG], BF16)
            cos2q = pool.alloc([64, SEG], F32)
            sin2q = pool.alloc([64, SEG], F32)
            qn = pool.alloc([128, SEG], BF16)
            qr = pool.alloc([128, SEG], BF16)
            tqrz = Tile("qrz")
            sc.op("pool", lambda e: e.memset(qr[64:128, :], 0.0), writes=[tqrz])
            wh = [pool.alloc([128, 2304], BF16) for _ in range(2)]
            pbuf = [pool.alloc([128, CH], BF16) for _ in range(4)]
            obuf = [pool.alloc([128, CH], BF16) for _ in range(2)]
            rden = pool.alloc([128, CH], F32)
            tcqn = [Tile(f"cqn{c}") for c in range(NCHS)]
            ttabq = [Tile(f"tabq{c}") for c in range(NCHS)]
            tqn = [Tile(f"qn{c}") for c in range(NCHS)]
            tqr = [Tile(f"qr{c}") for c in range(NCHS)]
            twh = [[Tile(f"wh{b}_{k}") for k in range(4)] for b in range(2)]
            tp = [Tile(f"p{i}") for i in range(4)]
            tob = [Tile("ob0"), Tile("ob1")]
            trden = Tile("rden")
            m1 = pool.mark()
            wdq = pool.alloc([128, KC, 384], BF16)
            twdq = Tile("wdq")
            DMA("pool", wdq, w_dq.rearrange("(kc p) n -> p kc n", p=128), "wdq", writes=[twdq])
            hq = pool.alloc([128, KC, CH], BF16)
            tt = pool.alloc([128, KC, CH], F32)
            rstd = pool.alloc([128, CH], F32)
            rstdq = pool.alloc([128, CH], F32)
            sqq = pool.alloc([128, 3, CH], BF16)
            posb = pool.alloc([64, CH], I32)
            ang = pool.alloc([64, CH], F32)
            nq_ = pool.alloc([64, CH], I32)
            rr = pool.alloc([64, CH], F32)
            th = Tile("hq")
            tttk = [Tile(f"ttq{k}") for k in range(KC)]
            trs = Tile("rs")
            trsq = Tile("rsq")
            tsqq = Tile("sqq")
            tpos = Tile("posq")
            ttmp = Tile("ropetmpq")
            for c in range(NCHS):
                lc = slice(c * CH, (c + 1) * CH)
                gc = slice(s * SEG + c * CH, s * SEG + (c + 1) * CH)
                DMA("sp", posb, posq[:, gc].partition_broadcast(64), "posq", writes=[tpos])
                norm_rstd(X[:, :, gc], KC, CH, hq, 0, rstd, D, [tX[s][c]], th, trs)
                for kc in range(KC):
                    TT(tt[:, kc, :], X[:, kc, gc], rstd, ALU.mult, [trs, tX[s][c]], [tttk[kc]])
                    ACT(hq[:, kc, :], tt[:, kc, :], AF.Identity, [tttk[kc], tmod], [th],
                        bias=SH(0, 0)[:, kc:kc + 1], scale=AB[:, 0, 0, kc:kc + 1])
                for m in range(3):
                    for kc in range(KC):
                        MM(PS[1 + m][:, 0:CH], wdq[:, kc, m * 128:(m + 1) * 128], hq[:, kc, :], kc == 0, kc == KC - 1,
                           [th, twdq], [tPS[1 + m]])
                for m in range(3):
                    ACT(sqq[:, m, :], PS[1 + m][:, 0:CH], AF.Square, [tPS[1 + m]], [tsqq])
                for m in range(3):
                    MM(PS[4][:, 0:CH], ONESB[:], sqq[:, m, :], m == 0, m == 2, [tsqq, tCONST], [tPS[4]])
                ACT(rstdq, PS[4][:, 0:CH], AF.Ln, [tPS[4]], [trsq], bias=CONST[:, 0:1], scale=1.0 / 384)
                ACT(rstdq, rstdq, AF.Exp, [], [trsq], scale=-0.5)
                for m in range(3):
                    STT(cqn[:, m, lc], PS[1 + m][:, 0:CH], SMALL[:, C_GQ + m:C_GQ + m + 1], rstdq, ALU.mult, ALU.mult,
                        [tPS[1 + m], trsq, tl], [tcqn[c]])
                rope_tables(posb, ang, nq_, rr, cos2q[:, lc], sin2q[:, lc], tpos, ttabq[c], ttmp)
            if s == 0:
                dump("cqn0", cqn, tcqn, BF16)
            sc.barrier()
            pool.release(m1)
            kh = pool.alloc([128, NK], BF16)
            vh = pool.alloc([128, NK // 128, 128], BF16)
            u1 = pool.alloc([64, CH], F32)
            u2 = pool.alloc([64, CH], F32)
            NKCH = NK // 512
            tkh = [Tile(f"kh{i}") for i in range(NKCH)]
            tvh = [Tile(f"vh{i}") for i in range(NKCH)]
            tu = Tile("u")
            uq_src = w_uq.rearrange("(m p) n -> p m n", p=128)
            uqs_src = w_uq_sw.rearrange("(m p) n -> p m n", p=128)
            ukv_src = w_ukv.rearrange("(m p) n -> p m n", p=128)

            def wviews(b):
                w = wh[b]
                return (w[:, 0:576].rearrange("p (m n) -> p m n", m=3), w[:, 576:768].rearrange("p (m n) -> p m n", m=3),
                        w[:, 768:1280].rearrange("p (m n) -> p m n", m=2), w[:, 1280:2304])

            def load_wh(h):
                b = h % 2
                wuq, wuqs, wukv, wo_h = wviews(b)
                DMA("pool", wuq, uq_src[:, :, h * 192:(h + 1) * 192], f"wh{b}0", writes=[twh[b][0]])
                DMA("pool", wuqs, uqs_src[:, :, h * 64:(h + 1) * 64], f"wh{b}1", writes=[twh[b][1]])
                DMA("pool", wukv, ukv_src[:, :, h * 256:(h + 1) * 256], f"wh{b}2", writes=[twh[b][2]])
                DMA("pool", wo_h, w_o[h * 128:(h + 1) * 128, :], f"wh{b}3", writes=[twh[b][3]])

            load_wh(0)
            cnt = {"p": 0, "o": 0, "s": 0}

            def head(h):
                b = h % 2
                wuq, wuqs, wukv, wo_h = wviews(b)
                for c in range(NCHS):
                    lc = slice(c * CH, (c + 1) * CH)
                    for m in range(3):
                        MM(PS[6][:, 0:CH], wuq[:, m, 0:128], cqn[:, m, lc], m == 0, m == 2, [twh[b][0], tcqn[c]], [tPS[6]])
                    AMUL(qn[:, lc], PS[6][:, 0:CH], MLA_SCALE, [tPS[6]], [tqn[c]])
                    for m in range(3):
                        MM(PS[7][0:64, 0:CH], wuq[:, m, 128:192], cqn[:, m, lc], m == 0, m == 2, [twh[b][0], tcqn[c]], [tPS[7]])
                    STT(u1, PS[7][0:64, 0:CH], MLA_SCALE, cos2q[:, lc], ALU.mult, ALU.mult, [tPS[7], ttabq[c]], [tu])
                    for m in range(3):
                        MM(PS[7][0:64, 0:CH], wuqs[:, m, :], cqn[:, m, lc], m == 0, m == 2, [twh[b][1], tcqn[c]], [tPS[7]])
                    STT(u2, PS[7][0:64, 0:CH], MLA_SCALE, sin2q[:, lc], ALU.mult, ALU.mult, [tPS[7], ttabq[c]], [tu])
                    TT(qr[0:64, lc], u1, u2, ALU.add, [tu], [tqr[c]])
                for i in range(NKCH):
                    cols = slice(i * 512, (i + 1) * 512)
                    bk = 6 if i % 2 == 0 else 0
                    bv_ = 7 if i % 2 == 0 else 1
                    for m in range(2):
                        MM(PS[bk][:, 0:512], wukv[:, m, 0:128], LAT[:, m, cols], m == 0, m == 1, [twh[b][2], tLAT[i]], [tPS[bk]])
                    if i % 2 == 0:
                        VCOPY(kh[:, cols], PS[bk][:, 0:512], [tPS[bk]], [tkh[i]])
                    else:
                        ACOPY(kh[:, cols], PS[bk][:, 0:512], [tPS[bk]], [tkh[i]])
                    for t in range(4):
                        kt = i * 4 + t
                        for m in range(2):
                            MM(PS[bv_][:, t * 128:(t + 1) * 128], LAT[:, m, kt * 128:(kt + 1) * 128], wukv[:, m, 128:256],
                               m == 0, m == 1, [twh[b][2], tLAT[i]], [tPS[bv_]])
                    if i % 2 == 0:
                        ACOPY(vh[:, i * 4:(i + 1) * 4, :], PS[bv_][:, 0:512].rearrange("p (a b) -> p a b", a=4), [tPS[bv_]], [tvh[i]])
                    else:
                        VCOPY(vh[:, i * 4:(i + 1) * 4, :], PS[bv_][:, 0:512].rearrange("p (a b) -> p a b", a=4), [tPS[bv_]], [tvh[i]])
                info = []
                tiles = []
                for c in range(NCHS):
                    if s == 0:
                        nkt = 26 + 3 * c
                        full_upto = 3 * c - 2
                    else:
                        nkt = 58 + 3 * c
                        full_upto = 30 + 3 * c
                    nkt = min(nkt, NK // 128)
                    ob = cnt["o"] % 2
                    cnt["o"] += 1
                    info.append((nkt, full_upto, ob))
                    tiles += [(c, kt) for kt in range(nkt)]

                def front(c, kt):
                    nkt, full_upto, ob = info[c]
                    lc = slice(c * CH, (c + 1) * CH)
                    sb = cnt["s"] % 2
                    cnt["s"] += 1
                    pS = PS[sb]
                    kcols = slice(kt * 128, (kt + 1) * 128)
                    MM(pS[:, 0:CH], kh[:, kcols], qn[:, lc], True, False, [tkh[kt // 4], tqn[c]], [tPS[sb]])
                    MM(pS[:, 0:CH], KR[:, kcols], qr[:, lc], False, True, [tKR[kt // 4], tqr[c], tKRz, tqrz], [tPS[sb]])
                    pb = cnt["p"] % len(pbuf)
                    cnt["p"] += 1
                    P = pbuf[pb]
                    p0s = [-1, 7, 15, 23] if s == 0 else [31, 39, 47, 55]
                    may_full = kt >= min(p0s) + 3 * c + 3
                    may_diag = any(0 <= kt - (p + 3 * c) <= 2 for p in p0s)
                    if may_full:
                        ACT(P, pS[:, 0:CH], AF.Exp, [tPS[sb], tCONST], [tp[pb]], bias=BM[:, s, c, kt:kt + 1], scale=1.0)
                    else:
                        ACT(P, pS[:, 0:CH], AF.Exp, [tPS[sb]], [tp[pb]])
                    if may_diag:
                        STT(P, QLOC[:, lc], KK[:, s, kt:kt + 1], P, ALU.is_ge, ALU.mult, [tCONST], [tp[pb]])
                    return pb

                def back(c, kt, pb):
                    nkt, full_upto, ob = info[c]
                    lc = slice(c * CH, (c + 1) * CH)
                    gc = slice(s * SEG + c * CH, s * SEG + (c + 1) * CH)
                    po = PS[2 + ob]
                    pl = PS[4 + ob]
                    P = pbuf[pb]
                    MM(po[:, 0:CH], vh[:, kt, :], P, kt == 0, kt == nkt - 1, [tp[pb], tvh[kt // 4]], [tPS[2 + ob]])
                    MM(pl[:, 0:CH], ONESB[:], P, kt == 0, kt == nkt - 1, [tp[pb], tCONST], [tPS[4 + ob]])
                    if kt != nkt - 1:
                        return
                    O = obuf[ob]
                    ACT(rden, pl[:, 0:CH], AF.Ln, [tPS[4 + ob], tCONST], [trden], bias=CONST[:, 2:3], scale=1.0)
                    ACT(rden, rden, AF.Exp, [], [trden], scale=-1.0)
                    TT(O, po[:, 0:CH], rden, ALU.mult, [tPS[2 + ob], trden], [tob[ob]])

                    def proj():
                        for dm in range(KC):
                            MM(PS[6 + dm % 2][:, 0:CH], wo_h[:, dm * 128:(dm + 1) * 128], O, True, True, [tob[ob], twh[b][3]], [tPS[6 + dm % 2]])
                            STT(X[:, dm, gc], PS[6 + dm % 2][:, 0:CH], GT(0, 0)[:, dm:dm + 1], X[:, dm, gc], ALU.mult, ALU.add,
                                [tPS[6 + dm % 2], tmod], [tX[s][c]])
                    deferred.append([8, proj])

                DEPTH = 2
                pend = []
                deferred = []

                def tick():
                    for d in deferred:
                        d[0] -= 1
                    while deferred and deferred[0][0] <= 0:
                        deferred.pop(0)[1]()

                for (c, kt) in tiles:
                    pend.append((c, kt, front(c, kt)))
                    if len(pend) > DEPTH:
                        back(*pend.pop(0))
                        tick()
                while pend:
                    back(*pend.pop(0))
                    tick()
                while deferred:
                    deferred.pop(0)[1]()

            for h in range(8):
                if h + 1 < 8:
                    load_wh(h + 1)
                head(h)
            sc.barrier()
            pool.release(m0)

        for s in range(NSEG):
            attn_segment(s)
        allX = [t for ts in tX for t in ts]
        dump("xa0", X[:], allX)
        if stop_after == "L0attn":
            return finish(nc, sc, es, dumps, X, outT, tX, None)
        pool.release(0)

        def chunk_list(own):
            out = []
            for s in range(NSEG):
                segs = [(128, 384), (512, 384), (896, 256)] if own else [(0, 384), (384, 384), (768, 384)]
                for (off, w) in segs:
                    g0 = s * SEG + off
                    olds = sorted(set([off // CH, (off + w - 1) // CH]))
                    out.append(dict(s=s, off=off, gc=slice(g0, g0 + w), w=w, xt=[tX[s][o] for o in olds],
                                    olds=[s * NCHS + o for o in olds]))
            return out

        def norm_mod_to(l, which, H, tH, chunks, ps_base=0, defer=False):
            tt = pool.alloc([128, KC, CH], F32)
            rstd = pool.alloc([128, CH], F32)
            tttk = [Tile(f"tt{k}") for k in range(KC)]
            trs = Tile("rs")
            gmul = AB[:, l, which, :]

            def one(i):
                ch = chunks[i]
                gc, w = ch["gc"], ch["w"]
                norm_rstd(X[:, :, gc], KC, w, H[:, :, gc], ps_base + i % 2, rstd[:, 0:w], D, ch["xt"], tH[i], trs)
                for kc in range(KC):
                    TT(tt[:, kc, 0:w], X[:, kc, gc], rstd[:, 0:w], ALU.mult, [trs] + ch["xt"], [tttk[kc]])
                    ACT(H[:, kc, gc], tt[:, kc, 0:w], AF.Identity, [tttk[kc], tmod], [tH[i]],
                        bias=SH(l, which)[:, kc:kc + 1], scale=gmul[:, kc:kc + 1])

            if defer:
                return [(lambda i=i: one(i)) for i in range(len(chunks))]
            for i in range(len(chunks)):
                one(i)

        def mlp(l, chunks):
            m0 = pool.mark()
            H = pool.alloc([128, KC, T0], BF16)
            tH = [Tile(f"H{i}") for i in range(len(chunks))]
            W1 = [pool.alloc([128, KC, 1024], BF16) for _ in range(2)]
            W2 = [pool.alloc([128, 8, 1024], BF16) for _ in range(2)]
            tW1 = [Tile("W1a"), Tile("W1b")]
            tW2 = [Tile("W2a"), Tile("W2b")]
            A = [pool.alloc([128, 8, CH], BF16) for _ in range(2)]
            R = [pool.alloc([128, CH], BF16) for _ in range(2)]
            tA = [Tile("A0"), Tile("A1")]
            tR = [Tile("R0"), Tile("R1")]
            w1src = w_ff1[l].rearrange("(kc p) n -> p kc n", p=128)
            w2src = w_ff2[l].rearrange("(f p) n -> p f n", p=128)

            def loadW(fq):
                b = fq % 2
                DMA("pool", W1[b], w1src[:, :, fq * 1024:(fq + 1) * 1024], f"W1{b}", writes=[tW1[b]])
                DMA("pool", W2[b], w2src[:, fq * 8:(fq + 1) * 8, :], f"W2{b}", writes=[tW2[b]])

            loadW(0)
            norms = norm_mod_to(l, 1, H, tH, chunks, ps_base=5, defer=True)
            nn = {"n": 0}

            def need_norm(upto):
                while nn["n"] <= min(upto, len(chunks) - 1):
                    norms[nn["n"]]()
                    nn["n"] += 1

            cnt = {"u": 0, "a": 0, "y": 0}
            for fq in range(4):
                b = fq % 2
                if fq + 1 < 4:
                    loadW(fq + 1)
                for i, ch in enumerate(chunks):
                    need_norm(i + 1)
                    gc, w = ch["gc"], ch["w"]
                    ab = cnt["a"] % 2
                    cnt["a"] += 1
                    for fc in range(8):
                        ub = cnt["u"] % 3
                        cnt["u"] += 1
                        for kc in range(KC):
                            MM(PS[ub][:, 0:w], W1[b][:, kc, fc * 128:(fc + 1) * 128], H[:, kc, gc], kc == 0, kc == KC - 1,
                               [tW1[b], tH[i]], [tPS[ub]])
                        rb = fc % 2
                        ACT(R[rb][:, 0:w], PS[ub][:, 0:w], AF.Relu, [tPS[ub]], [tR[rb]])
                        TT(A[ab][:, fc, 0:w], R[rb][:, 0:w], R[rb][:, 0:w], ALU.mult, [tR[rb]], [tA[ab]])
                    for dm in range(KC):
                        yb = 3 + cnt["y"] % 2
                        cnt["y"] += 1
                        for fc in range(8):
                            MM(PS[yb][:, 0:w], W2[b][:, fc, dm * 128:(dm + 1) * 128], A[ab][:, fc, 0:w], fc == 0, fc == 7,
                               [tW2[b], tA[ab]], [tPS[yb]])
                        STT(X[:, dm, gc], PS[yb][:, 0:w], GT(l, 1)[:, dm:dm + 1], X[:, dm, gc], ALU.mult, ALU.add,
                            [tPS[yb], tmod], ch["xt"])
            sc.barrier()
            pool.release(m0)

        mlp(0, chunk_list(False))
        dump("xm0", X[:], allX)
        if stop_after == "L0":
            return finish(nc, sc, es, dumps, X, outT, tX, None)

        def swa_layer():
            l = 1
            m0 = pool.mark()
            H = pool.alloc([128, KC, T0], BF16)
            tH = [Tile(f"H1_{i}") for i in range(NCH)]
            KT = pool.alloc([128, 2, T0], BF16)
            KTs = pool.alloc([128, 2, T0], BF16)
            NT = T0 // 128
            VP = pool.alloc([128, NT, 4, 65], BF16)
            tKT = [Tile(f"KT{i}") for i in range(NCH)]
            tVP = [Tile(f"VP{i}") for i in range(NT)]
            tvone = Tile("vone")
            sc.op("pool", lambda e: e.memset(VP[:, :, :, 64:65], 1.0), writes=[tvone])
            bvb = pool.alloc([128, 256], F32)
            sk = pool.alloc([128, 16], F32)
            esink = pool.alloc([128, 16], F32)
            bqs = pool.alloc([128, 8], F32)
            m1 = pool.mark()
            norm_mod_to(l, 0, H, tH, chunk_list(False))
            sc.barrier()
            pool.release(m1)
            wkv = pool.alloc([128, KC, 768], BF16)
            twkv = [Tile("wkv0"), Tile("wkv1")]
            qsrc = w_qkv.rearrange("(kc p) n -> p kc n", p=128)
            DMA("pool", wkv[:, :, 0:512], qsrc[:, :, 1024:1536], "wkv0", writes=[twkv[0]])
            DMA("pool", wkv[:, :, 512:768], w_k_sw.rearrange("(kc p) n -> p kc n", p=128), "wkv1", writes=[twkv[1]])
            tbv = Tile("bvb")
            DMA("sp", bvb, b_v.partition_broadcast(128), "c2", writes=[tbv])
            tsk = Tile("sk")
            DMA("sp", sk, sinks.partition_broadcast(128), "c2", writes=[tsk])
            for i in range(NCH):
                gc = slice(i * CH, (i + 1) * CH)
                for m in range(2):
                    for kc in range(KC):
                        MM(PS[m][:, 0:CH], wkv[:, kc, m * 128:(m + 1) * 128], H[:, kc, gc], kc == 0, kc == KC - 1, [twkv[0], tH[i]], [tPS[m]])
                    ACT(KT[:, m, gc], PS[m][:, 0:CH], AF.Identity, [tPS[m], tl], [tKT[i]], bias=SMALL[:, C_BQ + 8 + m:C_BQ + 9 + m], scale=1.0)
                    for kc in range(KC):
                        MM(PS[2 + m][:, 0:CH], wkv[:, kc, 512 + m * 128:512 + (m + 1) * 128], H[:, kc, gc], kc == 0, kc == KC - 1,
                           [twkv[1], tH[i]], [tPS[2 + m]])
                    ACT(KTs[:, m, gc], PS[2 + m][:, 0:CH], AF.Identity, [tPS[2 + m], tl], [tKT[i]], bias=SMALL[:, C_BKSW + m:C_BKSW + m + 1], scale=1.0)
                for t3 in range(3):
                    t = i * 3 + t3
                    tc = slice(t * 128, (t + 1) * 128)
                    pb = 4 + t % 2
                    for kc in range(KC):
                        MM(PS[pb][:, 0:256], H[:, kc, tc], wkv[:, kc, 256:512], kc == 0, kc == KC - 1, [twkv[0], tH[i]], [tPS[pb]])
                    TT(VP[:, t, :, 0:64], PS[pb][:, 0:256].rearrange("p (g d) -> p g d", g=4), bvb[:].rearrange("p (g d) -> p g d", g=4),
                       ALU.add, [tPS[pb], tbv, tvone], [tVP[t]])
            sc.barrier()
            pool.release(m1)
            if stop_after == "L1kv":
                return True
            wq = pool.alloc([128, KC, 1024], BF16)
            twq = Tile("wq")
            DMA("pool", wq, qsrc[:, :, 0:1024], "wq", writes=[twq])
            BIAS = pool.alloc([128, 4, 2, 512], F32)
            tBIAS = Tile("bias")
            d0 = pool.alloc([128, 128], F32)
            mc = pool.alloc([128, 128], F32)
            mp = pool.alloc([128, 128], F32)
            sc.op("pool", lambda e: e.iota(d0, pattern=[[1, 128]], base=0, channel_multiplier=-1, allow_small_or_imprecise_dtypes=True),
                  writes=[tBIAS])
            TS(mc, d0, 0.0, NEG_BIG, ALU.is_lt, ALU.mult, [tBIAS], [tBIAS])
            TS(mp, d0, 0.0, NEG_BIG, ALU.is_ge, ALU.mult, [], [tBIAS])
            ORDR = [0, 2, 1, 3]
            for hd in range(16):
                g, hh = hd // 4, hd % 4
                j = ORDR.index(hh)
                slope = 2.0 ** (-(hd + 1) / 2.0)
                STT(BIAS[:, g, 1, j * 128:(j + 1) * 128], d0, -slope, mc, ALU.mult, ALU.add, [], [tBIAS])
                STT(BIAS[:, g, 0, j * 128:(j + 1) * 128], d0, -slope, mp, ALU.mult, ALU.add, [], [tBIAS])
                TS(BIAS[:, g, 0, j * 128:(j + 1) * 128], BIAS[:, g, 0, j * 128:(j + 1) * 128], -128.0 * slope, None, ALU.add, None, [], [tBIAS])
            ACT(esink, sk, AF.Exp, [tsk], [tBIAS])
            TS(bqs, SMALL[:, C_BQ:C_BQ + 8], SWA_SCALE, None, ALU.mult, None, [tl], [tBIAS])
            QT = [pool.alloc([128, 8, CH], BF16) for _ in range(2)]
            tQT = [Tile("QT0"), Tile("QT1")]
            SB = [pool.alloc([128, 512], F32) for _ in range(2)]
            tSB = [Tile("SB0"), Tile("SB1")]
            PB = [pool.alloc([128, 512], BF16) for _ in range(6)]
            tPB = [Tile(f"PB{i}") for i in range(6)]
            OTs = [pool.alloc([128, 1024], BF16) for _ in range(2)]
            tOTs = [Tile("OT0"), Tile("OT1")]
            den = pool.alloc([128, 4], F32)
            tden = Tile("den")
            PST = PS[7][:].bitcast(BF16)
            cnt = {"sb": 0, "pb": 0, "sc": 0}
            if stop_after == "L1bias":
                return True
            if True:
                def qgen(i):
                    gc = slice(i * CH, (i + 1) * CH)
                    qb = i % 2
                    for m in range(8):
                        pq = 6 + m % 2
                        for kc in range(KC):
                            MM(PS[pq][:, 0:CH], wq[:, kc, m * 128:(m + 1) * 128], H[:, kc, gc], kc == 0, kc == KC - 1, [twq, tH[i]], [tPS[pq]])
                        ACT(QT[qb][:, m, :], PS[pq][:, 0:CH], AF.Identity, [tPS[pq], tBIAS], [tQT[qb]], bias=bqs[:, m:m + 1], scale=SWA_SCALE)

                def l1_front(i, t3, g):
                    s, c, qb = i // NCHS, i % NCHS, i % 2
                    lt = c * 3 + t3
                    t = i * 3 + t3
                    qc = slice(t3 * 128, (t3 + 1) * 128)
                    pbs = []
                    for kk in range(2):
                        kt = t - 1 + kk
                        kcols = slice(kt * 128, (kt + 1) * 128)
                        pair = 2 * (cnt["sc"] % 2)
                        cnt["sc"] += 1
                        for hh in range(4):
                            m = 2 * g + hh // 2
                            half = hh % 2
                            pr = slice(half * 64, (half + 1) * 64)
                            Ksrc = KT if half == g % 2 else KTs
                            bank = pair + half
                            col = (hh // 2) * 128
                            MM(PS[bank][:, col:col + 128], Ksrc[pr, g // 2, kcols], QT[qb][pr, m, qc], True, True,
                               [tKT[kt // 3], tQT[qb]], [tPS[bank]])
                        sb = cnt["sb"] % 2
                        cnt["sb"] += 1
                        TT(SB[sb][:, 0:256], PS[pair][:, 0:256], BIAS[:, g, kk, 0:256], ALU.add, [tPS[pair], tBIAS], [tSB[sb]])
                        TT(SB[sb][:, 256:512], PS[pair + 1][:, 0:256], BIAS[:, g, kk, 256:512], ALU.add, [tPS[pair + 1], tBIAS], [tSB[sb]])
                        pb = cnt["pb"] % 6
                        cnt["pb"] += 1
                        pbs.append(pb)
                        if lt == 1 and kk == 0:
                            ACT(PB[pb], SB[sb], AF.Exp, [tSB[sb], tCONST], [tPB[pb]], bias=SMALL[:, C_HB + s:C_HB + s + 1], scale=1.0)
                        else:
                            ACT(PB[pb], SB[sb], AF.Exp, [tSB[sb]], [tPB[pb]])
                    return pbs

                def l1_back(i, t3, g, pbs):
                    t = i * 3 + t3
                    pso = 4 + g % 2
                    oi = t % 2
                    for hh in range(4):
                        for kk in range(2):
                            kt = t - 1 + kk
                            pb = pbs[kk]
                            j = ORDR.index(hh)
                            MM(PS[pso][:, hh * 65:(hh + 1) * 65], PB[pb][:, j * 128:(j + 1) * 128], VP[:, kt, g, :], kk == 0, kk == 1,
                               [tPB[pb], tVP[kt], tvone], [tPS[pso]])
                    po3 = PS[pso][:, 0:260].rearrange("p (h d) -> p h d", h=4)
                    TT(den, po3[:, :, 64], esink[:, 4 * g:4 * g + 4], ALU.add, [tPS[pso], tBIAS], [tden], strict=True)
                    RECIP(den, den, [tden], [tden], strict=True)
                    for hh in range(4):
                        hd = 4 * g + hh
                        TS(OTs[oi][:, hd * 64:(hd + 1) * 64], PS[pso][:, hh * 65:hh * 65 + 64], den[:, hh:hh + 1], None, ALU.mult, None,
                           [tPS[pso], tden], [tOTs[oi]], strict=(hh == 0))
                    if g != 3:
                        return
                    for m in range(8):
                        sc.op("pe", lambda e, m=m, oi=oi: e.transpose(PST[:, m * 128:(m + 1) * 128], OTs[oi][:, m * 128:(m + 1) * 128], IDENT[:]),
                              reads=[tOTs[oi], tCONST], writes=[tPS[7]])
                    tcols = slice(t * 128, (t + 1) * 128)
                    ACOPY(H[:, :, tcols], PST.rearrange("p (m q) -> p m q", m=8), [tPS[7]], [tH[i]])

                items = [(i, t3, g) for i in range(NCH) for t3 in range(3) if (i % NCHS) * 3 + t3 != 0 for g in range(4)]
                pend = []
                qdone = set()

                def ensure_q(i):
                    if i not in qdone:
                        qdone.add(i)
                        qgen(i)

                for idx, (i, t3, g) in enumerate(items):
                    ensure_q(i)
                    pend.append((i, t3, g, l1_front(i, t3, g)))
                    if idx + 3 < len(items):
                        ensure_q(items[idx + 3][0])
                    if len(pend) > 2:
                        l1_back(*pend.pop(0))
                while pend:
                    l1_back(*pend.pop(0))
            sc.barrier()
            pool.release(m1)
            wo1 = pool.alloc([128, KC, 1024], BF16)
            two = Tile("wo1")
            DMA("pool", wo1, w_o1.rearrange("(m p) n -> p m n", p=128), "wo1", writes=[two])
            gb = pool.alloc([128, 8], F32)
            tgb = Tile("gb")
            TT(gb, GT(1, 0), SMALL[:, C_BO:C_BO + 8], ALU.mult, [tmod, tl], [tgb])
            for ch in chunk_list(True):
                gc, w = ch["gc"], ch["w"]
                hts = [tH[o] for o in ch["olds"]]
                for dm in range(KC):
                    pb = dm % 2
                    for m in range(8):
                        MM(PS[pb][:, 0:w], wo1[:, m, dm * 128:(dm + 1) * 128], H[:, m, gc], m == 0, m == 7, [two] + hts, [tPS[pb]])
                    STT(X[:, dm, gc], PS[pb][:, 0:w], GT(1, 0)[:, dm:dm + 1], X[:, dm, gc], ALU.mult, ALU.add, [tPS[pb], tmod], ch["xt"])
                    TS(X[:, dm, gc], X[:, dm, gc], gb[:, dm:dm + 1], None, ALU.add, None, [tgb], ch["xt"])
            sc.barrier()
            pool.release(m0)

        if swa_layer():
            return finish(nc, sc, es, dumps, X, outT, tX, None)
        dump("xa1", X[:], allX)
        if stop_after == "L1attn":
            return finish(nc, sc, es, dumps, X, outT, tX, None)
        mlp(1, chunk_list(True))
        dump("xm1", X[:], allX)

        def final_norm():
            sq = pool.alloc([128, KC, CH], BF16)
            rstd = pool.alloc([128, CH], F32)
            Y = [pool.alloc([128, KC, CH], F32) for _ in range(2)]
            tY = [Tile("Y0"), Tile("Y1")]
            tsq = Tile("sqf")
            trs = Tile("rsf")
            xo = outT.rearrange("(kc p) t -> p kc t", p=128)
            evs = []
            for i, ch in enumerate(chunk_list(True)):
                gc, w, s_ = ch["gc"], ch["w"], ch["s"]
                yb = i % 2
                norm_rstd(X[:, :, gc], KC, w, sq[:, :, 0:w], i % 2, rstd[:, 0:w], D, ch["xt"], tsq, trs)
                for kc in range(KC):
                    STT(Y[yb][:, kc, 0:w], X[:, kc, gc], SMALL[:, C_GFIN + kc:C_GFIN + kc + 1], rstd[:, 0:w], ALU.mult, ALU.mult,
                        ch["xt"] + [trs, tl], [tY[yb]])
                o0 = s_ * BLK + ch["off"] - HALO
                evs.append(DMA("sp", xo[:, :, o0:o0 + w], Y[yb][:, :, 0:w], "out", reads=[tY[yb]]))
            return evs

        final_norm()
        return finish(nc, sc, es, dumps, X, outT, tX, "done")


def finish(nc, sc, es, dumps, X, outT, tX, mode):
    if mode is None:
        xo = outT.rearrange("(kc p) t -> p kc t", p=128)
        for s in range(NSEG):
            src = X[:, :, s * SEG + HALO:(s + 1) * SEG]
            dst = xo[:, :, s * BLK:(s + 1) * BLK]
            sc.op("sp", lambda e, src=src, dst=dst: e.dma_start(out=dst, in_=src), reads=tX[s], dma_key="out")
    sc.op("sp", lambda e: None, extra=[sc.last_dma[k] for k in ("out", "dump") if k in sc.last_dma])
    emit(nc, sc, es)
    return nc, dumps


_LAST_SC = {}


def emit(nc, sc, es):
    _LAST_SC.clear()
    _LAST_SC.update({e: q for e, q in sc.q.items()})
    _LAST_SC['nwaits'] = [sum(len(w) for (_, w, _, _) in q) for q in sc.q.values()]
    E = es.enter_context
    signo = {}
    for eng in ENGS:
        n = 0
        for (fn, waits, ev, dk) in sc.q[eng]:
            if dk is None and ev.needed:
                n += 1
                signo[(eng, ev.idx)] = n
    esem = {eng: E(nc.semaphore("s_" + eng)) for eng in ENGS}
    dsem = {k: E(nc.semaphore("d_" + k)) for k in sc.dma_cnt}
    block = E(nc.Block())

    def replay(eng, e):
        for (fn, waits, ev, dk) in sc.q[eng]:
            for d in waits:
                if d.key.startswith("dma:"):
                    e.wait_ge(dsem[d.key[4:]], d.val)
                else:
                    e.wait_ge(esem[d.key], signo[(d.key, d.idx)])
            inst = fn(e)
            if inst is None:
                continue
            if dk is not None:
                inst.then_inc(dsem[dk], 16)
            elif ev.needed:
                inst.then_inc(esem[eng], 1)

    @block.tensor
    def _(e):
        replay("pe", e)

    @block.scalar
    def _(e):
        replay("act", e)

    @block.vector
    def _(e):
        replay("dve", e)

    @block.gpsimd
    def _(e):
        replay("pool", e)

    @block.sync
    def _(e):
        replay("sp", e)


def _core_layout(c):
    b, j = c // 4, c % 4
    blocks = [j, 7 - j]
    return b, blocks


def make_in_maps(inp):
    f32 = np.float32
    x = np.asarray(inp["x"], f32)
    pos = np.asarray(inp["positions"], np.int32)
    half = 32
    inv = (10000.0 ** (-np.arange(half, dtype=f32) / half)).astype(f32)
    invf = np.concatenate([inv, inv])[:, None].astype(f32)

    def featT(v):
        return np.ascontiguousarray(np.asarray(v, f32).reshape(-1, 128).T)

    w_uq = np.asarray(inp["mla_w_uq"][0], f32)
    uq = w_uq.reshape(384, 8, 192)
    w_uq_sw = np.ascontiguousarray(np.concatenate([uq[:, :, 160:192], uq[:, :, 128:160]], -1).reshape(384, 512))
    w_dkv = np.asarray(inp["mla_w_dkv"][0], f32)
    w_dkr_sw = np.ascontiguousarray(np.concatenate([w_dkv[:, 288:320], w_dkv[:, 256:288]], -1))
    w_qkv = np.asarray(inp["swa_w_qkv"][0], f32)
    b_qkv = np.asarray(inp["swa_b_qkv"][0], f32)
    wk = w_qkv[:, 1024:1280].reshape(1024, 2, 2, 64)
    w_k_sw = np.ascontiguousarray(wk[:, :, ::-1, :].reshape(1024, 256))
    bk = b_qkv[1024:1280].reshape(2, 2, 64)
    b_k_sw = np.ascontiguousarray(bk[:, ::-1, :].reshape(256))
    shared = {
        "w_ada": np.asarray(inp["w_ada"], f32),
        "b_adaT": np.ascontiguousarray(np.stack([featT(inp["b_ada"][l]) for l in range(2)])),
        "gmixT": np.ascontiguousarray(np.stack([featT(inp["g_mix"][l]) for l in range(2)])),
        "gmlpT": np.ascontiguousarray(np.stack([featT(inp["g_mlp"][l]) for l in range(2)])),
        "gfinT": featT(inp["g_final"]),
        "w_dq": np.asarray(inp["mla_w_dq"][0], f32),
        "g_qT": featT(inp["mla_g_q"][0]),
        "w_uq": w_uq,
        "w_uq_sw": w_uq_sw,
        "w_dkv": w_dkv,
        "w_dkr_sw": w_dkr_sw,
        "g_kvT": featT(inp["mla_g_kv"][0]),
        "w_ukv": np.asarray(inp["mla_w_ukv"][0], f32),
        "w_o": np.asarray(inp["mla_w_o"][0], f32),
        "invf": invf,
        "w_qkv": w_qkv,
        "w_k_sw": w_k_sw,
        "b_qkvT": featT(b_qkv),
        "b_k_swT": featT(b_k_sw),
        "b_v": np.ascontiguousarray(b_qkv[None, 1280:1536]),
        "sinks": np.asarray(inp["swa_sinks"], f32).reshape(1, 16),
        "w_o1": np.asarray(inp["swa_w_o"][0], f32),
        "b_oT": featT(inp["swa_b_o"][0]),
        "w_ff1": np.asarray(inp["w_ff1"], f32),
        "w_ff2": np.asarray(inp["w_ff2"], f32),
    }
    xT_b = [np.ascontiguousarray(x[b].T) for b in range(2)]
    maps = []
    for c in range(8):
        b, blocks = _core_layout(c)
        xt = np.zeros((D, T0), f32)
        pq = np.zeros((1, T0), np.int32)
        meta = np.zeros((1, 4), f32)
        for s, blk in enumerate(blocks):
            lo = blk * BLK - HALO
            hi = (blk + 1) * BLK
            lo_c = max(lo, 0)
            off = s * SEG + (lo_c - lo)
            xt[:, off:s * SEG + SEG] = xT_b[b][:, lo_c:hi]
            pq[0, off:s * SEG + SEG] = pos[b, lo_c:hi]
            meta[0, s] = lo
            meta[0, 2 + s] = 1.0 if lo >= 0 else 0.0
        m = dict(shared)
        m.update({
            "xT": xt, "xTall": xT_b[b], "posq": pq, "posall": np.ascontiguousarray(pos[b][None, :]),
            "segmeta": meta, "cT": featT(inp["c"][b]),
        })
        maps.append(m)
    return maps


def assemble(results):
    out = np.zeros((2, S, D), np.float32)
    for c in range(8):
        b, blocks = _core_layout(c)
        o = results[c]["outT"]
        for s, blk in enumerate(blocks):
            out[b, blk * BLK:(blk + 1) * BLK, :] = o[:, s * BLK:(s + 1) * BLK].T
    return out


_CACHE = {}


def kernel(**inputs):
    maps = make_in_maps(inputs)
    if "nc" not in _CACHE:
        _CACHE["nc"] = build_program()[0]
    res = run_bass_kernel_spmd(_CACHE["nc"], maps, core_ids=list(range(8)))
    return assemble(res.results)
```

```python
import math
import numpy as np
from contextlib import ExitStack
import concourse.bass as bass
import concourse.mybir as mybir
from concourse.bass_utils import run_bass_kernel_spmd

F32 = mybir.dt.float32
BF16 = mybir.dt.bfloat16
F16 = mybir.dt.float16
I32 = mybir.dt.int32
AF = mybir.ActivationFunctionType
ALU = mybir.AluOpType

D = 1024
KC = 8
S = 8192
NSEG = 2
BLK = 1024
HALO = 128
SEG = BLK + HALO
T0 = NSEG * SEG
CH = 384
NCHS = SEG // CH
NCH = T0 // CH
KCHUNK = 512
NKC = S // KCHUNK
EPS = 1e-6
MLA_SCALE = 192 ** -0.5
SWA_SCALE = 64 ** -0.5
TWO_PI = 2.0 * math.pi
CW_HI = 6.28125
CW_LO = TWO_PI - 6.28125
PI_SAFE = 3.1415925
NEG_BIG = -30000.0
POOL_KIB = 128


class Ev:
    __slots__ = ("key", "val", "clock", "needed", "eng", "idx")

    def __init__(self, key, val, eng, idx):
        self.key = key
        self.val = val
        self.eng = eng
        self.idx = idx
        self.clock = None
        self.needed = False


class Tile:
    __slots__ = ("name", "w", "r")

    def __init__(self, name=""):
        self.name = name
        self.w = None
        self.r = {}


ENGS = ["pe", "act", "dve", "pool", "sp"]


class Sched:
    def __init__(self):
        self.q = {e: [] for e in ENGS}
        self.seen = {e: {} for e in ENGS}
        self.cnt = {e: 0 for e in ENGS}
        self.dma_cnt = {}
        self.last = {e: None for e in ENGS}
        self.last_dma = {}
        self.bar = []

    def op(self, eng, fn, reads=(), writes=(), dma_key=None, extra=(), strict=False):
        deps = []
        for t in reads:
            if t.w is not None:
                deps.append(t.w)
        for t in writes:
            if t.w is not None:
                deps.append(t.w)
            deps.extend(t.r.values())
        deps.extend(extra)
        deps.extend(self.bar)
        seen = self.seen[eng]
        waits = {}
        for d in deps:
            if d.key == eng and not strict:
                continue
            if seen.get(d.key, -1) >= d.val:
                continue
            for k, v in d.clock.items():
                if seen.get(k, -1) < v:
                    seen[k] = v
            if seen.get(d.key, -1) < d.val:
                seen[d.key] = d.val
            d.needed = True
            cur = waits.get(d.key)
            if cur is None or cur.val < d.val:
                waits[d.key] = d
        idx = self.cnt[eng]
        self.cnt[eng] += 1
        if dma_key is None:
            ev = Ev(eng, idx, eng, idx)
        else:
            v = self.dma_cnt.get(dma_key, 0) + 16
            self.dma_cnt[dma_key] = v
            ev = Ev("dma:" + dma_key, v, eng, idx)
            ev.needed = True
            self.last_dma[dma_key] = ev
        clock = dict(seen)
        clock[eng] = idx
        if dma_key is not None:
            clock[eng] = idx - 1
        ev.clock = clock
        self.q[eng].append((fn, list(waits.values()), ev, dma_key))
        self.last[eng] = ev
        for t in writes:
            t.w = ev
            t.r = {}
        for t in reads:
            t.r[ev.key] = ev
        return ev

    def barrier(self):
        evs = [e for e in self.last.values() if e is not None and not e.key.startswith("dma:")]
        evs += list(self.last_dma.values())
        self.bar = evs


class PoolAlloc:
    def __init__(self, ap, nbytes):
        self.ap = ap
        self.nbytes = nbytes
        self.off = 0

    def alloc(self, shape, dtype):
        esz = 4 if dtype in (F32, I32) else 2
        n = 1
        for s in shape[1:]:
            n *= s
        nb = (n * esz + 63) // 64 * 64
        assert self.off + nb <= self.nbytes, f"pool overflow {self.off + nb} > {self.nbytes}"
        a = self.ap[:, self.off // 4:(self.off + nb) // 4]
        self.off += nb
        if dtype != F32:
            a = a.bitcast(dtype)
        a = a[:, 0:n]
        if len(shape) == 3:
            a = a.rearrange("p (a b) -> p a b", a=shape[1])
        elif len(shape) == 4:
            a = a.rearrange("p (a b c) -> p a b c", a=shape[1], b=shape[2])
        if shape[0] < 128:
            a = a[0:shape[0]]
        return a

    def mark(self):
        return self.off

    def release(self, m):
        self.off = m


def build_program(stop_after="all", debug=False):
    nc = bass.Bass("TRN2", target_bir_lowering=False)
    sc = Sched()
    dumps = []

    def din(name, shape, dt=F32):
        return nc.dram_tensor(name, list(shape), dt, kind="ExternalInput").ap()

    xT = din("xT", [D, T0])
    xTall = din("xTall", [D, S])
    posq = din("posq", [1, T0], I32)
    posall = din("posall", [1, S], I32)
    segmeta = din("segmeta", [1, 4])
    cT = din("cT", [128, KC])
    w_ada = din("w_ada", [2, D, 6 * D])
    b_adaT = din("b_adaT", [2, 128, 48])
    gmixT = din("gmixT", [2, 128, KC])
    gmlpT = din("gmlpT", [2, 128, KC])
    gfinT = din("gfinT", [128, KC])
    w_dq = din("w_dq", [D, 384])
    g_qT = din("g_qT", [128, 3])
    w_uq = din("w_uq", [384, 1536])
    w_uq_sw = din("w_uq_sw", [384, 512])
    w_dkv = din("w_dkv", [D, 320])
    w_dkr_sw = din("w_dkr_sw", [D, 64])
    g_kvT = din("g_kvT", [128, 2])
    w_ukv = din("w_ukv", [256, 2048])
    w_o = din("w_o", [D, D])
    invf = din("invf", [64, 1])
    w_qkv = din("w_qkv", [D, 1536])
    w_k_sw = din("w_k_sw", [D, 256])
    b_qkvT = din("b_qkvT", [128, 12])
    b_k_swT = din("b_k_swT", [128, 2])
    b_v = din("b_v", [1, 256])
    sinks = din("sinks", [1, 16])
    w_o1 = din("w_o1", [D, D])
    b_oT = din("b_oT", [128, KC])
    w_ff1 = din("w_ff1", [2, D, 4 * D])
    w_ff2 = din("w_ff2", [2, 4 * D, D])
    outT = nc.dram_tensor("outT", [D, NSEG * BLK], F32, kind="ExternalOutput").ap()

    es = ExitStack()
    with es:
        E = es.enter_context
        X = E(nc.sbuf_tensor("X", [128, KC, T0], F32))
        POOLT = E(nc.sbuf_tensor("POOLT", [128, POOL_KIB * 256], F32))
        CONST = E(nc.sbuf_tensor("CONST", [128, 16], F32))
        MODS = E(nc.sbuf_tensor("MODS", [128, 2, 48], F32))
        AB = E(nc.sbuf_tensor("AB", [128, 2, 2, KC], F32))
        SMALL = E(nc.sbuf_tensor("SMALL", [128, 64], F32))
        ONESB = E(nc.sbuf_tensor("ONESB", [128, 128], BF16))
        IDENT = E(nc.sbuf_tensor("IDENT", [128, 128], BF16))
        QLOC = E(nc.sbuf_tensor("QLOC", [128, SEG], F16))
        KK = E(nc.sbuf_tensor("KK", [128, NSEG, 64], F32))
        META = E(nc.sbuf_tensor("META", [128, 4], F32))
        BM = E(nc.sbuf_tensor("BM", [128, NSEG, NCHS, 64], F32))
        CONDB = E(nc.sbuf_tensor("CONDB", [128, KC], BF16))
        PS = [E(nc.psum_tensor(f"PS{i}", [128, 512], F32)) for i in range(8)]
        pool = PoolAlloc(POOLT, POOL_KIB * 1024)

        tX = [[Tile(f"X{s}_{c}") for c in range(NCHS)] for s in range(NSEG)]
        tPS = [Tile(f"PS{i}") for i in range(8)]
        tCONST = Tile("const")

        C_GQ = 0
        C_GKV = 3
        C_INVF = 5
        C_SGN = 6
        C_HB = 7
        C_GFIN = 16
        C_BO = 24
        C_GB = 32
        C_BQ = 40
        C_BKSW = 52

        def dump(name, ap, tiles, dt=F32):
            if not debug:
                return
            shape = list(ap.shape)
            dr = nc.dram_tensor("dbg_" + name, shape, dt, kind="ExternalOutput").ap()
            dumps.append("dbg_" + name)
            sc.op("sp", lambda e, dr=dr, ap=ap: e.dma_start(out=dr, in_=ap), reads=tiles, dma_key="dump")


        def MM(out, lhsT, rhs, start, stop, reads, writes):
            return sc.op("pe", lambda e: e.matmul(out, lhsT, rhs, start=start, stop=stop), reads=reads, writes=writes)

        def ACT(out, in_, func, reads, writes, bias=None, scale=None):
            kw = {}
            if bias is not None:
                kw["bias"] = bias
            if scale is not None:
                kw["scale"] = scale
            return sc.op("act", lambda e: e.activation(out=out, in_=in_, func=func, **kw), reads=reads, writes=writes)

        def AMUL(out, in_, c, reads, writes):
            return sc.op("act", lambda e: e.mul(out, in_, c), reads=reads, writes=writes)

        def ACOPY(out, in_, reads, writes):
            return sc.op("act", lambda e: e.copy(out, in_), reads=reads, writes=writes)

        def TT(out, in0, in1, op, reads, writes, strict=False):
            return sc.op("dve", lambda e: e.tensor_tensor(out, in0, in1, op), reads=reads, writes=writes, strict=strict)

        def TS(out, in0, s1, s2, op0, op1, reads, writes, strict=False):
            if op1 is None:
                return sc.op("dve", lambda e: e.tensor_scalar(out, in0, s1, s2, op0), reads=reads, writes=writes, strict=strict)
            return sc.op("dve", lambda e: e.tensor_scalar(out, in0, s1, s2, op0, op1), reads=reads, writes=writes, strict=strict)

        def STT(out, in0, scalar, in1, op0, op1, reads, writes, strict=False):
            return sc.op("dve", lambda e: e.scalar_tensor_tensor(out, in0, scalar, in1, op0, op1), reads=reads, writes=writes, strict=strict)

        def RECIP(out, in_, reads, writes, strict=False):
            return sc.op("dve", lambda e: e.reciprocal(out, in_), reads=reads, writes=writes, strict=strict)

        def VCOPY(out, in_, reads, writes):
            return sc.op("dve", lambda e: e.tensor_copy(out, in_), reads=reads, writes=writes)

        def DMA(q, out, in_, key, reads=(), writes=()):
            return sc.op(q, lambda e: e.dma_start(out=out, in_=in_), reads=reads, writes=writes, dma_key=key)

        tl = Tile("smallloads")

        def setup():
            P = "pool"
            sc.op(P, lambda e: e.memset(ONESB[:], 1.0), writes=[tCONST])
            sc.op(P, lambda e: e.memset(CONST[:, 0:1], EPS), writes=[tCONST])
            sc.op(P, lambda e: e.memset(CONST[:, 1:2], 0.0), writes=[tCONST])
            sc.op(P, lambda e: e.memset(CONST[:, 2:3], 1e-18), writes=[tCONST])
            sc.op(P, lambda e: e.memset(SMALL[0:32, C_SGN:C_SGN + 1], -1.0), writes=[tCONST])
            sc.op(P, lambda e: e.memset(SMALL[32:64, C_SGN:C_SGN + 1], 1.0), writes=[tCONST])
            tmp = pool.alloc([128, 128], F32)
            sc.op(P, lambda e: e.iota(tmp, pattern=[[1, 128]], base=0, channel_multiplier=-1,
                                      allow_small_or_imprecise_dtypes=True), writes=[tCONST])
            sc.op(P, lambda e: e.tensor_single_scalar(IDENT[:], tmp, 0.0, ALU.is_equal), writes=[tCONST])
            sc.op(P, lambda e: e.iota(QLOC[:], pattern=[[1, SEG]], base=0, channel_multiplier=0,
                                      allow_small_or_imprecise_dtypes=True), writes=[tCONST])
            kid = pool.alloc([128, 64], F32)
            sc.op(P, lambda e: e.iota(kid, pattern=[[128, 64]], base=0, channel_multiplier=1,
                                      allow_small_or_imprecise_dtypes=True), writes=[tCONST])
            loads = [
                (META[:], segmeta.partition_broadcast(128)),
                (SMALL[:, C_GQ:C_GQ + 3], g_qT),
                (SMALL[:, C_GKV:C_GKV + 2], g_kvT),
                (SMALL[0:64, C_INVF:C_INVF + 1], invf),
                (SMALL[:, C_GFIN:C_GFIN + 8], gfinT),
                (SMALL[:, C_BO:C_BO + 8], b_oT),
                (SMALL[:, C_BQ:C_BQ + 12], b_qkvT),
                (SMALL[:, C_BKSW:C_BKSW + 2], b_k_swT),
            ]
            for o, i in loads:
                DMA("sp", o, i, "c0", writes=[tl])
            kid0 = pool.alloc([128, 64], F32)
            kb = pool.alloc([128, 64], F32)
            sc.op(P, lambda e: e.iota(kid0, pattern=[[128, 64]], base=0, channel_multiplier=0,
                                      allow_small_or_imprecise_dtypes=True), writes=[tCONST])
            for s in range(NSEG):
                TS(KK[:, s, :], kid, META[:, s:s + 1], None, ALU.subtract, None, [tl, tCONST], [tCONST])
                TS(kb, kid0, META[:, s:s + 1], None, ALU.subtract, None, [tl, tCONST], [tCONST])
                for c in range(NCHS):
                    TS(BM[:, s, c, :], kb, 384.0 * c + 383.0, NEG_BIG, ALU.is_gt, ALU.mult, [], [tCONST])
            TS(SMALL[:, C_HB:C_HB + 2], META[:, 2:4], -1.0, -NEG_BIG, ALU.add, ALU.mult, [tl], [tCONST])

        setup()

        tmod = Tile("mods")

        def phase_A():
            m0 = pool.mark()
            cf = pool.alloc([128, KC], F32)
            tc_ = Tile("c")
            DMA("sp", cf, cT, "c1a", writes=[tc_])
            ACT(CONDB[:], cf, AF.Silu, [tc_], [tCONST])
            badd = pool.alloc([128, 2, 48], F32)
            gm = pool.alloc([128, 2, 2, KC], F32)
            tb = Tile("badd")
            DMA("sp", badd, b_adaT.rearrange("l p n -> p l n"), "c1", writes=[tb])
            DMA("sp", gm[:, :, 0, :], gmixT.rearrange("l p n -> p l n"), "c1", writes=[tb])
            DMA("sp", gm[:, :, 1, :], gmlpT.rearrange("l p n -> p l n"), "c1", writes=[tb])
            wa = [pool.alloc([128, KC, 768], BF16) for _ in range(2)]
            twa = [Tile("wa0"), Tile("wa1")]
            n = 0
            for l in range(2):
                wsrc = w_ada[l].rearrange("(kc p) n -> p kc n", p=128)
                for cc in range(8):
                    b = n % 2
                    n += 1
                    DMA("pool", wa[b], wsrc[:, :, cc * 768:(cc + 1) * 768], f"wa{b}", writes=[twa[b]])
                    for nn in range(6):
                        col = l * 48 + cc * 6 + nn
                        for kc in range(KC):
                            MM(PS[0][:, col:col + 1], wa[b][:, kc, nn * 128:(nn + 1) * 128], CONDB[:, kc:kc + 1],
                               kc == 0, kc == KC - 1, [twa[b], tCONST], [tPS[0]])
                TT(MODS[:, l, :], PS[0][:, l * 48:(l + 1) * 48], badd[:, l, :], ALU.add, [tPS[0], tb], [tmod])
                STT(AB[:, l, 0, :], MODS[:, l, 8:16], 1.0, gm[:, l, 0, :], ALU.add, ALU.mult, [tb, tmod], [tmod], strict=True)
                STT(AB[:, l, 1, :], MODS[:, l, 32:40], 1.0, gm[:, l, 1, :], ALU.add, ALU.mult, [tb], [tmod])
            dump("mods", MODS[:], [tmod])
            sc.barrier()
            pool.release(m0)

        phase_A()

        def SH(l, which):
            return MODS[:, l, 0:8] if which == 0 else MODS[:, l, 24:32]

        def GT(l, which):
            return MODS[:, l, 16:24] if which == 0 else MODS[:, l, 40:48]

        def norm_rstd(src3, nk, n, sqbuf, ps_i, rstd, dim, reads, tsq, trstd):
            ACT(sqbuf, src3, AF.Square, reads, [tsq])
            for k in range(nk):
                MM(PS[ps_i][:, 0:n], ONESB[:], sqbuf[:, k, :], k == 0, k == nk - 1, [tsq, tCONST], [tPS[ps_i]])
            ACT(rstd, PS[ps_i][:, 0:n], AF.Ln, [tPS[ps_i]], [trstd], bias=CONST[:, 0:1], scale=1.0 / dim)
            ACT(rstd, rstd, AF.Exp, [], [trstd], scale=-0.5)

        def rope_tables(pos_ap, ang, nq, r, cos2, sin2, tpos, ttab, tt):
            inv = SMALL[0:64, C_INVF:C_INVF + 1]
            sgn = SMALL[0:64, C_SGN:C_SGN + 1]
            VCOPY(ang, pos_ap, [tpos], [tt])
            TS(ang, ang, inv, None, ALU.mult, None, [tl], [tt])
            for which in range(2):
                if which == 1:
                    TS(ang, ang, math.pi / 2, None, ALU.add, None, [], [tt])
                TS(r, ang, 1.0 / TWO_PI, None, ALU.mult, None, [], [tt])
                VCOPY(nq, r, [], [tt])
                STT(r, nq, -CW_HI, ang, ALU.mult, ALU.add, [], [tt])
                STT(r, nq, -CW_LO, r, ALU.mult, ALU.add, [], [tt])
                TS(r, r, -PI_SAFE, PI_SAFE, ALU.max, ALU.min, [], [tt])
                if which == 0:
                    ACT(sin2, r, AF.Sin, [tt, tCONST], [ttab], scale=sgn)
                else:
                    ACT(cos2, r, AF.Sin, [tt], [ttab])

        LAT = pool.alloc([128, 2, S], BF16)
        KR = pool.alloc([128, S], BF16)
        tKRz = Tile("krz")
        sc.op("pool", lambda e: e.memset(KR[64:128, :], 0.0), writes=[tKRz])
        tLAT = [Tile(f"lat{i}") for i in range(NKC)]
        tKR = [Tile(f"kr{i}") for i in range(NKC)]
        m_attn = pool.mark()

        def phase_B():
            wdkv = pool.alloc([128, KC, 384], BF16)
            twd = Tile("wdkv")
            twd2 = Tile("wdkv2")
            DMA("pool", wdkv[:, :, 0:320], w_dkv.rearrange("(kc p) n -> p kc n", p=128), "wdkv", writes=[twd])
            DMA("pool", wdkv[:, :, 320:384], w_dkr_sw.rearrange("(kc p) n -> p kc n", p=128), "wdkv2", writes=[twd2])
            xa = [pool.alloc([128, KC, KCHUNK], F32) for _ in range(2)]
            hb = [pool.alloc([128, KC, KCHUNK], BF16) for _ in range(2)]
            posb = [pool.alloc([64, KCHUNK], I32) for _ in range(2)]
            rstd = pool.alloc([128, KCHUNK], F32)
            rstd2 = pool.alloc([128, KCHUNK], F32)
            sqc = pool.alloc([128, 2, KCHUNK], BF16)
            ang = pool.alloc([64, KCHUNK], F32)
            nq = pool.alloc([64, KCHUNK], I32)
            rr = pool.alloc([64, KCHUNK], F32)
            cos2 = pool.alloc([64, KCHUNK], F32)
            sin2 = pool.alloc([64, KCHUNK], F32)
            ku = pool.alloc([64, KCHUNK], F32)
            kv = pool.alloc([64, KCHUNK], F32)
            txa = [Tile("xa0"), Tile("xa1")]
            th = [Tile("h0"), Tile("h1")]
            tpos = [Tile("pos0"), Tile("pos1")]
            trs = Tile("rstd")
            trs2 = Tile("rstd2")
            tsqc = Tile("sqc")
            ttab = Tile("tab")
            ttmp = Tile("ropetmp")
            tku = Tile("ku")
            xsrc = xTall.rearrange("(kc p) t -> p kc t", p=128)

            txak = [[Tile(f"xa{b}_{k}") for k in range(KC)] for b in range(2)]

            def A1(i):
                b = i % 2
                cols = slice(i * KCHUNK, (i + 1) * KCHUNK)
                DMA("sp", xa[b], xsrc[:, :, cols], f"xa{b}", writes=[txa[b]] + txak[b])
                DMA("sp", posb[b], posall[:, cols].partition_broadcast(64), f"pos{b}", writes=[tpos[b]])
                norm_rstd(xa[b], KC, KCHUNK, hb[b], b, rstd, D, [txa[b]], th[b], trs)
                for kc in range(KC):
                    TT(xa[b][:, kc, :], xa[b][:, kc, :], rstd, ALU.mult, [trs, txa[b]], [txak[b][kc]])

            def A2(i):
                b = i % 2
                for kc in range(KC):
                    ACT(hb[b][:, kc, :], xa[b][:, kc, :], AF.Identity, [txak[b][kc], tmod], [th[b]],
                        bias=SH(0, 0)[:, kc:kc + 1], scale=AB[:, 0, 0, kc:kc + 1])

            def B1(i):
                b = i % 2
                cols = slice(i * KCHUNK, (i + 1) * KCHUNK)
                for (pi, c0, c1, m) in [(2, 0, 128, 128), (3, 128, 256, 128), (4, 256, 320, 64), (5, 320, 384, 64)]:
                    for kc in range(KC):
                        MM(PS[pi][0:m, 0:KCHUNK], wdkv[:, kc, c0:c1], hb[b][:, kc, :], kc == 0, kc == KC - 1,
                           [th[b], twd, twd2], [tPS[pi]])
                ACT(sqc[:, 0, :], PS[2][:, 0:KCHUNK], AF.Square, [tPS[2]], [tsqc])
                ACT(sqc[:, 1, :], PS[3][:, 0:KCHUNK], AF.Square, [tPS[3]], [tsqc])
                for k in range(2):
                    MM(PS[6][:, 0:KCHUNK], ONESB[:], sqc[:, k, :], k == 0, k == 1, [tsqc, tCONST], [tPS[6]])
                ACT(rstd2, PS[6][:, 0:KCHUNK], AF.Ln, [tPS[6]], [trs2], bias=CONST[:, 0:1], scale=1.0 / 256)
                ACT(rstd2, rstd2, AF.Exp, [], [trs2], scale=-0.5)
                for k in range(2):
                    STT(LAT[:, k, cols], PS[2 + k][:, 0:KCHUNK], SMALL[:, C_GKV + k:C_GKV + k + 1], rstd2, ALU.mult, ALU.mult,
                        [tPS[2 + k], trs2, tl], [tLAT[i]])

            def B2(i):
                cols = slice(i * KCHUNK, (i + 1) * KCHUNK)
                TT(ku, PS[4][0:64, 0:KCHUNK], cos2, ALU.mult, [tPS[4], ttab], [tku])
                TT(kv, PS[5][0:64, 0:KCHUNK], sin2, ALU.mult, [tPS[5], ttab], [tku])
                TT(KR[0:64, cols], ku, kv, ALU.add, [tku], [tKR[i]])

            A1(0)
            A2(0)
            rope_tables(posb[0], ang, nq, rr, cos2, sin2, tpos[0], ttab, ttmp)
            for i in range(NKC):
                if i + 1 < NKC:
                    A1(i + 1)
                B1(i)
                if i + 1 < NKC:
                    A2(i + 1)
                B2(i)
                if i + 1 < NKC:
                    rope_tables(posb[(i + 1) % 2], ang, nq, rr, cos2, sin2, tpos[(i + 1) % 2], ttab, ttmp)
            dump("lat", LAT, tLAT, BF16)
            dump("kr", KR[0:64, :], tKR, BF16)
            sc.barrier()

        xsrc_own = xT.rearrange("(kc p) t -> p kc t", p=128)
        for s in range(NSEG):
            for c in range(NCHS):
                cols = slice(s * SEG + c * CH, s * SEG + (c + 1) * CH)
                DMA("sp", X[:, :, cols], xsrc_own[:, :, cols], f"x{s}{c}", writes=[tX[s][c]])
        phase_B()
        pool.release(m_attn)
        if stop_after == "B":
            return finish(nc, sc, es, dumps, X, outT, tX, None)

        def attn_segment(s):
            m0 = pool.mark()
            NK = 4096 if s == 0 else 8192
            cqn = pool.alloc([128, 3, SEG], BF16)
            cos2q = pool.alloc([64, SEG], F32)
            sin2q = pool.alloc([64, SEG], F32)
            qn = pool.alloc([128, SEG], BF16)
            qr = pool.alloc([128, SEG], BF16)
            tqrz = Tile("qrz")
            sc.op("pool", lambda e: e.memset(qr[64:128, :], 0.0), writes=[tqrz])
            wh = [pool.alloc([128, 2304], BF16) for _ in range(2)]
            pbuf = [pool.alloc([128, CH], BF16) for _ in range(4)]
            obuf = [pool.alloc([128, CH], BF16) for _ in range(2)]
            rden = pool.alloc([128, CH], F32)
            tcqn = [Tile(f"cqn{c}") for c in range(NCHS)]
            ttabq = [Tile(f"tabq{c}") for c in range(NCHS)]
            tqn = [Tile(f"qn{c}") for c in range(NCHS)]
            tqr = [Tile(f"qr{c}") for c in range(NCHS)]
            twh = [[Tile(f"wh{b}_{k}") for k in range(4)] for b in range(2)]
            tp = [Tile(f"p{i}") for i in range(4)]
            tob = [Tile("ob0"), Tile("ob1")]
            trden = Tile("rden")
            m1 = pool.mark()
            wdq = pool.alloc([128, KC, 384], BF16)
            twdq = Tile("wdq")
            DMA("pool", wdq, w_dq.rearrange("(kc p) n -> p kc n", p=128), "wdq", writes=[twdq])
            hq = pool.alloc([128, KC, CH], BF16)
            tt = pool.alloc([128, KC, CH], F32)
            rstd = pool.alloc([128, CH], F32)
            rstdq = pool.alloc([128, CH], F32)
            sqq = pool.alloc([128, 3, CH], BF16)
            posb = pool.alloc([64, CH], I32)
            ang = pool.alloc([64, CH], F32)
            nq_ = pool.alloc([64, CH], I32)
            rr = pool.alloc([64, CH], F32)
            th = Tile("hq")
            tttk = [Tile(f"ttq{k}") for k in range(KC)]
            trs = Tile("rs")
            trsq = Tile("rsq")
            tsqq = Tile("sqq")
            tpos = Tile("posq")
            ttmp = Tile("ropetmpq")
            for c in range(NCHS):
                lc = slice(c * CH, (c + 1) * CH)
                gc = slice(s * SEG + c * CH, s * SEG + (c + 1) * CH)
                DMA("sp", posb, posq[:, gc].partition_broadcast(64), "posq", writes=[tpos])
                norm_rstd(X[:, :, gc], KC, CH, hq, 0, rstd, D, [tX[s][c]], th, trs)
                for kc in range(KC):
                    TT(tt[:, kc, :], X[:, kc, gc], rstd, ALU.mult, [trs, tX[s][c]], [tttk[kc]])
                    ACT(hq[:, kc, :], tt[:, kc, :], AF.Identity, [tttk[kc], tmod], [th],
                        bias=SH(0, 0)[:, kc:kc + 1], scale=AB[:, 0, 0, kc:kc + 1])
                for m in range(3):
                    for kc in range(KC):
                        MM(PS[1 + m][:, 0:CH], wdq[:, kc, m * 128:(m + 1) * 128], hq[:, kc, :], kc == 0, kc == KC - 1,
                           [th, twdq], [tPS[1 + m]])
                for m in range(3):
                    ACT(sqq[:, m, :], PS[1 + m][:, 0:CH], AF.Square, [tPS[1 + m]], [tsqq])
                for m in range(3):
                    MM(PS[4][:, 0:CH], ONESB[:], sqq[:, m, :], m == 0, m == 2, [tsqq, tCONST], [tPS[4]])
                ACT(rstdq, PS[4][:, 0:CH], AF.Ln, [tPS[4]], [trsq], bias=CONST[:, 0:1], scale=1.0 / 384)
                ACT(rstdq, rstdq, AF.Exp, [], [trsq], scale=-0.5)
                for m in range(3):
                    STT(cqn[:, m, lc], PS[1 + m][:, 0:CH], SMALL[:, C_GQ + m:C_GQ + m + 1], rstdq, ALU.mult, ALU.mult,
                        [tPS[1 + m], trsq, tl], [tcqn[c]])
                rope_tables(posb, ang, nq_, rr, cos2q[:, lc], sin2q[:, lc], tpos, ttabq[c], ttmp)
            if s == 0:
                dump("cqn0", cqn, tcqn, BF16)
            sc.barrier()
            pool.release(m1)
            kh = pool.alloc([128, NK], BF16)
            vh = pool.alloc([128, NK // 128, 128], BF16)
            u1 = pool.alloc([64, CH], F32)
            u2 = pool.alloc([64, CH], F32)
            NKCH = NK // 512
            tkh = [Tile(f"kh{i}") for i in range(NKCH)]
            tvh = [Tile(f"vh{i}") for i in range(NKCH)]
            tu = Tile("u")
            uq_src = w_uq.rearrange("(m p) n -> p m n", p=128)
            uqs_src = w_uq_sw.rearrange("(m p) n -> p m n", p=128)
            ukv_src = w_ukv.rearrange("(m p) n -> p m n", p=128)

            def wviews(b):
                w = wh[b]
                return (w[:, 0:576].rearrange("p (m n) -> p m n", m=3), w[:, 576:768].rearrange("p (m n) -> p m n", m=3),
                        w[:, 768:1280].rearrange("p (m n) -> p m n", m=2), w[:, 1280:2304])

            def load_wh(h):
                b = h % 2
                wuq, wuqs, wukv, wo_h = wviews(b)
                DMA("pool", wuq, uq_src[:, :, h * 192:(h + 1) * 192], f"wh{b}0", writes=[twh[b][0]])
                DMA("pool", wuqs, uqs_src[:, :, h * 64:(h + 1) * 64], f"wh{b}1", writes=[twh[b][1]])
                DMA("pool", wukv, ukv_src[:, :, h * 256:(h + 1) * 256], f"wh{b}2", writes=[twh[b][2]])
                DMA("pool", wo_h, w_o[h * 128:(h + 1) * 128, :], f"wh{b}3", writes=[twh[b][3]])

            load_wh(0)
            cnt = {"p": 0, "o": 0, "s": 0}

            def head(h):
                b = h % 2
                wuq, wuqs, wukv, wo_h = wviews(b)
                for c in range(NCHS):
                    lc = slice(c * CH, (c + 1) * CH)
                    for m in range(3):
                        MM(PS[6][:, 0:CH], wuq[:, m, 0:128], cqn[:, m, lc], m == 0, m == 2, [twh[b][0], tcqn[c]], [tPS[6]])
                    AMUL(qn[:, lc], PS[6][:, 0:CH], MLA_SCALE, [tPS[6]], [tqn[c]])
                    for m in range(3):
                        MM(PS[7][0:64, 0:CH], wuq[:, m, 128:192], cqn[:, m, lc], m == 0, m == 2, [twh[b][0], tcqn[c]], [tPS[7]])
                    STT(u1, PS[7][0:64, 0:CH], MLA_SCALE, cos2q[:, lc], ALU.mult, ALU.mult, [tPS[7], ttabq[c]], [tu])
                    for m in range(3):
                        MM(PS[7][0:64, 0:CH], wuqs[:, m, :], cqn[:, m, lc], m == 0, m == 2, [twh[b][1], tcqn[c]], [tPS[7]])
                    STT(u2, PS[7][0:64, 0:CH], MLA_SCALE, sin2q[:, lc], ALU.mult, ALU.mult, [tPS[7], ttabq[c]], [tu])
                    TT(qr[0:64, lc], u1, u2, ALU.add, [tu], [tqr[c]])
                for i in range(NKCH):
                    cols = slice(i * 512, (i + 1) * 512)
                    bk = 6 if i % 2 == 0 else 0
                    bv_ = 7 if i % 2 == 0 else 1
                    for m in range(2):
                        MM(PS[bk][:, 0:512], wukv[:, m, 0:128], LAT[:, m, cols], m == 0, m == 1, [twh[b][2], tLAT[i]], [tPS[bk]])
                    if i % 2 == 0:
                        VCOPY(kh[:, cols], PS[bk][:, 0:512], [tPS[bk]], [tkh[i]])
                    else:
                        ACOPY(kh[:, cols], PS[bk][:, 0:512], [tPS[bk]], [tkh[i]])
                    for t in range(4):
                        kt = i * 4 + t
                        for m in range(2):
                            MM(PS[bv_][:, t * 128:(t + 1) * 128], LAT[:, m, kt * 128:(kt + 1) * 128], wukv[:, m, 128:256],
                               m == 0, m == 1, [twh[b][2], tLAT[i]], [tPS[bv_]])
                    if i % 2 == 0:
                        ACOPY(vh[:, i * 4:(i + 1) * 4, :], PS[bv_][:, 0:512].rearrange("p (a b) -> p a b", a=4), [tPS[bv_]], [tvh[i]])
                    else:
                        VCOPY(vh[:, i * 4:(i + 1) * 4, :], PS[bv_][:, 0:512].rearrange("p (a b) -> p a b", a=4), [tPS[bv_]], [tvh[i]])
                info = []
                tiles = []
                for c in range(NCHS):
                    if s == 0:
                        nkt = 26 + 3 * c
                        full_upto = 3 * c - 2
                    else:
                        nkt = 58 + 3 * c
                        full_upto = 30 + 3 * c
                    nkt = min(nkt, NK // 128)
                    ob = cnt["o"] % 2
                    cnt["o"] += 1
                    info.append((nkt, full_upto, ob))
                    tiles += [(c, kt) for kt in range(nkt)]

                def front(c, kt):
                    nkt, full_upto, ob = info[c]
                    lc = slice(c * CH, (c + 1) * CH)
                    sb = cnt["s"] % 2
                    cnt["s"] += 1
                    pS = PS[sb]
                    kcols = slice(kt * 128, (kt + 1) * 128)
                    MM(pS[:, 0:CH], kh[:, kcols], qn[:, lc], True, False, [tkh[kt // 4], tqn[c]], [tPS[sb]])
                    MM(pS[:, 0:CH], KR[:, kcols], qr[:, lc], False, True, [tKR[kt // 4], tqr[c], tKRz, tqrz], [tPS[sb]])
                    pb = cnt["p"] % len(pbuf)
                    cnt["p"] += 1
                    P = pbuf[pb]
                    p0s = [-1, 7, 15, 23] if s == 0 else [31, 39, 47, 55]
                    may_full = kt >= min(p0s) + 3 * c + 3
                    may_diag = any(0 <= kt - (p + 3 * c) <= 2 for p in p0s)
                    if may_full:
                        ACT(P, pS[:, 0:CH], AF.Exp, [tPS[sb], tCONST], [tp[pb]], bias=BM[:, s, c, kt:kt + 1], scale=1.0)
                    else:
                        ACT(P, pS[:, 0:CH], AF.Exp, [tPS[sb]], [tp[pb]])
                    if may_diag:
                        STT(P, QLOC[:, lc], KK[:, s, kt:kt + 1], P, ALU.is_ge, ALU.mult, [tCONST], [tp[pb]])
                    return pb

                def back(c, kt, pb):
                    nkt, full_upto, ob = info[c]
                    lc = slice(c * CH, (c + 1) * CH)
                    gc = slice(s * SEG + c * CH, s * SEG + (c + 1) * CH)
                    po = PS[2 + ob]
                    pl = PS[4 + ob]
                    P = pbuf[pb]
                    MM(po[:, 0:CH], vh[:, kt, :], P, kt == 0, kt == nkt - 1, [tp[pb], tvh[kt // 4]], [tPS[2 + ob]])
                    MM(pl[:, 0:CH], ONESB[:], P, kt == 0, kt == nkt - 1, [tp[pb], tCONST], [tPS[4 + ob]])
                    if kt != nkt - 1:
                        return
                    O = obuf[ob]
                    ACT(rden, pl[:, 0:CH], AF.Ln, [tPS[4 + ob], tCONST], [trden], bias=CONST[:, 2:3], scale=1.0)
                    ACT(rden, rden, AF.Exp, [], [trden], scale=-1.0)
                    TT(O, po[:, 0:CH], rden, ALU.mult, [tPS[2 + ob], trden], [tob[ob]])

                    def proj():
                        for dm in range(KC):
                            MM(PS[6 + dm % 2][:, 0:CH], wo_h[:, dm * 128:(dm + 1) * 128], O, True, True, [tob[ob], twh[b][3]], [tPS[6 + dm % 2]])
                            STT(X[:, dm, gc], PS[6 + dm % 2][:, 0:CH], GT(0, 0)[:, dm:dm + 1], X[:, dm, gc], ALU.mult, ALU.add,
                                [tPS[6 + dm % 2], tmod], [tX[s][c]])
                    deferred.append([8, proj])

                DEPTH = 2
                pend = []
                deferred = []

                def tick():
                    for d in deferred:
                        d[0] -= 1
                    while deferred and deferred[0][0] <= 0:
                        deferred.pop(0)[1]()

                for (c, kt) in tiles:
                    pend.append((c, kt, front(c, kt)))
                    if len(pend) > DEPTH:
                        back(*pend.pop(0))
                        tick()
                while pend:
                    back(*pend.pop(0))
                    tick()
                while deferred:
                    deferred.pop(0)[1]()

            for h in range(8):
                if h + 1 < 8:
                    load_wh(h + 1)
                head(h)
            sc.barrier()
            pool.release(m0)

        for s in range(NSEG):
            attn_segment(s)
        allX = [t for ts in tX for t in ts]
        dump("xa0", X[:], allX)
        if stop_after == "L0attn":
            return finish(nc, sc, es, dumps, X, outT, tX, None)
        pool.release(0)

        def chunk_list(own):
            out = []
            for s in range(NSEG):
                segs = [(128, 384), (512, 384), (896, 256)] if own else [(0, 384), (384, 384), (768, 384)]
                for (off, w) in segs:
                    g0 = s * SEG + off
                    olds = sorted(set([off // CH, (off + w - 1) // CH]))
                    out.append(dict(s=s, off=off, gc=slice(g0, g0 + w), w=w, xt=[tX[s][o] for o in olds],
                                    olds=[s * NCHS + o for o in olds]))
            return out

        def norm_mod_to(l, which, H, tH, chunks, ps_base=0, defer=False):
            tt = pool.alloc([128, KC, CH], F32)
            rstd = pool.alloc([128, CH], F32)
            tttk = [Tile(f"tt{k}") for k in range(KC)]
            trs = Tile("rs")
            gmul = AB[:, l, which, :]

            def one(i):
                ch = chunks[i]
                gc, w = ch["gc"], ch["w"]
                norm_rstd(X[:, :, gc], KC, w, H[:, :, gc], ps_base + i % 2, rstd[:, 0:w], D, ch["xt"], tH[i], trs)
                for kc in range(KC):
                    TT(tt[:, kc, 0:w], X[:, kc, gc], rstd[:, 0:w], ALU.mult, [trs] + ch["xt"], [tttk[kc]])
                    ACT(H[:, kc, gc], tt[:, kc, 0:w], AF.Identity, [tttk[kc], tmod], [tH[i]],
                        bias=SH(l, which)[:, kc:kc + 1], scale=gmul[:, kc:kc + 1])

            if defer:
                return [(lambda i=i: one(i)) for i in range(len(chunks))]
            for i in range(len(chunks)):
                one(i)

        def mlp(l, chunks):
            m0 = pool.mark()
            H = pool.alloc([128, KC, T0], BF16)
            tH = [Tile(f"H{i}") for i in range(len(chunks))]
            W1 = [pool.alloc([128, KC, 1024], BF16) for _ in range(2)]
            W2 = [pool.alloc([128, 8, 1024], BF16) for _ in range(2)]
            tW1 = [Tile("W1a"), Tile("W1b")]
            tW2 = [Tile("W2a"), Tile("W2b")]
            A = [pool.alloc([128, 8, CH], BF16) for _ in range(2)]
            R = [pool.alloc([128, CH], BF16) for _ in range(2)]
            tA = [Tile("A0"), Tile("A1")]
            tR = [Tile("R0"), Tile("R1")]
            w1src = w_ff1[l].rearrange("(kc p) n -> p kc n", p=128)
            w2src = w_ff2[l].rearrange("(f p) n -> p f n", p=128)

            def loadW(fq):
                b = fq % 2
                DMA("pool", W1[b], w1src[:, :, fq * 1024:(fq + 1) * 1024], f"W1{b}", writes=[tW1[b]])
                DMA("pool", W2[b], w2src[:, fq * 8:(fq + 1) * 8, :], f"W2{b}", writes=[tW2[b]])

            loadW(0)
            norms = norm_mod_to(l, 1, H, tH, chunks, ps_base=5, defer=True)
            nn = {"n": 0}

            def need_norm(upto):
                while nn["n"] <= min(upto, len(chunks) - 1):
                    norms[nn["n"]]()
                    nn["n"] += 1

            cnt = {"u": 0, "a": 0, "y": 0}
            for fq in range(4):
                b = fq % 2
                if fq + 1 < 4:
                    loadW(fq + 1)
                for i, ch in enumerate(chunks):
                    need_norm(i + 1)
                    gc, w = ch["gc"], ch["w"]
                    ab = cnt["a"] % 2
                    cnt["a"] += 1
                    for fc in range(8):
                        ub = cnt["u"] % 3
                        cnt["u"] += 1
                        for kc in range(KC):
                            MM(PS[ub][:, 0:w], W1[b][:, kc, fc * 128:(fc + 1) * 128], H[:, kc, gc], kc == 0, kc == KC - 1,
                               [tW1[b], tH[i]], [tPS[ub]])
                        rb = fc % 2
                        ACT(R[rb][:, 0:w], PS[ub][:, 0:w], AF.Relu, [tPS[ub]], [tR[rb]])
                        TT(A[ab][:, fc, 0:w], R[rb][:, 0:w], R[rb][:, 0:w], ALU.mult, [tR[rb]], [tA[ab]])
                    for dm in range(KC):
                        yb = 3 + cnt["y"] % 2
                        cnt["y"] += 1
                        for fc in range(8):
                            MM(PS[yb][:, 0:w], W2[b][:, fc, dm * 128:(dm + 1) * 128], A[ab][:, fc, 0:w], fc == 0, fc == 7,
                               [tW2[b], tA[ab]], [tPS[yb]])
                        STT(X[:, dm, gc], PS[yb][:, 0:w], GT(l, 1)[:, dm:dm + 1], X[:, dm, gc], ALU.mult, ALU.add,
                            [tPS[yb], tmod], ch["xt"])
            sc.barrier()
            pool.release(m0)

        mlp(0, chunk_list(False))
        dump("xm0", X[:], allX)
        if stop_after == "L0":
            return finish(nc, sc, es, dumps, X, outT, tX, None)

        def swa_layer():
            l = 1
            m0 = pool.mark()
            H = pool.alloc([128, KC, T0], BF16)
            tH = [Tile(f"H1_{i}") for i in range(NCH)]
            KT = pool.alloc([128, 2, T0], BF16)
            KTs = pool.alloc([128, 2, T0], BF16)
            NT = T0 // 128
            VP = pool.alloc([128, NT, 4, 65], BF16)
            tKT = [Tile(f"KT{i}") for i in range(NCH)]
            tVP = [Tile(f"VP{i}") for i in range(NT)]
            tvone = Tile("vone")
            sc.op("pool", lambda e: e.memset(VP[:, :, :, 64:65], 1.0), writes=[tvone])
            bvb = pool.alloc([128, 256], F32)
            sk = pool.alloc([128, 16], F32)
            esink = pool.alloc([128, 16], F32)
            bqs = pool.alloc([128, 8], F32)
            m1 = pool.mark()
            norms1 = norm_mod_to(l, 0, H, tH, chunk_list(False), ps_base=6, defer=True)
            nn1 = {"n": 0}

            def need_norm1(upto):
                while nn1["n"] <= min(upto, NCH - 1):
                    norms1[nn1["n"]]()
                    nn1["n"] += 1
            wkv = pool.alloc([128, KC, 768], BF16)
            twkv = [Tile("wkv0"), Tile("wkv1")]
            qsrc = w_qkv.rearrange("(kc p) n -> p kc n", p=128)
            DMA("pool", wkv[:, :, 0:512], qsrc[:, :, 1024:1536], "wkv0", writes=[twkv[0]])
            DMA("pool", wkv[:, :, 512:768], w_k_sw.rearrange("(kc p) n -> p kc n", p=128), "wkv1", writes=[twkv[1]])
            tbv = Tile("bvb")
            DMA("sp", bvb, b_v.partition_broadcast(128), "c2a", writes=[tbv])
            tsk = Tile("sk")
            DMA("sp", sk, sinks.partition_broadcast(128), "c2b", writes=[tsk])
            for i in range(NCH):
                need_norm1(i + 1)
                gc = slice(i * CH, (i + 1) * CH)
                for m in range(2):
                    for kc in range(KC):
                        MM(PS[m][:, 0:CH], wkv[:, kc, m * 128:(m + 1) * 128], H[:, kc, gc], kc == 0, kc == KC - 1, [twkv[0], tH[i]], [tPS[m]])
                    ACT(KT[:, m, gc], PS[m][:, 0:CH], AF.Identity, [tPS[m], tl], [tKT[i]], bias=SMALL[:, C_BQ + 8 + m:C_BQ + 9 + m], scale=1.0)
                    for kc in range(KC):
                        MM(PS[2 + m][:, 0:CH], wkv[:, kc, 512 + m * 128:512 + (m + 1) * 128], H[:, kc, gc], kc == 0, kc == KC - 1,
                           [twkv[1], tH[i]], [tPS[2 + m]])
                    ACT(KTs[:, m, gc], PS[2 + m][:, 0:CH], AF.Identity, [tPS[2 + m], tl], [tKT[i]], bias=SMALL[:, C_BKSW + m:C_BKSW + m + 1], scale=1.0)
                for t3 in range(3):
                    t = i * 3 + t3
                    tc = slice(t * 128, (t + 1) * 128)
                    pb = 4 + t % 2
                    for kc in range(KC):
                        MM(PS[pb][:, 0:256], H[:, kc, tc], wkv[:, kc, 256:512], kc == 0, kc == KC - 1, [twkv[0], tH[i]], [tPS[pb]])
                    TT(VP[:, t, :, 0:64], PS[pb][:, 0:256].rearrange("p (g d) -> p g d", g=4), bvb[:].rearrange("p (g d) -> p g d", g=4),
                       ALU.add, [tPS[pb], tbv, tvone], [tVP[t]])
            sc.barrier()
            pool.release(m1)
            if stop_after == "L1kv":
                return True
            wq = pool.alloc([128, KC, 1024], BF16)
            twq = Tile("wq")
            DMA("pool", wq, qsrc[:, :, 0:1024], "wq", writes=[twq])
            BIAS = pool.alloc([128, 4, 2, 512], F32)
            tBIAS = Tile("bias")
            d0 = pool.alloc([128, 128], F32)
            mc = pool.alloc([128, 128], F32)
            mp = pool.alloc([128, 128], F32)
            sc.op("pool", lambda e: e.iota(d0, pattern=[[1, 128]], base=0, channel_multiplier=-1, allow_small_or_imprecise_dtypes=True),
                  writes=[tBIAS])
            TS(mc, d0, 0.0, NEG_BIG, ALU.is_lt, ALU.mult, [tBIAS], [tBIAS])
            TS(mp, d0, 0.0, NEG_BIG, ALU.is_ge, ALU.mult, [], [tBIAS])
            ORDR = [0, 2, 1, 3]
            for hd in range(16):
                g, hh = hd // 4, hd % 4
                j = ORDR.index(hh)
                slope = 2.0 ** (-(hd + 1) / 2.0)
                STT(BIAS[:, g, 1, j * 128:(j + 1) * 128], d0, -slope, mc, ALU.mult, ALU.add, [], [tBIAS])
                STT(BIAS[:, g, 0, j * 128:(j + 1) * 128], d0, -slope, mp, ALU.mult, ALU.add, [], [tBIAS])
                TS(BIAS[:, g, 0, j * 128:(j + 1) * 128], BIAS[:, g, 0, j * 128:(j + 1) * 128], -128.0 * slope, None, ALU.add, None, [], [tBIAS])
            ACT(esink, sk, AF.Exp, [tsk], [tBIAS])
            TS(bqs, SMALL[:, C_BQ:C_BQ + 8], SWA_SCALE, None, ALU.mult, None, [tl], [tBIAS])
            QT = [pool.alloc([128, 8, CH], BF16) for _ in range(2)]
            tQT = [Tile("QT0"), Tile("QT1")]
            SB = [pool.alloc([128, 512], F32) for _ in range(2)]
            tSB = [Tile("SB0"), Tile("SB1")]
            PB = [pool.alloc([128, 512], BF16) for _ in range(6)]
            tPB = [Tile(f"PB{i}") for i in range(6)]
            OTs = [pool.alloc([128, 1024], BF16) for _ in range(2)]
            tOTs = [Tile("OT0"), Tile("OT1")]
            den = pool.alloc([128, 4], F32)
            tden = Tile("den")
            PST = PS[7][:].bitcast(BF16)
            cnt = {"sb": 0, "pb": 0, "sc": 0}
            if stop_after == "L1bias":
                return True
            if True:
                def qgen(i):
                    gc = slice(i * CH, (i + 1) * CH)
                    qb = i % 2
                    for m in range(8):
                        pq = 6 + m % 2
                        for kc in range(KC):
                            MM(PS[pq][:, 0:CH], wq[:, kc, m * 128:(m + 1) * 128], H[:, kc, gc], kc == 0, kc == KC - 1, [twq, tH[i]], [tPS[pq]])
                        ACT(QT[qb][:, m, :], PS[pq][:, 0:CH], AF.Identity, [tPS[pq], tBIAS], [tQT[qb]], bias=bqs[:, m:m + 1], scale=SWA_SCALE)

                def l1_front(i, t3, g):
                    s, c, qb = i // NCHS, i % NCHS, i % 2
                    lt = c * 3 + t3
                    t = i * 3 + t3
                    qc = slice(t3 * 128, (t3 + 1) * 128)
                    pbs = []
                    for kk in range(2):
                        kt = t - 1 + kk
                        kcols = slice(kt * 128, (kt + 1) * 128)
                        pair = 2 * (cnt["sc"] % 2)
                        cnt["sc"] += 1
                        for hh in range(4):
                            m = 2 * g + hh // 2
                            half = hh % 2
                            pr = slice(half * 64, (half + 1) * 64)
                            Ksrc = KT if half == g % 2 else KTs
                            bank = pair + half
                            col = (hh // 2) * 128
                            MM(PS[bank][:, col:col + 128], Ksrc[pr, g // 2, kcols], QT[qb][pr, m, qc], True, True,
                               [tKT[kt // 3], tQT[qb]], [tPS[bank]])
                        sb = cnt["sb"] % 2
                        cnt["sb"] += 1
                        TT(SB[sb][:, 0:256], PS[pair][:, 0:256], BIAS[:, g, kk, 0:256], ALU.add, [tPS[pair], tBIAS], [tSB[sb]])
                        TT(SB[sb][:, 256:512], PS[pair + 1][:, 0:256], BIAS[:, g, kk, 256:512], ALU.add, [tPS[pair + 1], tBIAS], [tSB[sb]])
                        pb = cnt["pb"] % 6
                        cnt["pb"] += 1
                        pbs.append(pb)
                        if lt == 1 and kk == 0:
                            ACT(PB[pb], SB[sb], AF.Exp, [tSB[sb], tCONST], [tPB[pb]], bias=SMALL[:, C_HB + s:C_HB + s + 1], scale=1.0)
                        else:
                            ACT(PB[pb], SB[sb], AF.Exp, [tSB[sb]], [tPB[pb]])
                    return pbs

                def l1_back(i, t3, g, pbs):
                    t = i * 3 + t3
                    pso = 4 + g % 2
                    oi = t % 2
                    for hh in range(4):
                        for kk in range(2):
                            kt = t - 1 + kk
                            pb = pbs[kk]
                            j = ORDR.index(hh)
                            MM(PS[pso][:, hh * 65:(hh + 1) * 65], PB[pb][:, j * 128:(j + 1) * 128], VP[:, kt, g, :], kk == 0, kk == 1,
                               [tPB[pb], tVP[kt], tvone], [tPS[pso]])
                    po3 = PS[pso][:, 0:260].rearrange("p (h d) -> p h d", h=4)
                    TT(den, po3[:, :, 64], esink[:, 4 * g:4 * g + 4], ALU.add, [tPS[pso], tBIAS], [tden], strict=True)
                    RECIP(den, den, [tden], [tden], strict=True)
                    for hh in range(4):
                        hd = 4 * g + hh
                        TS(OTs[oi][:, hd * 64:(hd + 1) * 64], PS[pso][:, hh * 65:hh * 65 + 64], den[:, hh:hh + 1], None, ALU.mult, None,
                           [tPS[pso], tden], [tOTs[oi]], strict=(hh == 0))
                    if g != 3:
                        return
                    for m in range(8):
                        sc.op("pe", lambda e, m=m, oi=oi: e.transpose(PST[:, m * 128:(m + 1) * 128], OTs[oi][:, m * 128:(m + 1) * 128], IDENT[:]),
                              reads=[tOTs[oi], tCONST], writes=[tPS[7]])
                    tcols = slice(t * 128, (t + 1) * 128)
                    ACOPY(H[:, :, tcols], PST.rearrange("p (m q) -> p m q", m=8), [tPS[7]], [tH[i]])

                items = [(i, t3, g) for i in range(NCH) for t3 in range(3) if (i % NCHS) * 3 + t3 != 0 for g in range(4)]
                pend = []
                qdone = set()

                def ensure_q(i):
                    if i not in qdone:
                        qdone.add(i)
                        qgen(i)

                for idx, (i, t3, g) in enumerate(items):
                    ensure_q(i)
                    pend.append((i, t3, g, l1_front(i, t3, g)))
                    if idx + 3 < len(items):
                        ensure_q(items[idx + 3][0])
                    if len(pend) > 2:
                        l1_back(*pend.pop(0))
                while pend:
                    l1_back(*pend.pop(0))
            sc.barrier()
            pool.release(m1)
            wo1 = pool.alloc([128, KC, 1024], BF16)
            two = Tile("wo1")
            DMA("pool", wo1, w_o1.rearrange("(m p) n -> p m n", p=128), "wo1", writes=[two])
            gb = pool.alloc([128, 8], F32)
            tgb = Tile("gb")
            TT(gb, GT(1, 0), SMALL[:, C_BO:C_BO + 8], ALU.mult, [tmod, tl], [tgb])
            for ch in chunk_list(True):
                gc, w = ch["gc"], ch["w"]
                hts = [tH[o] for o in ch["olds"]]
                for dm in range(KC):
                    pb = dm % 2
                    for m in range(8):
                        MM(PS[pb][:, 0:w], wo1[:, m, dm * 128:(dm + 1) * 128], H[:, m, gc], m == 0, m == 7, [two] + hts, [tPS[pb]])
                    STT(X[:, dm, gc], PS[pb][:, 0:w], GT(1, 0)[:, dm:dm + 1], X[:, dm, gc], ALU.mult, ALU.add, [tPS[pb], tmod], ch["xt"])
                    TS(X[:, dm, gc], X[:, dm, gc], gb[:, dm:dm + 1], None, ALU.add, None, [tgb], ch["xt"])
            sc.barrier()
            pool.release(m0)

        if swa_layer():
            return finish(nc, sc, es, dumps, X, outT, tX, None)
        dump("xa1", X[:], allX)
        if stop_after == "L1attn":
            return finish(nc, sc, es, dumps, X, outT, tX, None)
        mlp(1, chunk_list(True))
        dump("xm1", X[:], allX)

        def final_norm():
            sq = pool.alloc([128, KC, CH], BF16)
            rstd = pool.alloc([128, CH], F32)
            Y = [pool.alloc([128, KC, CH], F32) for _ in range(2)]
            tY = [Tile("Y0"), Tile("Y1")]
            tsq = Tile("sqf")
            trs = Tile("rsf")
            xo = outT.rearrange("(kc p) t -> p kc t", p=128)
            evs = []
            for i, ch in enumerate(chunk_list(True)):
                gc, w, s_ = ch["gc"], ch["w"], ch["s"]
                yb = i % 2
                norm_rstd(X[:, :, gc], KC, w, sq[:, :, 0:w], i % 2, rstd[:, 0:w], D, ch["xt"], tsq, trs)
                for kc in range(KC):
                    STT(Y[yb][:, kc, 0:w], X[:, kc, gc], SMALL[:, C_GFIN + kc:C_GFIN + kc + 1], rstd[:, 0:w], ALU.mult, ALU.mult,
                        ch["xt"] + [trs, tl], [tY[yb]])
                o0 = s_ * BLK + ch["off"] - HALO
                evs.append(DMA("sp", xo[:, :, o0:o0 + w], Y[yb][:, :, 0:w], f"out{yb}", reads=[tY[yb]]))
            return evs

        final_norm()
        return finish(nc, sc, es, dumps, X, outT, tX, "done")


def finish(nc, sc, es, dumps, X, outT, tX, mode):
    if mode is None:
        xo = outT.rearrange("(kc p) t -> p kc t", p=128)
        for s in range(NSEG):
            src = X[:, :, s * SEG + HALO:(s + 1) * SEG]
            dst = xo[:, :, s * BLK:(s + 1) * BLK]
            sc.op("sp", lambda e, src=src, dst=dst: e.dma_start(out=dst, in_=src), reads=tX[s], dma_key="out")
    sc.op("sp", lambda e: None, extra=[ev for k, ev in sc.last_dma.items() if k.startswith("out") or k.startswith("dump")])
    emit(nc, sc, es)
    return nc, dumps


_LAST_SC = {}


def emit(nc, sc, es):
    _LAST_SC.clear()
    _LAST_SC.update({e: q for e, q in sc.q.items()})
    _LAST_SC['nwaits'] = [sum(len(w) for (_, w, _, _) in q) for q in sc.q.values()]
    E = es.enter_context
    signo = {}
    for eng in ENGS:
        n = 0
        for (fn, waits, ev, dk) in sc.q[eng]:
            if dk is None and ev.needed:
                n += 1
                signo[(eng, ev.idx)] = n
    esem = {eng: E(nc.semaphore("s_" + eng)) for eng in ENGS}
    dsem = {k: E(nc.semaphore("d_" + k)) for k in sc.dma_cnt}
    block = E(nc.Block())

    def replay(eng, e):
        for (fn, waits, ev, dk) in sc.q[eng]:
            for d in waits:
                if d.key.startswith("dma:"):
                    e.wait_ge(dsem[d.key[4:]], d.val)
                else:
                    e.wait_ge(esem[d.key], signo[(d.key, d.idx)])
            inst = fn(e)
            if inst is None:
                continue
            if dk is not None:
                inst.then_inc(dsem[dk], 16)
            elif ev.needed:
                inst.then_inc(esem[eng], 1)

    @block.tensor
    def _(e):
        replay("pe", e)

    @block.scalar
    def _(e):
        replay("act", e)

    @block.vector
    def _(e):
        replay("dve", e)

    @block.gpsimd
    def _(e):
        replay("pool", e)

    @block.sync
    def _(e):
        replay("sp", e)


def _core_layout(c):
    b, j = c // 4, c % 4
    blocks = [j, 7 - j]
    return b, blocks


def make_in_maps(inp):
    f32 = np.float32
    x = np.asarray(inp["x"], f32)
    pos = np.asarray(inp["positions"], np.int32)
    half = 32
    inv = (10000.0 ** (-np.arange(half, dtype=f32) / half)).astype(f32)
    invf = np.concatenate([inv, inv])[:, None].astype(f32)

    def featT(v):
        return np.ascontiguousarray(np.asarray(v, f32).reshape(-1, 128).T)

    w_uq = np.asarray(inp["mla_w_uq"][0], f32)
    uq = w_uq.reshape(384, 8, 192)
    w_uq_sw = np.ascontiguousarray(np.concatenate([uq[:, :, 160:192], uq[:, :, 128:160]], -1).reshape(384, 512))
    w_dkv = np.asarray(inp["mla_w_dkv"][0], f32)
    w_dkr_sw = np.ascontiguousarray(np.concatenate([w_dkv[:, 288:320], w_dkv[:, 256:288]], -1))
    w_qkv = np.asarray(inp["swa_w_qkv"][0], f32)
    b_qkv = np.asarray(inp["swa_b_qkv"][0], f32)
    wk = w_qkv[:, 1024:1280].reshape(1024, 2, 2, 64)
    w_k_sw = np.ascontiguousarray(wk[:, :, ::-1, :].reshape(1024, 256))
    bk = b_qkv[1024:1280].reshape(2, 2, 64)
    b_k_sw = np.ascontiguousarray(bk[:, ::-1, :].reshape(256))
    shared = {
        "w_ada": np.asarray(inp["w_ada"], f32),
        "b_adaT": np.ascontiguousarray(np.stack([featT(inp["b_ada"][l]) for l in range(2)])),
        "gmixT": np.ascontiguousarray(np.stack([featT(inp["g_mix"][l]) for l in range(2)])),
        "gmlpT": np.ascontiguousarray(np.stack([featT(inp["g_mlp"][l]) for l in range(2)])),
        "gfinT": featT(inp["g_final"]),
        "w_dq": np.asarray(inp["mla_w_dq"][0], f32),
        "g_qT": featT(inp["mla_g_q"][0]),
        "w_uq": w_uq,
        "w_uq_sw": w_uq_sw,
        "w_dkv": w_dkv,
        "w_dkr_sw": w_dkr_sw,
        "g_kvT": featT(inp["mla_g_kv"][0]),
        "w_ukv": np.asarray(inp["mla_w_ukv"][0], f32),
        "w_o": np.asarray(inp["mla_w_o"][0], f32),
        "invf": invf,
        "w_qkv": w_qkv,
        "w_k_sw": w_k_sw,
        "b_qkvT": featT(b_qkv),
        "b_k_swT": featT(b_k_sw),
        "b_v": np.ascontiguousarray(b_qkv[None, 1280:1536]),
        "sinks": np.asarray(inp["swa_sinks"], f32).reshape(1, 16),
        "w_o1": np.asarray(inp["swa_w_o"][0], f32),
        "b_oT": featT(inp["swa_b_o"][0]),
        "w_ff1": np.asarray(inp["w_ff1"], f32),
        "w_ff2": np.asarray(inp["w_ff2"], f32),
    }
    xT_b = [np.ascontiguousarray(x[b].T) for b in range(2)]
    maps = []
    for c in range(8):
        b, blocks = _core_layout(c)
        xt = np.zeros((D, T0), f32)
        pq = np.zeros((1, T0), np.int32)
        meta = np.zeros((1, 4), f32)
        for s, blk in enumerate(blocks):
            lo = blk * BLK - HALO
            hi = (blk + 1) * BLK
            lo_c = max(lo, 0)
            off = s * SEG + (lo_c - lo)
            xt[:, off:s * SEG + SEG] = xT_b[b][:, lo_c:hi]
            pq[0, off:s * SEG + SEG] = pos[b, lo_c:hi]
            meta[0, s] = lo
            meta[0, 2 + s] = 1.0 if lo >= 0 else 0.0
        m = dict(shared)
        m.update({
            "xT": xt, "xTall": xT_b[b], "posq": pq, "posall": np.ascontiguousarray(pos[b][None, :]),
            "segmeta": meta, "cT": featT(inp["c"][b]),
        })
        maps.append(m)
    return maps


def assemble(results):
    out = np.zeros((2, S, D), np.float32)
    for c in range(8):
        b, blocks = _core_layout(c)
        o = results[c]["outT"]
        for s, blk in enumerate(blocks):
            out[b, blk * BLK:(blk + 1) * BLK, :] = o[:, s * BLK:(s + 1) * BLK].T
    return out


_CACHE = {}


def kernel(**inputs):
    maps = make_in_maps(inputs)
    if "nc" not in _CACHE:
        _CACHE["nc"] = build_program()[0]
    res = run_bass_kernel_spmd(_CACHE["nc"], maps, core_ids=list(range(8)))
    return assemble(res.results)
```
